# Optimizing a Trainium2 kernel written in Bass

```python
import math
import jax
import jax.numpy as jnp
from jax import lax
import numpy as np

D_MODEL = 1024
BATCH = 2
SEQ = 8192
DEPTH = 1
DEC_BATCH = 32
DEC_SEQ = 8
PAST_LEN = 16384
PAGE_SIZE = 128

H_MOBA = 8
HD_MOBA = 64
MOBA_W = H_MOBA * HD_MOBA
MOBA_BLOCK = 256
MOBA_TOPK = 3
Q_BLOCK = 128
H_GLA = 4
DK_GLA = 64
DV_GLA = 128
GLA_KW = H_GLA * DK_GLA
GLA_VW = H_GLA * DV_GLA
GLA_RANK = 16
GLA_TAU = 16.0
GLA_CHUNK = 64
N_MEM = 256
H_CROSS = 4
HD_CROSS = 128
CROSS_W = H_CROSS * HD_CROSS
D_FF = 2816
CONV_W = 3
N_BRANCH = 3
EPS = 1e-6
IN_SIZES = (MOBA_W, MOBA_W, MOBA_W, GLA_KW, GLA_KW, GLA_VW, GLA_VW, GLA_RANK, CROSS_W)
IN_COLS = sum(IN_SIZES)

kernel_name = 'moba_gla_memxattn_convffn_step'


def rmsnorm(x, g):
    xf = x.astype(jnp.float32)
    y = xf * lax.rsqrt(jnp.mean(xf * xf, axis=-1, keepdims=True) + EPS)
    return (y * g.astype(jnp.float32)).astype(x.dtype)


def moba_prompt(q, k, v):
    B, S, H, hd = q.shape
    scale = hd ** -0.5
    nblk = -(-S // MOBA_BLOCK)
    pad = nblk * MOBA_BLOCK - S
    padw = ((0, 0), (0, pad), (0, 0), (0, 0))
    kb = jnp.pad(k, padw).reshape(B, nblk, MOBA_BLOCK, H, hd).transpose(0, 3, 1, 2, 4)
    vb = jnp.pad(v, padw).reshape(B, nblk, MOBA_BLOCK, H, hd).transpose(0, 3, 1, 2, 4)
    kmean = kb.astype(jnp.float32).mean(axis=3)
    topk = min(MOBA_TOPK, nblk - 1)
    nq = S // Q_BLOCK
    qb = q.reshape(B, nq, Q_BLOCK, H, hd).transpose(1, 0, 3, 2, 4)
    bi = jnp.arange(B)[:, None, None, None]
    hi = jnp.arange(H)[None, :, None, None]
    blk_ids = jnp.arange(nblk)

    def one_block(args):
        c, qc = args
        cur = (c * Q_BLOCK) // MOBA_BLOCK
        qpos = c * Q_BLOCK + jnp.arange(Q_BLOCK)
        kpos = cur * MOBA_BLOCK + jnp.arange(MOBA_BLOCK)
        k_own = lax.dynamic_index_in_dim(kb, cur, axis=2, keepdims=False)
        v_own = lax.dynamic_index_in_dim(vb, cur, axis=2, keepdims=False)
        s_own = jnp.einsum('bhqd,bhkd->bhqk', qc, k_own).astype(jnp.float32) * scale
        s_own = jnp.where(kpos[None, :] <= qpos[:, None], s_own, -jnp.inf)
        if topk == 0:
            p = jax.nn.softmax(s_own, axis=-1).astype(v.dtype)
            return jnp.einsum('bhqk,bhkd->bhqd', p, v_own)
        gate = jnp.einsum('bhqd,bhnd->bhqn', qc.astype(jnp.float32), kmean)
        gate = jnp.where(blk_ids < cur, gate, -jnp.inf)
        _, idx = lax.top_k(gate, topk)
        valid = idx < cur
        kg = kb[bi, hi, idx]
        vg = vb[bi, hi, idx]
        s_sel = jnp.einsum('bhqd,bhqnkd->bhqnk', qc, kg).astype(jnp.float32) * scale
        s_sel = jnp.where(valid[..., None], s_sel, -jnp.inf).reshape(B, H, Q_BLOCK, topk * MOBA_BLOCK)
        p = jax.nn.softmax(jnp.concatenate([s_own, s_sel], axis=-1), axis=-1).astype(v.dtype)
        p_own = p[..., :MOBA_BLOCK]
        p_sel = p[..., MOBA_BLOCK:].reshape(B, H, Q_BLOCK, topk, MOBA_BLOCK)
        return (jnp.einsum('bhqk,bhkd->bhqd', p_own, v_own)
                + jnp.einsum('bhqnk,bhqnkd->bhqd', p_sel, vg))

    out = lax.map(one_block, (jnp.arange(nq), qb))
    return out.transpose(1, 0, 3, 2, 4).reshape(B, S, H, hd)


def moba_sample(q, k_new, v_new, cache_k, cache_v, page_table):
    DB, DS, H, hd = q.shape
    scale = hd ** -0.5
    past = page_table.shape[1] * PAGE_SIZE
    k_past = cache_k[page_table].reshape(DB, past, H, hd)
    nfull = past // MOBA_BLOCK
    cur_start = nfull * MOBA_BLOCK
    qpos = past + jnp.arange(DS)
    own_pos = jnp.arange(cur_start, past)
    v_own_past = cache_v[page_table[:, own_pos // PAGE_SIZE], own_pos % PAGE_SIZE]
    k_own = jnp.concatenate([k_past[:, cur_start:], k_new.astype(k_past.dtype)], axis=1)
    v_own = jnp.concatenate([v_own_past, v_new.astype(v_own_past.dtype)], axis=1)
    kpos = jnp.concatenate([own_pos, qpos])
    qh = q.transpose(0, 2, 1, 3)
    s_own = jnp.einsum('bhqd,bkhd->bhqk', qh, k_own).astype(jnp.float32) * scale
    s_own = jnp.where(kpos[None, :] <= qpos[:, None], s_own, -jnp.inf)
    n_own = k_own.shape[1]
    topk = min(MOBA_TOPK, nfull)
    if topk == 0:
        p = jax.nn.softmax(s_own, axis=-1).astype(v_own.dtype)
        o = jnp.einsum('bhqk,bkhd->bhqd', p, v_own)
        return o.transpose(0, 2, 1, 3)
    kmean = k_past[:, :cur_start].reshape(DB, nfull, MOBA_BLOCK, H, hd).astype(jnp.float32).mean(axis=2)
    gate = jnp.einsum('bhqd,bnhd->bhqn', qh.astype(jnp.float32), kmean)
    _, idx = lax.top_k(gate, topk)
    pos = idx[..., None] * MOBA_BLOCK + jnp.arange(MOBA_BLOCK)
    bi = jnp.arange(DB)[:, None, None, None, None]
    hi = jnp.arange(H)[None, :, None, None, None]
    kg = k_past[bi, pos, hi]
    vg = cache_v[page_table[bi, pos // PAGE_SIZE], pos % PAGE_SIZE, hi]
    s_sel = jnp.einsum('bhqd,bhqnkd->bhqnk', qh, kg).astype(jnp.float32) * scale
    s_sel = s_sel.reshape(DB, H, DS, topk * MOBA_BLOCK)
    p = jax.nn.softmax(jnp.concatenate([s_own, s_sel], axis=-1), axis=-1).astype(v_own.dtype)
    p_own = p[..., :n_own]
    p_sel = p[..., n_own:].reshape(DB, H, DS, topk, MOBA_BLOCK)
    o = (jnp.einsum('bhqk,bkhd->bhqd', p_own, v_own)
         + jnp.einsum('bhqnk,bhqnkd->bhqd', p_sel, vg.astype(v_own.dtype)))
    return o.transpose(0, 2, 1, 3)


def gla_chunked(q, k, v, log_a, state0):
    B, L, H, dk = q.shape
    dv = v.shape[-1]
    C = math.gcd(L, GLA_CHUNK)
    n = L // C

    def blocks(t):
        return t.astype(jnp.float32).reshape(B, n, C, H, t.shape[-1]).transpose(1, 0, 3, 2, 4)

    qs = blocks(q) * (dk ** -0.5)
    ks, vs, gs = blocks(k), blocks(v), blocks(log_a)
    causal = jnp.tril(jnp.ones((C, C), dtype=bool))

    def step(S, inp):
        qc, kc, vc, gc = inp
        G = jnp.cumsum(gc, axis=2)
        o_inter = jnp.einsum('bhcd,bhde->bhce', qc * jnp.exp(G), S)
        diff = G[:, :, :, None, :] - G[:, :, None, :, :]
        decay = jnp.exp(jnp.where(causal[:, :, None], diff, -jnp.inf))
        A = jnp.einsum('bhid,bhjd,bhijd->bhij', qc, kc, decay)
        o_intra = jnp.einsum('bhij,bhje->bhie', A, vc)
        G_last = G[:, :, -1]
        S_new = (jnp.exp(G_last)[..., None] * S
                 + jnp.einsum('bhjd,bhje->bhde', kc * jnp.exp(G_last[:, :, None] - G), vc))
        return S_new, o_inter + o_intra

    S_fin, o = lax.scan(step, state0.astype(jnp.float32), (qs, ks, vs, gs))
    return o.transpose(1, 0, 3, 2, 4).reshape(B, L, H, dv), S_fin


def cross_attn(q, mk, mv):
    scale = q.shape[-1] ** -0.5
    s = jnp.einsum('bqhd,bmhd->bhqm', q, mk.astype(q.dtype)).astype(jnp.float32) * scale
    p = jax.nn.softmax(s, axis=-1).astype(q.dtype)
    return jnp.einsum('bhqm,bmhd->bqhd', p, mv.astype(q.dtype))


def memory_kv(mem, norm_mem, w_mem_kv):
    B, M, _ = mem.shape
    mk, mv = jnp.split(rmsnorm(mem, norm_mem) @ w_mem_kv, 2, axis=-1)
    return mk.reshape(B, M, H_CROSS, HD_CROSS), mv.reshape(B, M, H_CROSS, HD_CROSS)


def conv_ffn(hn, prev, w_up, w_conv, b_conv, w_down):
    L = hn.shape[1]
    u, g = jnp.split(hn @ w_up, 2, axis=-1)
    ext = jnp.concatenate([prev.astype(u.dtype), u], axis=1)
    c = b_conv
    for i in range(CONV_W):
        c = c + w_conv[i] * ext[:, i:i + L]
    out = (jax.nn.gelu(c) * g) @ w_down
    return out, ext[:, L:]


def mixer_layer(x, moba_fn, gla_state0, mem_k, mem_v, conv_prev,
                norm_mix, w_in, w_gla_a2, b_gla_a, norm_gla, w_br_moba, w_br_gla, w_br_cross,
                w_gate, b_gate, w_out, norm_ffn, w_up, w_conv, b_conv, w_down):
    B, L, _ = x.shape
    f32 = jnp.float32
    xn = rmsnorm(x, norm_mix)
    points = np.cumsum(IN_SIZES)[:-1].tolist()
    q_m, k_m, v_m, q_g, k_g, v_g, r_g, a_g, q_c = jnp.split(xn @ w_in, points, axis=-1)

    def split_heads(t, h):
        return t.reshape(B, L, h, t.shape[-1] // h)

    k_m = split_heads(k_m, H_MOBA)
    v_m = split_heads(v_m, H_MOBA)
    o_m = moba_fn(split_heads(q_m, H_MOBA), k_m, v_m).reshape(B, L, MOBA_W).astype(x.dtype)
    log_a = jax.nn.log_sigmoid((a_g @ w_gla_a2 + b_gla_a).astype(f32)) / GLA_TAU
    o_g, gla_state = gla_chunked(split_heads(q_g, H_GLA), split_heads(k_g, H_GLA),
                                 split_heads(v_g, H_GLA), split_heads(log_a, H_GLA), gla_state0)
    o_g = o_g * lax.rsqrt(jnp.mean(o_g * o_g, axis=-1, keepdims=True) + EPS)
    o_g = (o_g.reshape(B, L, GLA_VW) * norm_gla.astype(f32) * jax.nn.silu(r_g.astype(f32))).astype(x.dtype)
    o_c = cross_attn(split_heads(q_c, H_CROSS), mem_k, mem_v).reshape(B, L, CROSS_W)
    gates = jax.nn.sigmoid((xn @ w_gate + b_gate).astype(f32)).astype(x.dtype)
    g_m, g_g, g_c = jnp.split(gates, N_BRANCH, axis=-1)
    merged = g_m * (o_m @ w_br_moba) + g_g * (o_g @ w_br_gla) + g_c * (o_c @ w_br_cross)
    h = x + merged @ w_out
    f, conv_state = conv_ffn(rmsnorm(h, norm_ffn), conv_prev, w_up, w_conv, b_conv, w_down)
    return h + f, k_m, v_m, gla_state, conv_state


def setup_inputs(seed: int = 0) -> dict:
    key = jax.random.key(seed)
    ks = iter(jax.random.split(key, 40))
    f32 = jnp.float32

    def nrm(shape, scale):
        return jax.random.normal(next(ks), shape, f32) * scale

    def gain(shape):
        return 1.0 + 0.02 * jax.random.normal(next(ks), shape, f32)

    n_pages = PAST_LEN // PAGE_SIZE
    n_used = DEC_BATCH * n_pages
    n_phys = n_used + max(1, n_used // 4)
    x_prompt = nrm((BATCH, SEQ, D_MODEL), 1.0)
    x_sample = nrm((DEC_BATCH, DEC_SEQ, D_MODEL), 1.0)
    cache_moba_k = nrm((DEPTH, n_phys, PAGE_SIZE, H_MOBA, HD_MOBA), 1.0)
    cache_moba_v = nrm((DEPTH, n_phys, PAGE_SIZE, H_MOBA, HD_MOBA), 1.0)
    state_gla = nrm((DEPTH, DEC_BATCH, H_GLA, DK_GLA, DV_GLA), 0.5)
    state_conv = nrm((DEPTH, DEC_BATCH, CONV_W - 1, D_FF), 1.0)
    cache_mem_k = nrm((DEPTH, DEC_BATCH, N_MEM, H_CROSS, HD_CROSS), 1.0)
    cache_mem_v = nrm((DEPTH, DEC_BATCH, N_MEM, H_CROSS, HD_CROSS), 1.0)
    page_table = jax.random.permutation(next(ks), n_phys)[:n_used].reshape(DEC_BATCH, n_pages).astype(jnp.int32)
    mem_prompt = nrm((BATCH, N_MEM, D_MODEL), 1.0)
    return {
        'x_prompt': x_prompt,
        'x_sample': x_sample,
        'cache_moba_k': cache_moba_k,
        'cache_moba_v': cache_moba_v,
        'state_gla': state_gla,
        'state_conv': state_conv,
        'cache_mem_k': cache_mem_k,
        'cache_mem_v': cache_mem_v,
        'page_table': page_table,
        'mem_prompt': mem_prompt,
        'norm_mix': gain((DEPTH, D_MODEL)),
        'w_in': nrm((DEPTH, D_MODEL, IN_COLS), D_MODEL ** -0.5),
        'w_gla_a2': nrm((DEPTH, GLA_RANK, GLA_KW), GLA_RANK ** -0.5),
        'b_gla_a': nrm((DEPTH, GLA_KW), 0.1),
        'norm_gla': gain((DEPTH, GLA_VW)),
        'norm_mem': gain((DEPTH, D_MODEL)),
        'w_mem_kv': nrm((DEPTH, D_MODEL, 2 * CROSS_W), D_MODEL ** -0.5),
        'w_br_moba': nrm((DEPTH, MOBA_W, D_MODEL), MOBA_W ** -0.5),
        'w_br_gla': nrm((DEPTH, GLA_VW, D_MODEL), GLA_VW ** -0.5),
        'w_br_cross': nrm((DEPTH, CROSS_W, D_MODEL), CROSS_W ** -0.5),
        'w_gate': nrm((DEPTH, D_MODEL, N_BRANCH * D_MODEL), D_MODEL ** -0.5),
        'b_gate': nrm((DEPTH, N_BRANCH * D_MODEL), 0.1),
        'w_out': nrm((DEPTH, D_MODEL, D_MODEL), D_MODEL ** -0.5),
        'norm_ffn': gain((DEPTH, D_MODEL)),
        'w_up': nrm((DEPTH, D_MODEL, 2 * D_FF), D_MODEL ** -0.5),
        'w_conv': nrm((DEPTH, CONV_W, D_FF), CONV_W ** -0.5),
        'b_conv': nrm((DEPTH, D_FF), 0.02),
        'w_down': nrm((DEPTH, D_FF, D_MODEL), D_FF ** -0.5),
        'norm_final': gain((D_MODEL,)),
    }


def reference(x_prompt, x_sample, cache_moba_k, cache_moba_v, state_gla, state_conv, cache_mem_k, cache_mem_v,
              page_table, mem_prompt, norm_mix, w_in, w_gla_a2, b_gla_a, norm_gla, norm_mem, w_mem_kv,
              w_br_moba, w_br_gla, w_br_cross, w_gate, b_gate, w_out, norm_ffn, w_up, w_conv, b_conv,
              w_down, norm_final):
    hp, hs = x_prompt, x_sample
    kp_l, vp_l, ks_l, vs_l, gp_l, gs_l, cp_l, cs_l, mkp_l, mvp_l = ([] for _ in range(10))
    for l in range(DEPTH):
        lw = (norm_mix[l], w_in[l], w_gla_a2[l], b_gla_a[l], norm_gla[l], w_br_moba[l], w_br_gla[l],
              w_br_cross[l], w_gate[l], b_gate[l], w_out[l], norm_ffn[l], w_up[l], w_conv[l], b_conv[l],
              w_down[l])
        mk_p, mv_p = memory_kv(mem_prompt, norm_mem[l], w_mem_kv[l])
        gla0 = jnp.zeros((hp.shape[0], H_GLA, DK_GLA, DV_GLA), jnp.float32)
        conv0 = jnp.zeros((hp.shape[0], CONV_W - 1, D_FF), hp.dtype)
        hp, k_p, v_p, g_p, c_p = mixer_layer(hp, moba_prompt, gla0, mk_p, mv_p, conv0, *lw)
        moba_s = functools.partial(moba_sample, cache_k=cache_moba_k[l], cache_v=cache_moba_v[l],
                                   page_table=page_table)
        hs, k_s, v_s, g_s, c_s = mixer_layer(hs, moba_s, state_gla[l], cache_mem_k[l], cache_mem_v[l],
                                             state_conv[l], *lw)
        kp_l.append(k_p); vp_l.append(v_p); ks_l.append(k_s); vs_l.append(v_s)
        gp_l.append(g_p); gs_l.append(g_s); cp_l.append(c_p); cs_l.append(c_s)
        mkp_l.append(mk_p); mvp_l.append(mv_p)
    y_prompt = rmsnorm(hp, norm_final)
    y_sample = rmsnorm(hs, norm_final)
    return (y_prompt, y_sample,
            jnp.stack(kp_l), jnp.stack(vp_l), jnp.stack(ks_l), jnp.stack(vs_l),
            jnp.stack(gp_l), jnp.stack(gs_l), jnp.stack(cp_l), jnp.stack(cs_l),
            jnp.stack(mkp_l), jnp.stack(mvp_l))


import functools
```

```python
import contextlib
import numpy as np
import concourse.bass as bass
import concourse.mybir as mybir
from concourse.bass_utils import run_bass_kernel_spmd

F32 = mybir.dt.float32
BF16 = mybir.dt.bfloat16
I32 = mybir.dt.int32
AF = mybir.ActivationFunctionType
ALU = mybir.AluOpType
AX = mybir.AxisListType
ENG = ("pe", "act", "dve", "pool", "sp")
NEG = -30000.0


class _Buf:
    __slots__ = ("lw", "rd")

    def __init__(self):
        self.lw = None
        self.rd = []


class _Op:
    __slots__ = ("eng", "fn", "deps", "dma", "sig", "sem", "val", "idx", "guard")

    def __init__(self, eng, fn, dma):
        self.eng, self.fn, self.dma = eng, fn, dma
        self.deps = set()
        self.sig = False
        self.sem = None
        self.val = 0
        self.guard = None


class Sched:
    NDMA = 6

    def __init__(self):
        self.ops = []
        self.bufs = {}
        self.bar = None

    def barrier(self, fn):
        o = _Op("dve", fn, False)
        o.idx = len(self.ops)
        for b in self.bufs.values():
            if b.lw is not None:
                o.deps.add(b.lw)
            o.deps.update(b.rd)
        self.ops.append(o)
        self.bar = o.idx
        self.bufs = {}

    limit = 10 ** 9

    def op(self, eng, fn, reads=(), writes=(), dma=False):
        if len(self.ops) >= self.limit:
            return None
        o = _Op(eng, fn, dma)
        o.idx = len(self.ops)
        if self.bar is not None:
            o.deps.add(self.bar)
        bufs = self.bufs
        for r in reads:
            b = bufs.get(r)
            if b is None:
                b = bufs[r] = _Buf()
            if b.lw is not None:
                o.deps.add(b.lw)
            if r.startswith("ps"):
                o.deps.update(b.rd)
        for w in writes:
            b = bufs.get(w)
            if b is None:
                b = bufs[w] = _Buf()
            if b.lw is not None:
                o.deps.add(b.lw)
            o.deps.update(b.rd)
        for r in reads:
            bufs[r].rd.append(o.idx)
        for w in writes:
            b = bufs[w]
            b.lw = o.idx
            b.rd = []
        o.deps.discard(o.idx)
        self.ops.append(o)
        return o

    def emit(self, nc):
        ops = self.ops
        for o in ops:
            for d in o.deps:
                p = ops[d]
                if p.eng == "pe" and o.eng == "pe" and not p.dma and not o.dma:
                    continue
                p.sig = True
        alld = [o for o in ops if o.dma]
        for o in alld:
            o.sig = True
        with contextlib.ExitStack() as st:
            csem = {e: st.enter_context(nc.semaphore("c_" + e)) for e in ENG}
            dsem = {e: [st.enter_context(nc.semaphore("d_%s%d" % (e, i))) for i in range(self.NDMA)]
                    for e in ("sp", "pool")}
            ccount = {e: 0 for e in ENG}
            dcount = {e: 0 for e in ENG}
            lastd = {}
            for o in ops:
                if not o.sig:
                    continue
                if o.dma:
                    i = dcount[o.eng]
                    dcount[o.eng] += 1
                    o.sem = dsem[o.eng][i % self.NDMA]
                    o.val = 16 * (i // self.NDMA + 1)
                    o.guard = lastd.get(id(o.sem))
                    lastd[id(o.sem)] = o
                else:
                    ccount[o.eng] += 1
                    o.sem = csem[o.eng]
                    o.val = ccount[o.eng]
            block = st.enter_context(nc.Block())
            per = {e: [o for o in ops if o.eng == e] for e in ENG}

            def run(e, engobj):
                waited = {}
                for o in per[e]:
                    need = {}
                    for d in o.deps:
                        p = ops[d]
                        if not p.sig:
                            continue
                        k = id(p.sem)
                        if need.get(k, (None, 0))[1] < p.val:
                            need[k] = (p.sem, p.val)
                    if o.guard is not None:
                        g = o.guard
                        k = id(g.sem)
                        if need.get(k, (None, 0))[1] < g.val:
                            need[k] = (g.sem, g.val)
                    for k, (s, v) in need.items():
                        if waited.get(k, 0) < v:
                            engobj.wait_ge(s, v)
                            waited[k] = v
                    ins = o.fn(engobj)
                    if o.sig:
                        ins.then_inc(o.sem, 16 if o.dma else 1)
                if e == "sp":
                    fin = {}
                    for o in alld:
                        k = id(o.sem)
                        if fin.get(k, (None, 0))[1] < o.val:
                            fin[k] = (o.sem, o.val)
                    for k, (s, v) in fin.items():
                        if waited.get(k, 0) < v:
                            engobj.wait_ge(s, v)
                    for ee in ENG:
                        if ccount[ee] > 0:
                            engobj.wait_ge(csem[ee], ccount[ee])

            @block.sync
            def _(eng):
                run("sp", eng)

            @block.scalar
            def _(eng):
                run("act", eng)

            @block.vector
            def _(eng):
                run("dve", eng)

            @block.gpsimd
            def _(eng):
                run("pool", eng)

            @block.tensor
            def _(eng):
                run("pe", eng)


SEQ = 8192
NOWN = 2048
HALO = 256
GROUPS = [(i * 256, 256) for i in range(9)]
NCOL = HALO + NOWN
DFF = 2816
NF = 22
INC = 3600


def build_nc(phases=("full", "moba", "gla", "cross", "tail", "smp"), ngroups=9, nfull=32, debug=False, n_phys=5120):
    nc = bass.Bass("TRN2", target_bir_lowering=False)

    def din(name, shape, dt=F32):
        return nc.dram_tensor(name, list(shape), dt, kind="ExternalInput").ap()

    def dout(name, shape, dt=F32):
        return nc.dram_tensor(name, list(shape), dt, kind="ExternalOutput").ap()

    xT_full = din("xT_full", [1024, SEQ])
    xT_own = din("xT_own", [1024, NCOL])
    memT = din("memT", [1024, 256])
    w_in = din("w_in", [1024, INC])
    w_a2 = din("w_a2", [16, 256])
    b_a = din("b_a", [1, 256])
    w_mem = din("w_mem", [1024, 1024])
    w_brm = din("w_brm", [512, 1024])
    w_brg = din("w_brg", [512, 1024])
    w_brc = din("w_brc", [512, 1024])
    w_gate = din("w_gate", [1024, 3072])
    w_out = din("w_out", [1024, 1024])
    w_up = din("w_up", [1024, 2 * DFF])
    w_down = din("w_down", [DFF, 1024])
    vecs = din("vecs", [128, 8 * 4 + 24 + 4 + NF * 4])
    blkvalid = din("blkvalid", [128, 9, 32])
    selr = din("selr", [128, 4])
    consts = din("consts", [128, 128 * 6])
    c32 = din("c32", [128, 516])
    ohrows = din("ohrows", [32, SEQ])
    xT_smp = din("xT_smp", [1024, 32])
    ptab = din("ptab", [4, 128], I32)
    poolKT = din("poolKT", [n_phys * 128, 512])
    poolV = din("poolV", [n_phys * 128, 512])
    sgla = din("sgla", [4, 128, 2, 128])
    sconv = din("sconv", [128, NF, 4, 2])
    mkTs = din("mkTs", [4, 4, 128, 256])
    mvs = din("mvs", [4, 256, 512])
    csm_d = din("csm", [128, 192])
    yT_s = dout("yT_s", [1024, 32])
    kT_s = dout("kT_s", [512, 32])
    v_s = dout("v_s", [32, 512])
    gla_s = dout("gla_s", [4, 128, 2, 128])
    conv_s = dout("conv_s", [128, NF, 4, 2])
    Ks = nc.dram_tensor("Ks", [512, SEQ], BF16, kind="Internal").ap()
    wsrc = dict(w_in=(w_in, [1024, INC]), w_brm=(w_brm, [512, 1024]), w_brg=(w_brg, [512, 1024]), w_brc=(w_brc, [512, 1024]),
                w_gate=(w_gate, [1024, 3072]), w_out=(w_out, [1024, 1024]), w_up=(w_up, [1024, 2 * DFF]), w_down=(w_down, [DFF, 1024]),
                w_mem=(w_mem, [1024, 1024]))
    wbf = {k: nc.dram_tensor(k + "_bf", shp, BF16, kind="Internal").ap() for k, (_, shp) in wsrc.items()}
    Vs = nc.dram_tensor("Vs", [SEQ, 512], BF16, kind="Internal").ap()

    yT = dout("yT", [1024, NOWN])
    kT_o = dout("kT_o", [512, NOWN])
    v_o = dout("v_o", [NOWN, 512])
    gla_o = dout("gla_o", [128, 2, 128])
    conv_o = dout("conv_o", [128, NF, 2])
    mk_o = dout("mk_o", [256, 512])
    mv_o = dout("mv_o", [256, 512])

    dbgo = {}
    if debug:
        for nm in ("om", "og", "oc", "mrg"):
            dbgo[nm] = nc.dram_tensor("d_" + nm, [128, 8, 256], BF16, kind="ExternalOutput").ap()
        for nm in ("h1", "h2", "qta", "xn"):
            dbgo[nm] = nc.dram_tensor("d_" + nm, [128, 8, 256], F32 if nm.startswith("h") else BF16, kind="ExternalOutput").ap()
    S = Sched()
    st = contextlib.ExitStack()
    with st:
        def sb(name, shape, dt=F32):
            return st.enter_context(nc.sbuf_tensor(name, list(shape), dt))

        psf = [st.enter_context(nc.psum_tensor("psf%d" % i, [128, 512], F32)) for i in range(6)]
        psb = [st.enter_context(nc.psum_tensor("psb%d" % i, [128, 1024], BF16)) for i in range(2)]
        pctr = [0, 0]

        def PS():
            i = pctr[0] % 4
            pctr[0] += 1
            return psf[i], "psf%d" % i

        def PSA(i):
            return psf[4 + i], "psf%d" % (4 + i)

        def PSB():
            i = pctr[1] % 2
            pctr[1] += 1
            return psb[i], "psb%d" % i

        def mm(out, lhsT, rhs, start, stop, R, W):
            S.op("pe", lambda e: e.matmul(out, lhsT=lhsT, rhs=rhs, start=start, stop=stop), reads=R, writes=W)

        def act(out, in_, func, R, W, **kw):
            S.op("act", lambda e: e.activation(out=out, in_=in_, func=func, **kw), reads=R, writes=W)

        def dve(fn, R, W):
            S.op("dve", fn, reads=R, writes=W)

        def pool(fn, R, W):
            S.op("pool", fn, reads=R, writes=W)

        def dma(out, in_, R, W, q="sp", **kw):
            S.op(q, lambda e: e.dma_start(out=out, in_=in_, **kw), reads=R, writes=W, dma=True)

        for k_, (src_, shp_) in wsrc.items():
            rows = shp_[0]
            for r0 in range(0, rows, 256):
                r1 = min(rows, r0 + 256)
                dma(wbf[k_][r0:r1, :], src_[r0:r1, :], [], ["wscr"], q="pool")
        w_in, w_brm, w_brg, w_brc, w_gate, w_out, w_up, w_down, w_mem = (wbf[k_] for k_ in (
            "w_in", "w_brm", "w_brg", "w_brc", "w_gate", "w_out", "w_up", "w_down", "w_mem"))
        cb = sb("cb", [128, 6, 128], BF16)
        c32t = sb("c32t", [128, 516], F32)
        vt = sb("vt", [128, 8 * 4 + 24 + 4 + NF * 4], F32)
        bv = sb("bv", [128, 9, 32], F32)
        selt = sb("selt", [128, 4], F32)
        ones = sb("ones", [128, 128], BF16)
        dma(cb[:], consts.rearrange("p (a b) -> p a b", a=6), [], ["cb"], q="pool")
        dma(c32t[:], c32, [], ["c32t"])
        dma(vt[:], vecs, [], ["vt"])
        dma(bv[:], blkvalid, [], ["bv"])
        dma(selt[:], selr, [], ["selt"])
        dve(lambda e: e.memset(ones[:], 1.0), [], ["ones"])
        keep = sb("keep", [128, 1], F32)
        dve(lambda e: e.tensor_scalar(out=keep[:], in0=selt[:, 0:1], scalar1=-1.0, scalar2=1.0, op0=ALU.mult, op1=ALU.add), ["selt"], ["keep"])
        ident = cb[:, 0, :]
        tri01 = cb[:, 1, :]
        blk2 = cb[:, 2, :]
        triU32 = c32t[:, 0:128]
        triI32 = c32t[:, 128:256]
        chk32 = c32t[:, 256:258]
        g_mix = lambda k: vt[:, k:k + 1]
        g_ffn = lambda k: vt[:, 8 + k:9 + k]
        g_fin = lambda k: vt[:, 16 + k:17 + k]
        g_mem = lambda k: vt[:, 24 + k:25 + k]
        b_gate = lambda j: vt[:, 32 + j:33 + j]
        g_gla = lambda h: vt[:, 56 + h:57 + h]
        wc = lambda i, f: vt[:, 60 + i * NF + f:61 + i * NF + f]

        xTt0_ = sb("xTt0", [128, 8, 256], F32)
        xTt = [xTt0_, xTt0_]
        sqt = sb("sqt", [128, 8, 256], BF16)
        xn = sb("xn", [128, 8, 256], BF16)
        rstd = sb("rstd", [128, 256], F32)
        wt = [sb("wt%d" % i, [128, 8, 512], BF16) for i in range(2)]
        wctr = [0]

        def norm(src, srcname, N, gfn, dst, dstname):
            act(sqt[:, :, :N], src[:, :, :N], AF.Square, [srcname], ["sqt"])
            p, pn = PS()
            for k in range(8):
                mm(p[:, :N], ones[:], sqt[:, k, :N], k == 0, k == 7, ["ones", "sqt"], [pn])
            act(rstd[:, :N], p[:, :N], AF.Sqrt, [pn], ["rstd"], scale=1.0 / 1024, bias=1e-6)
            dve(lambda e: e.reciprocal(out=rstd[:, :N], in_=rstd[:, :N]), ["rstd"], ["rstd"])
            for k in range(8):
                dve(lambda e, k=k: e.scalar_tensor_tensor(out=dst[:, k, :N], in0=src[:, k, :N], scalar=gfn(k), in1=rstd[:, :N],
                                                          op0=ALU.mult, op1=ALU.mult),
                    [srcname, "vt", "rstd"], [dstname])

        def loadw(src_ap, ncols, krows=8):
            i = wctr[0] % 2
            wctr[0] += 1
            t = wt[i]
            dma(t[:, :krows, :ncols], src_ap.rearrange("(k p) c -> p k c", p=128), ["wscr"], ["wt%d" % i])
            return t, "wt%d" % i

        mkT = sb("mkT", [128, 4, 256], BF16)
        mvb = sb("mvb", [128, 2, 512], BF16)
        stg = sb("stg", [128, 512], F32)
        dma(xTt[0][:, :, :256], memT.rearrange("(k p) t -> p k t", p=128), [], ["xTt0"])
        norm(xTt[0], "xTt0", 256, g_mem, xn, "xn")
        for half, (dst_o,) in enumerate([(mk_o,), (mv_o,)]):
            w, wn = loadw(w_mem[:, half * 512:(half + 1) * 512], 512)
            for tt in range(2):
                p, pn = PS()
                for k in range(8):
                    mm(p[:], xn[:, k, tt * 128:(tt + 1) * 128], w[:, k, :], k == 0, k == 7, ["xn", wn], [pn])
                act(stg[:], p[:], AF.Copy, [pn], ["stg"])
                if half == 1:
                    dve(lambda e, p=p, tt=tt: e.tensor_copy(out=mvb[:, tt, :], in_=p[:]), [pn], ["mvb", pn])
                dma(dst_o[tt * 128:(tt + 1) * 128, :], stg[:], ["stg"], [])
            if half == 0:
                for h in range(4):
                    p, pn = PS()
                    for k in range(8):
                        mm(p[:, :256], w[:, k, h * 128:(h + 1) * 128], xn[:, k, :256], k == 0, k == 7, ["xn", wn], [pn])
                    dve(lambda e, p=p, h=h: e.tensor_copy(out=mkT[:, h, :], in_=p[:, :256]), [pn], ["mkT"])

        wf = sb("wf", [128, 8, 1808], BF16)
        dma(wf[:, :, 0:1024], w_in[:, 512:1536].rearrange("(k p) c -> p k c", p=128), ["wscr"], ["wf"])
        dma(wf[:, :, 1024:1280], w_in[:, 1792:2048].rearrange("(k p) c -> p k c", p=128), ["wscr"], ["wf"])
        dma(wf[:, :, 1280:1808], w_in[:, 2048:2576].rearrange("(k p) c -> p k c", p=128), ["wscr"], ["wf"])
        dma(wf[:, :, 1792:1808], w_in[:, 3072:3088].rearrange("(k p) c -> p k c", p=128), ["wf", "wscr"], ["wf"])
        wa2 = sb("wa2", [16, 256], BF16)
        ba = sb("ba", [1, 256], BF16)
        dma(wa2[:], w_a2, [], ["wa2"], q="pool")
        dma(ba[:], b_a, [], ["ba"], q="pool")
        kmT = sb("kmT", [128, 4, 32], F32)
        kst = sb("kst", [128, 4, 256], BF16)
        vst = sb("vst", [128, 2, 512], BF16)
        Sst = sb("Sst", [128, 2, 128], F32)
        Ssave = sb("Ssave", [128, 3, 2, 128], F32)
        agT = sb("agT", [16, 512], BF16)
        nl = sb("nl", [128, 256], F32)
        eR = sb("eR", [128, 256], F32)
        ktil = sb("ktil", [128, 256], BF16)
        vgb = sb("vgb", [128, 512], BF16)
        dch = sb("dch", [128, 2, 2], F32)
        dve(lambda e: e.memset(Sst[:], 0.0), [], ["Sst"])

        def gla_prep(xnt, xnname, col0, wk_ap, wv_ap, wa_ap, wname):
            p, pn = PS()
            for k in range(8):
                mm(p[:16, :128], wa_ap(k), xnt[:, k, col0:col0 + 128], k == 0, k == 7, [xnname, wname], [pn])
            dve(lambda e, p=p: e.tensor_copy(out=agT[:, :128], in_=p[:16, :128]), [pn], ["agT"])
            p2, pn2 = PS()
            mm(p2[:, :256], agT[:, :128], wa2[:], True, False, ["agT", "wa2"], [pn2])
            mm(p2[:, :256], ones[0:1, :], ba[:], False, True, ["ones", "ba"], [pn2])
            act(nl[:], p2[:, :256], AF.Exp, [pn2], ["nl"], scale=-1.0)
            act(nl[:], nl[:], AF.Ln, ["nl"], ["nl"], bias=1.0)
            p3, pn3 = PS()
            mm(p3[:, :256], triU32, nl[:], True, True, ["c32t", "nl"], [pn3])
            act(eR[:], p3[:, :256], AF.Exp, [pn3], ["eR"])
            pk, pkn = PS()
            for k in range(8):
                mm(pk[:, :256], xnt[:, k, col0:col0 + 128], wk_ap(k), k == 0, k == 7, [xnname, wname], [pkn])
            dve(lambda e, pk=pk: e.tensor_tensor(out=ktil[:], in0=pk[:, :256], in1=eR[:], op=ALU.mult), [pkn, "eR"], ["ktil"])
            pv, pvn = PS()
            for k in range(8):
                mm(pv[:], xnt[:, k, col0:col0 + 128], wv_ap(k), k == 0, k == 7, [xnname, wname], [pvn])
            act(vgb[:], pv[:], AF.Copy, [pvn], ["vgb"])
            pd, pdn = PS()
            for half in range(2):
                mm(pd[:, half * 2:half * 2 + 2], nl[:, half * 128:(half + 1) * 128], chk32, True, True, ["nl", "c32t"], [pdn])
            act(dch[:], pd[:, 0:4].rearrange("p (a b) -> p a b", a=2), AF.Exp, [pdn], ["dch"])

        def gla_state_update(ci):
            for hp in range(2):
                pu, pun = PS()
                mm(pu[:, :256], ktil[ci * 64:(ci + 1) * 64, hp * 128:(hp + 1) * 128], vgb[ci * 64:(ci + 1) * 64, hp * 256:(hp + 1) * 256],
                   True, True, ["ktil", "vgb"], [pun])
                for hh in range(2):
                    rs = slice(hh * 64, (hh + 1) * 64)
                    dve(lambda e, pu=pu, hp=hp, hh=hh, rs=rs: e.scalar_tensor_tensor(
                        out=Sst[rs, hp, :], in0=Sst[rs, hp, :], scalar=dch[rs, hp, ci:ci + 1], in1=pu[rs, hh * 128:(hh + 1) * 128],
                        op0=ALU.mult, op1=ALU.add), [pun, "dch", "Sst"], ["Sst"])

        for g in range(nfull if "full" in phases else 0):
            xt = xTt[0]
            xname = "xTt0"
            dma(xt[:], xT_full[:, g * 256:(g + 1) * 256].rearrange("(k p) t -> p k t", p=128), [], [xname])
            norm(xt, xname, 256, g_mix, xn, "xn")
            for c in range(4):
                p, pn = PS()
                for k in range(8):
                    mm(p[:, :256], wf[:, k, c * 128:(c + 1) * 128], xn[:, k, :], k == 0, k == 7, ["xn", "wf"], [pn])
                act(kst[:, c, :], p[:, :256], AF.Copy, [pn], ["kst"])
                dve(lambda e, p=p, c=c, g=g: e.reduce_sum(out=kmT[:, c, g:g + 1], in_=p[:, :256], axis=AX.X), [pn], ["kmT"])
            dma(Ks.rearrange("(c p) t -> p c t", p=128)[:, :, g * 256:(g + 1) * 256], kst[:], ["kst"], ["Ks"])
            for tt in range(2):
                p, pn = PS()
                for k in range(8):
                    mm(p[:], xn[:, k, tt * 128:(tt + 1) * 128], wf[:, k, 512:1024], k == 0, k == 7, ["xn", "wf"], [pn])
                act(vst[:, tt, :], p[:], AF.Copy, [pn], ["vst"])
            dma(Vs[g * 256:(g + 1) * 256, :].rearrange("(t p) c -> p t c", p=128), vst[:], ["vst"], ["Vs"])
            for tt in range(2):
                gla_prep(xn, "xn", tt * 128, lambda k: wf[:, k, 1024:1280], lambda k: wf[:, k, 1280:1792],
                         lambda k: wf[:, k, 1792:1808], "wf")
                for ci in range(2):
                    chunk = g * 4 + tt * 2 + ci
                    for i, cb_ in enumerate((28, 60, 92)):
                        if chunk == cb_:
                            dve(lambda e, i=i: e.tensor_copy(out=Ssave[:, i, :, :], in_=Sst[:]), ["Sst"], ["Ssave"])
                    gla_state_update(ci)
        dma(gla_o, Sst[:], ["Sst"], [])
        kmbd = sb("kmbd", [128, 4, 64], BF16)
        dve(lambda e: e.memset(kmbd[:], 0.0), [], ["kmbd"])
        dve(lambda e: e.tensor_copy(out=kmbd[0:64, :, 0:32], in_=kmT[0:64, :, :]), ["kmT", "kmbd"], ["kmbd"])
        dve(lambda e: e.tensor_copy(out=kmbd[64:128, :, 32:64], in_=kmT[64:128, :, :]), ["kmT", "kmbd"], ["kmbd"])

        Srun = sb("Srun", [128, 2, 128], F32)
        dve(lambda e: e.tensor_scalar(out=Srun[:], in0=Ssave[:, 0, :, :], scalar1=selt[:, 1:2], scalar2=None, op0=ALU.mult), ["Ssave", "selt"], ["Srun"])
        for i in (1, 2):
            dve(lambda e, i=i: e.scalar_tensor_tensor(out=Srun[:], in0=Ssave[:, i, :, :], scalar=selt[:, i + 1:i + 2], in1=Srun[:],
                                                      op0=ALU.mult, op1=ALU.add), ["Ssave", "selt", "Srun"], ["Srun"])

        xo = xTt[0]
        hT = sb("hT", [128, 8, 256], F32)
        qT = sb("qT", [128, 4, 256], BF16)
        Bq = sb("Bq", [128, 2, 8, 96], BF16)
        qTa = sb("qTa", [96, 8, 256], BF16)
        kla = sb("kla", [96, 8, 256], BF16)
        vloc = sb("vloc", [128, 2, 512], BF16)
        gsel = sb("gsel", [128, 8, 32], F32)
        m8 = sb("m8", [128, 8, 8], F32)
        kaug0 = sb("kaug0", [96, SEQ], BF16)
        kaug = [kaug0, kaug0]
        vaug0 = sb("vaug0", [128, 64, 128], BF16)
        pT = [sb("pT%d" % i, [128, 512], BF16) for i in range(2)]
        rden = sb("rden", [128, 256], F32)
        omT = sb("omT", [128, 4, 256], BF16)
        ogT = sb("ogT", [128, 4, 256], BF16)
        ocT = sb("ocT", [128, 4, 256], BF16)
        mrg = sb("mrg", [128, 8, 256], BF16)
        hn = mrg
        sg = sb("sg", [128, 3, 256], F32)
        tmpf = sb("tmpf", [128, 256], F32)
        uext2 = sb("uext2", [128, 258], F32)
        tmpf2 = uext2
        aT = sb("aT", [128, NF, 256], BF16)
        uext = sb("uext", [128, 258], F32)
        uprev = sb("uprev", [128, NF, 2], F32)
        yst = xTt[0]
        kstg = hT
        qgT = sb("qgT", [128, 2, 256], F32)
        kgT = sb("kgT", [128, 2, 256], F32)
        qtl = sb("qtl", [128, 4, 256], BF16)
        ktl = sb("ktl", [128, 2, 256], BF16)
        egp = sb("egp", [128, 128], F32)
        egn = sb("egn", [128, 128], F32)
        rsil = sb("rsil", [128, 4, 256], F32)
        ogf = sb("ogf", [128, 4, 256], F32)
        qcT = sb("qcT", [128, 4, 256], BF16)

        dma(kaug0[64:96, :], ohrows, [], ["kaug0"], q="pool")
        dve(lambda e: e.memset(kla[:], 0.0), [], ["kla"])
        dve(lambda e: e.memset(uprev[:], 0.0), [], ["uprev"])
        dve(lambda e: e.memset(Bq[:], 0.0), [], ["Bq"])
        dve(lambda e: e.memset(qtl[:], 0.0), [], ["qtl"])
        selmat = [c32t[:, 260 + i * 128:260 + (i + 1) * 128] for i in range(2)]
        triB = cb[:, 3, :]

        def st_update(ci, St, Sn):
            for hp in range(2):
                pu, pun = PS()
                mm(pu[:, :256], ktil[ci * 64:(ci + 1) * 64, hp * 128:(hp + 1) * 128], vgb[ci * 64:(ci + 1) * 64, hp * 256:(hp + 1) * 256],
                   True, True, ["ktil", "vgb"], [pun])
                for hh in range(2):
                    rs = slice(hh * 64, (hh + 1) * 64)
                    dve(lambda e, pu=pu, hp=hp, hh=hh, rs=rs: e.scalar_tensor_tensor(
                        out=St[rs, hp, :], in0=St[rs, hp, :], scalar=dch[rs, hp, ci:ci + 1], in1=pu[rs, hh * 128:(hh + 1) * 128],
                        op0=ALU.mult, op1=ALU.add), [pun, "dch", Sn], [Sn])

        Sbf = [sb("Sbf%d" % i, [128, 2, 128], BF16) for i in range(2)]
        ATf = sb("ATf", [128, 128], BF16)

        def gla_own(gi, c0, N):
            ntile = N // 128
            w, wn = loadw(w_in[:, 1536:2048], 512)
            for j, dst, dn in ((0, qgT, "qgT"), (1, kgT, "kgT")):
                for c in range(2):
                    p, pn = PS()
                    for k in range(8):
                        mm(p[:, :N], w[:, k, j * 256 + c * 128:j * 256 + (c + 1) * 128], xn[:, k, :N], k == 0, k == 7, ["xn", wn], [pn])
                    act(dst[:, c, :N], p[:, :N], AF.Copy, [pn], [dn])
            w, wn = loadw(w_in[:, 2560:3072], 512)
            for c in range(4):
                p, pn = PS()
                for k in range(8):
                    mm(p[:, :N], w[:, k, c * 128:(c + 1) * 128], xn[:, k, :N], k == 0, k == 7, ["xn", wn], [pn])
                act(rsil[:, c, :N], p[:, :N], AF.Silu, [pn], ["rsil"])
            for tt in range(ntile):
                ts_ = slice(tt * 128, (tt + 1) * 128)
                gla_prep(xn, "xn", tt * 128, lambda k: wf[:, k, 1024:1280], lambda k: wf[:, k, 1280:1792],
                         lambda k: wf[:, k, 1792:1808], "wf")
                for hp in range(2):
                    pgm, pgn = PS()
                    mm(pgm[:, :128], nl[:, hp * 128:(hp + 1) * 128], triI32, True, True, ["nl", "c32t"], [pgn])
                    act(egp[:], pgm[:, :128], AF.Exp, [pgn], ["egp"])
                    act(egn[:], pgm[:, :128], AF.Exp, [pgn], ["egn"], scale=-1.0)
                    for par in range(2):
                        prs = slice(par * 64, par * 64 + 64)
                        dve(lambda e, hp=hp, ts_=ts_, prs=prs, par=par: e.scalar_tensor_tensor(
                            out=qtl[prs, hp * 2 + par, ts_], in0=qgT[prs, hp, ts_], scalar=0.125, in1=egp[prs, :],
                            op0=ALU.mult, op1=ALU.mult), ["qgT", "egp"], ["qtl"])
                    dve(lambda e, hp=hp, ts_=ts_: e.tensor_tensor(out=ktl[:, hp, ts_], in0=kgT[:, hp, ts_], in1=egn[:], op=ALU.mult),
                        ["kgT", "egn"], ["ktl"])
                dve(lambda e: e.tensor_copy(out=Sbf[0][:], in_=Srun[:]), ["Srun"], ["Sbf0"])
                st_update(0, Srun, "Srun")
                dve(lambda e: e.tensor_copy(out=Sbf[1][:], in_=Srun[:]), ["Srun"], ["Sbf1"])
                st_update(1, Srun, "Srun")
                for h in range(4):
                    rs = slice((h % 2) * 64, (h % 2) * 64 + 64)
                    hp = h // 2
                    pa, pan = PS()
                    mm(pa[:, :128], ktl[:, hp, ts_], qtl[:, h, ts_], True, True, ["ktl", "qtl"], [pan])
                    dve(lambda e, pa=pa: e.tensor_tensor(out=ATf[:], in0=pa[:, :128], in1=triB, op=ALU.mult), [pan, "cb"], ["ATf"])
                    po, pon = PSA(0)
                    mm(po[:, :128], vgb[:, h * 128:(h + 1) * 128], ATf[:], True, False, ["vgb", "ATf"], [pon])
                    for ci in range(2):
                        mm(po[:, ci * 64:(ci + 1) * 64], Sbf[ci][:, hp, :], qtl[:, h, tt * 128 + ci * 64:tt * 128 + (ci + 1) * 64],
                           False, ci == 1, ["Sbf%d" % ci, "qtl"], [pon])
                    act(ogf[:, h, ts_], po[:, :128], AF.Copy, [pon], ["ogf"])
            gla_norm(N)

        def gla_norm(N):
            for h in range(4):
                act(sqt[:, h, :N], ogf[:, h, :N], AF.Square, ["ogf"], ["sqt"])
                p, pn = PS()
                mm(p[:, :N], ones[:], sqt[:, h, :N], True, True, ["ones", "sqt"], [pn])
                act(rstd[:, :N], p[:, :N], AF.Sqrt, [pn], ["rstd"], scale=1.0 / 128, bias=1e-6)
                dve(lambda e: e.reciprocal(out=rstd[:, :N], in_=rstd[:, :N]), ["rstd"], ["rstd"])
                dve(lambda e, h=h: e.scalar_tensor_tensor(out=ogf[:, h, :N], in0=ogf[:, h, :N], scalar=g_gla(h), in1=rstd[:, :N],
                                                          op0=ALU.mult, op1=ALU.mult), ["ogf", "vt", "rstd"], ["ogf"])
                dve(lambda e, h=h: e.tensor_tensor(out=ogT[:, h, :N], in0=ogf[:, h, :N], in1=rsil[:, h, :N], op=ALU.mult), ["ogf", "rsil"], ["ogT"])

        def gla_proj(N):
            w, wn = loadw(w_in[:, 1536:2048], 512)
            for j, dst, dn in ((0, qgT, "qgT"), (1, kgT, "kgT")):
                for c in range(2):
                    p, pn = PS()
                    for k in range(8):
                        mm(p[:, :N], w[:, k, j * 256 + c * 128:j * 256 + (c + 1) * 128], xn[:, k, :N], k == 0, k == 7, ["xn", wn], [pn])
                    act(dst[:, c, :N], p[:, :N], AF.Copy, [pn], [dn])
            w, wn = loadw(w_in[:, 2560:3072], 512)
            for c in range(4):
                p, pn = PS()
                for k in range(8):
                    mm(p[:, :N], w[:, k, c * 128:(c + 1) * 128], xn[:, k, :N], k == 0, k == 7, ["xn", wn], [pn])
                act(rsil[:, c, :N], p[:, :N], AF.Silu, [pn], ["rsil"])

        def cross_q(N):
            w, wn = loadw(w_in[:, 3088:3600], 512)
            for h in range(4):
                p, pn = PS()
                for k in range(8):
                    mm(p[:, :N], w[:, k, h * 128:(h + 1) * 128], xn[:, k, :N], k == 0, k == 7, ["xn", wn], [pn])
                act(qcT[:, h, :N], p[:, :N], AF.Copy, [pn], ["qcT"])

        def cross_att(c0_, N):
            cs = slice(c0_, c0_ + N)
            for h in range(4):
                po, pon = PSA(0)
                pd, pdn = PSA(1)
                for mt in range(2):
                    ps_, psn = PS()
                    mm(ps_[:, :N], mkT[:, h, mt * 128:(mt + 1) * 128], qcT[:, h, cs], True, True, ["mkT", "qcT"], [psn])
                    ptile = pT[mt]
                    ptn = "pT%d" % mt
                    act(ptile[:, :N], ps_[:, :N], AF.Exp, [psn], [ptn], scale=128 ** -0.5)
                    mm(po[:, :N], mvb[:, mt, h * 128:(h + 1) * 128], ptile[:, :N], mt == 0, mt == 1, ["mvb", ptn], [pon])
                    mm(pd[:, :N], ones[:], ptile[:, :N], mt == 0, mt == 1, ["ones", ptn], [pdn])
                dve(lambda e, pd=pd: e.reciprocal(out=rden[:, :N], in_=pd[:, :N]), [pdn], ["rden"])
                dve(lambda e, po=po, h=h: e.tensor_tensor(out=ocT[:, h, cs], in0=po[:, :N], in1=rden[:, :N], op=ALU.mult), [pon, "rden"], ["ocT"])

        def cross_own(gi, c0, N):
            cross_q(N)
            cross_att(0, N)

        def tail_own(gi, c0, N, smp=False):
            oc0 = c0 - HALO
            brs = ((w_brm, omT, "omT"), (w_brg, ogT, "ogT"), (w_brc, ocT, "ocT"))
            wbr = []
            for (wd, _, _) in brs:
                wbr.append(None)
            for half in range(8):
                wg_t = []
                for i in range(3):
                    wg_t.append(loadw3[i](w_gate[:, i * 1024 + half * 128:i * 1024 + (half + 1) * 128], 128))
                wb_t = []
                for i, (wd, _, _) in enumerate(brs):
                    wb_t.append(loadw3b[i](wd[:, half * 128:(half + 1) * 128], 128, 4))
                for c4 in range(1):
                    c8 = half
                    cs = slice(c4 * 128, (c4 + 1) * 128)
                    for i in range(3):
                        w, wn = wg_t[i]
                        p, pn = PS()
                        for k in range(8):
                            mm(p[:, :N], w[:, k, cs], xn[:, k, :N], k == 0, k == 7, ["xn", wn], [pn])
                        act(sg[:, i, :N], p[:, :N], AF.Sigmoid, [pn, "vt"], ["sg%d" % i], bias=b_gate(i * 8 + c8))
                    for i, (wd, oT_, on_) in enumerate(brs):
                        w, wn = wb_t[i]
                        p, pn = PS()
                        for k in range(4):
                            mm(p[:, :N], w[:, k, cs], oT_[:, k, :N], k == 0, k == 3, [on_, wn], [pn])
                        if i == 0:
                            dve(lambda e, p=p: e.tensor_tensor(out=tmpf[:, :N], in0=p[:, :N], in1=sg[:, 0, :N], op=ALU.mult), [pn, "sg0"], ["tmpf"])
                        else:
                            dve(lambda e, p=p, i=i: e.tensor_tensor(out=tmpf2[:, :N], in0=p[:, :N], in1=sg[:, i, :N], op=ALU.mult), [pn, "sg%d" % i], ["uext2"])
                            if i == 1:
                                dve(lambda e: e.tensor_tensor(out=tmpf[:, :N], in0=tmpf[:, :N], in1=tmpf2[:, :N], op=ALU.add), ["tmpf", "uext2"], ["tmpf"])
                            else:
                                dve(lambda e, c8=c8: e.tensor_tensor(out=mrg[:, c8, :N], in0=tmpf[:, :N], in1=tmpf2[:, :N], op=ALU.add),
                                    ["tmpf", "uext2"], ["mrg"])
            for half in range(2):
                w, wn = loadw(w_out[:, half * 512:(half + 1) * 512], 512)
                for c4 in range(4):
                    c8 = half * 4 + c4
                    p, pn = PS()
                    for k in range(8):
                        mm(p[:, :N], w[:, k, c4 * 128:(c4 + 1) * 128], mrg[:, k, :N], k == 0, k == 7, ["mrg", wn], [pn])
                    dve(lambda e, p=p, c8=c8: e.tensor_tensor(out=hT[:, c8, :N], in0=p[:, :N], in1=xo[:, c8, :N], op=ALU.add), [pn, "xTt0"], ["hT"])
            if debug and gi == 1:
                dma(dbgo["mrg"], mrg[:], ["mrg"], [])
                dma(dbgo["h1"], hT[:], ["hT"], [])
            norm(hT, "hT", N, g_ffn, hn, "mrg")
            for f4 in range(0, NF, 4):
                nf = min(4, NF - f4)
                wu, wun = loadw(w_up[:, f4 * 128:(f4 + nf) * 128], nf * 128)
                for fi in range(nf):
                    f = f4 + fi
                    pu, pun = PS()
                    for k in range(8):
                        mm(pu[:, :N], wu[:, k, fi * 128:(fi + 1) * 128], hn[:, k, :N], k == 0, k == 7, ["mrg", wun], [pun])
                    wgt, wgn = loadw3[f % 3](w_up[:, DFF + f * 128:DFF + (f + 1) * 128], 128)
                    pg_, pgn_ = PS()
                    for k in range(8):
                        mm(pg_[:, :N], wgt[:, k, :128], hn[:, k, :N], k == 0, k == 7, ["mrg", wgn], [pgn_])
                    ux, uxn = (uext, "uext") if f % 2 == 0 else (uext2, "uext2")
                    tf, tfn = (tmpf, "tmpf") if f % 2 == 0 else (sg[:, 0, :], "sg0")
                    if smp:
                        u3 = ux[:, 0:40].rearrange("p (s c) -> p s c", c=10)
                        t3 = tf[:, 0:32].rearrange("p (s c) -> p s c", c=8)
                        dve(lambda e, f=f, u3=u3: e.tensor_copy(out=u3[:, :, 0:2], in_=sconvt[:, f, :, :]), ["kst"], [uxn])
                        act(u3[:, :, 2:10], pu[:, :32].rearrange("p (s c) -> p s c", c=8), AF.Copy, [pun, uxn], [uxn])
                        dve(lambda e, f=f, u3=u3: e.tensor_copy(out=ulasts[:, f, :, :], in_=u3[:, :, 8:10]), [uxn], ["kst"])
                        dve(lambda e, f=f, u3=u3, t3=t3: e.tensor_scalar(out=t3, in0=u3[:, :, 2:10], scalar1=wc(2, f), scalar2=wc(3, f), op0=ALU.mult, op1=ALU.add),
                            [uxn, "vt"], [tfn])
                        dve(lambda e, f=f, u3=u3, t3=t3: e.scalar_tensor_tensor(out=t3, in0=u3[:, :, 1:9], scalar=wc(1, f), in1=t3, op0=ALU.mult, op1=ALU.add),
                            [uxn, "vt", tfn], [tfn])
                        dve(lambda e, f=f, u3=u3, t3=t3: e.scalar_tensor_tensor(out=t3, in0=u3[:, :, 0:8], scalar=wc(0, f), in1=t3, op0=ALU.mult, op1=ALU.add),
                            [uxn, "vt", tfn], [tfn])
                    else:
                        if gi == 1:
                            dve(lambda e, f=f, ux=ux: e.tensor_scalar(out=ux[:, 0:2], in0=uprev[:, f, :], scalar1=keep[:, 0:1], scalar2=None, op0=ALU.mult),
                                ["uprev", "keep"], [uxn])
                        else:
                            dve(lambda e, f=f, ux=ux: e.tensor_copy(out=ux[:, 0:2], in_=uprev[:, f, :]), ["uprev"], [uxn])
                        act(ux[:, 2:2 + N], pu[:, :N], AF.Copy, [pun, uxn], [uxn])
                        dve(lambda e, f=f, ux=ux: e.tensor_copy(out=uprev[:, f, :], in_=ux[:, N:N + 2]), [uxn], ["uprev"])
                        dve(lambda e, f=f, ux=ux, tf=tf: e.tensor_scalar(out=tf[:, :N], in0=ux[:, 2:2 + N], scalar1=wc(2, f), scalar2=wc(3, f), op0=ALU.mult, op1=ALU.add),
                            [uxn, "vt"], [tfn])
                        dve(lambda e, f=f, ux=ux, tf=tf: e.scalar_tensor_tensor(out=tf[:, :N], in0=ux[:, 1:1 + N], scalar=wc(1, f), in1=tf[:, :N], op0=ALU.mult, op1=ALU.add),
                            [uxn, "vt", tfn], [tfn])
                        dve(lambda e, f=f, ux=ux, tf=tf: e.scalar_tensor_tensor(out=tf[:, :N], in0=ux[:, 0:N], scalar=wc(0, f), in1=tf[:, :N], op0=ALU.mult, op1=ALU.add),
                            [uxn, "vt", tfn], [tfn])
                    act(tf[:, :N], tf[:, :N], AF.Gelu_apprx_tanh, [tfn], [tfn])
                    dve(lambda e, pg_=pg_, f=f, tf=tf: e.tensor_tensor(out=aT[:, f, :N], in0=tf[:, :N], in1=pg_[:, :N], op=ALU.mult), [tfn, pgn_], ["aT"])
            if smp:
                dma(conv_s, ulasts, ["kst"], [])
            elif gi == len(GROUPS) - 1:
                dma(conv_o, uprev[:], ["uprev"], [])
            for c8 in range(8):
                p, pn = PSA(c8 % 2)
                for f4 in range(0, NF, 4):
                    nf = min(4, NF - f4)
                    w, wn = loadwd(w_down[f4 * 128:(f4 + nf) * 128, c8 * 128:(c8 + 1) * 128], 128, nf)
                    for fi in range(nf):
                        f = f4 + fi
                        mm(p[:, :N], w[:, fi, :128], aT[:, f, :N], f == 0, f == NF - 1, ["aT", wn], [pn])
                dve(lambda e, p=p, c8=c8: e.tensor_tensor(out=hT[:, c8, :N], in0=p[:, :N], in1=hT[:, c8, :N], op=ALU.add), [pn, "hT"], ["hT"])
            if debug and gi == 1:
                dma(dbgo["h2"], hT[:], ["hT"], [])
            if gi > 0 or smp:
                act(sqt[:, :, :N], hT[:, :, :N], AF.Square, ["hT"], ["sqt"])
                p, pn = PS()
                for k in range(8):
                    mm(p[:, :N], ones[:], sqt[:, k, :N], k == 0, k == 7, ["ones", "sqt"], [pn])
                act(rstd[:, :N], p[:, :N], AF.Sqrt, [pn], ["rstd"], scale=1.0 / 1024, bias=1e-6)
                dve(lambda e: e.reciprocal(out=rstd[:, :N], in_=rstd[:, :N]), ["rstd"], ["rstd"])
                for k in range(8):
                    dve(lambda e, k=k: e.scalar_tensor_tensor(out=yst[:, k, :N], in0=hT[:, k, :N], scalar=g_fin(k), in1=rstd[:, :N],
                                                              op0=ALU.mult, op1=ALU.mult), ["hT", "vt", "rstd"], ["xTt0"])
                if smp:
                    dma(yT_s.rearrange("(k p) t -> p k t", p=128), yst[:, :, :N], ["xTt0"], [])
                else:
                    dma(yT.rearrange("(k p) t -> p k t", p=128)[:, :, oc0:oc0 + N], yst[:, :, :N], ["xTt0"], [])

        wg3 = [sb("wg3_%d" % i, [128, 8, 128], BF16) for i in range(3)]
        wb3 = [sb("wb3_%d" % i, [128, 4, 128], BF16) for i in range(3)]
        wdt = [sb("wdt%d" % i, [128, 4, 128], BF16) for i in range(2)]
        wdctr = [0]

        def mk_loadw3(i):
            def f(src_ap, ncols):
                dma(wg3[i][:, :, :ncols], src_ap.rearrange("(k p) c -> p k c", p=128), ["wscr"], ["wg3_%d" % i])
                return wg3[i], "wg3_%d" % i
            return f

        def mk_loadw3b(i):
            def f(src_ap, ncols, kr):
                dma(wb3[i][:, :kr, :ncols], src_ap.rearrange("(k p) c -> p k c", p=128), ["wscr"], ["wb3_%d" % i])
                return wb3[i], "wb3_%d" % i
            return f

        loadw3 = [mk_loadw3(i) for i in range(3)]
        loadw3b = [mk_loadw3b(i) for i in range(3)]

        def loadwd(src_ap, ncols, kr):
            i = wdctr[0] % 2
            wdctr[0] += 1
            dma(wdt[i][:, :kr, :ncols], src_ap.rearrange("(k p) c -> p k c", p=128), ["wscr"], ["wdt%d" % i])
            return wdt[i], "wdt%d" % i

        vstf = vst[:, :, :].rearrange("p a b -> p (a b)")
        kstf = kst[:, :, :].rearrange("p a b -> p (a b)").bitcast(F32)
        csm = vstf[:, 128:512].bitcast(F32)
        knT = vstf[:, 0:128].rearrange("p (c t) -> p c t", t=32)
        gs2 = vstf[0:64, 512:640].bitcast(F32)
        biasb = vstf[0:64, 640:704]
        biasT = vstf[0:64, 704:768]
        m8s = vstf[0:64, 768:784].bitcast(F32)
        rd1 = vstf[0:64, 784:786].bitcast(F32)
        sconvt = kstf[:, 0:176].rearrange("p (f s i) -> p f s i", s=4, i=2)
        ulasts = kstf[:, 176:352].rearrange("p (f s i) -> p f s i", s=4, i=2)
        Qbd = Bq[:, :, :, :].rearrange("p a b c -> p (a b c)")[:, 0:1024].rearrange("p (a b c) -> p a b c", a=4, b=4)
        vnew = qTa[0:8, :, :].rearrange("p a b -> p (a b)").rearrange("p (s c) -> p s c", c=512)
        gself = gsel[:, :, :].rearrange("p a b -> p (a b)")
        ptf = gself[:, 0:128]
        pti = gself[:, 128:256].bitcast(I32)
        idxu = rden[:, 0:128].bitcast(I32)
        cm8 = csm[0:8, 1:65]
        EtA = csm[0:64, 65:129].rearrange("p (h t) -> p h t", t=8)
        tri8I = csm[0:8, 129:137]
        tri8U = csm[0:8, 137:145]
        tri8m = csm[0:8, 145:153]
        m16c = csm[0:8, 153:154]

        def sample_phase():
            N = 32
            dma(csm, csm_d, [], ["vst"])
            dma(sconvt, sconv, [], ["kst"])
            dma(xo[:, :, :N], xT_smp.rearrange("(k p) t -> p k t", p=128), [], ["xTt0"])
            norm(xo, "xTt0", N, g_mix, xn, "xn")
            w, wn = loadw(w_in[:, 0:512], 512)
            for c in range(4):
                p, pn = PS()
                for k in range(8):
                    mm(p[:, :N], w[:, k, c * 128:(c + 1) * 128], xn[:, k, :N], k == 0, k == 7, ["xn", wn], [pn])
                act(qT[:, c, :N], p[:, :N], AF.Copy, [pn], ["qT"])
            w, wn = loadw(w_in[:, 512:1024], 512)
            for c in range(4):
                p, pn = PS()
                for k in range(8):
                    mm(p[:, :N], w[:, k, c * 128:(c + 1) * 128], xn[:, k, :N], k == 0, k == 7, ["xn", wn], [pn])
                act(knT[:, c, :], p[:, :N], AF.Copy, [pn], ["vst"])
                dve(lambda e, p=p, c=c: e.tensor_copy(out=kstg[:, c, :N], in_=p[:, :N]), [pn], ["hT"])
            dma(kT_s.rearrange("(c p) t -> p c t", p=128), kstg[:, 0:4, :N], ["hT"], [])
            w, wn = loadw(w_in[:, 1024:1536], 512)
            for s_ in range(4):
                p, pn = PS()
                for k in range(8):
                    mm(p[0:8, :], xn[:, k, s_ * 8:(s_ + 1) * 8], w[:, k, :], k == 0, k == 7, ["xn", wn], [pn])
                act(vnew[:, s_, :], p[0:8, :], AF.Copy, [pn], ["qTa"])
                dve(lambda e, p=p: e.tensor_copy(out=stg[0:8, :], in_=p[0:8, :]), [pn], ["stg"])
                dma(v_s[s_ * 8:(s_ + 1) * 8, :], stg[0:8, :], ["stg"], [])
            dve(lambda e: e.memset(Qbd, 0.0), [], ["Bq"])
            for hp in range(4):
                for h2 in range(2):
                    h = 2 * hp + h2
                    prs = slice(h2 * 64, h2 * 64 + 64)
                    dve(lambda e, hp=hp, h=h, prs=prs: e.tensor_copy(out=Qbd[prs, hp, :, h * 8:(h + 1) * 8],
                                                                     in_=qT[prs, hp, 0:32].rearrange("p (s t) -> p s t", t=8)), ["qT", "Bq"], ["Bq"])
            Sraw = vaug0[:, :, :].rearrange("p a (b c) -> p (a b) c", c=64)
            NS = 8
            kslot = [("wt0s%d" % j, wt[0][:, j, :]) for j in range(NS)]
            vslot = [("wt1s%d" % j, wt[1][:, j, :]) for j in range(NS)]
            dve(lambda e: e.memset(m8[0:1, 0, 0:1], 0.0), ["wt0", "wt1"], [n for n, _ in kslot + vslot] + ["m8"])
            for s_ in range(4):
                cs = slice(s_ * 8, (s_ + 1) * 8)
                dma(pti, ptab[s_:s_ + 1, :].to_broadcast([128, 128]), [], ["gsel"])
                dve(lambda e: e.tensor_copy(out=ptf, in_=pti), ["gsel"], ["gsel"])
                dve(lambda e: e.tensor_scalar(out=ptf, in0=ptf, scalar1=128.0, scalar2=csm[:, 0:1], op0=ALU.mult, op1=ALU.add), ["gsel", "vst"], ["gsel"])
                dve(lambda e: e.tensor_copy(out=idxu, in_=ptf), ["gsel"], ["rden"])
                pgate, pgaten = PSA(1)
                for pg in range(128):
                    ktn, ktf = kslot[pg % NS]
                    kt = ktf.rearrange("p (a b) -> p a b", b=128)
                    S.op("pool", lambda e, ktf=ktf, pg=pg: e.indirect_dma_start(
                        out=ktf, out_offset=None, in_=poolKT,
                        in_offset=bass.IndirectOffsetOnAxis(ap=idxu[:, pg:pg + 1], axis=0)), reads=["rden"], writes=[ktn], dma=True)
                    ps_, psn = PS()
                    for hp in range(4):
                        mm(ps_[:, :64], kt[:, hp, :], Qbd[:, hp, s_, :], hp == 0, hp == 3, [ktn, "Bq"], [psn])
                    act(Sraw[:, pg, :], ps_[:, :64], AF.Copy, [psn], ["vaug0"])
                    if pg >= 1:
                        q_ = pg - 1
                        mm(pgate[0:64, q_ // 2:q_ // 2 + 1], Sraw[:, q_, :], ones[:, 0:1], q_ % 2 == 0, q_ % 2 == 1, ["vaug0", "ones"], [pgaten])
                mm(pgate[0:64, 63:64], Sraw[:, 127, :], ones[:, 0:1], False, True, ["vaug0", "ones"], [pgaten])
                dve(lambda e, pgate=pgate: e.tensor_copy(out=gs2, in_=pgate[0:64, 0:64]), [pgaten], ["vst"])
                dve(lambda e: e.max(out=m8s, in_=gs2), ["vst"], ["vst"])
                dve(lambda e: e.tensor_scalar(out=gs2, in0=gs2, scalar1=m8s[:, 2:3], scalar2=None, op0=ALU.is_ge), ["vst", "vst"], ["vst"])
                dve(lambda e: e.tensor_scalar(out=biasb, in0=gs2, scalar1=-1.0, scalar2=-NEG, op0=ALU.add, op1=ALU.mult), ["vst"], ["vst"])
                pb, pbn = PSB()
                S.op("pe", lambda e, pb=pb: e.transpose(out=pb[0:64, 0:64], in_=biasb, identity=cb[0:64, 0, 0:64]), reads=["vst", "cb"], writes=[pbn])
                act(biasT, pb[0:64, 0:64], AF.Copy, [pbn], ["vst"])
                po, pon = PSA(0)
                pden, pdenn = PSA(1)
                pbl = {}

                def issue_bias(b4):
                    pb_, pb_n = PS()
                    for j in range(4):
                        blk = (b4 * 4 + j) // 2
                        mm(pb_[:, j * 64:(j + 1) * 64], cb[0:64, 0, blk:blk + 1].to_broadcast([64, 128]), biasT, True, True, ["cb", "vst"], [pb_n])
                    pbl[b4] = (pb_, pb_n)

                issue_bias(0)
                for b4 in range(32):
                    if b4 + 1 < 32:
                        issue_bias(b4 + 1)
                    pb_, pb_n = pbl.pop(b4)
                    dve(lambda e, pb_=pb_, b4=b4: e.scalar_tensor_tensor(
                        out=tmpf[:, :256], in0=Sraw[:, b4 * 4:(b4 + 1) * 4, :].rearrange("p a b -> p (a b)"), scalar=0.125, in1=pb_[:, :256],
                        op0=ALU.mult, op1=ALU.add), ["vaug0", pb_n], ["tmpf"])
                    ptile = pT[b4 % 2]
                    ptn = "pT%d" % (b4 % 2)
                    act(ptile[:, :256], tmpf[:, :256], AF.Exp, ["tmpf"], [ptn])
                    for j in range(4):
                        pg = b4 * 4 + j
                        vtn, vt_ = vslot[pg % NS]
                        S.op("pool", lambda e, vt_=vt_, pg=pg: e.indirect_dma_start(
                            out=vt_, out_offset=None, in_=poolV,
                            in_offset=bass.IndirectOffsetOnAxis(ap=idxu[:, pg:pg + 1], axis=0)), reads=["rden"], writes=[vtn], dma=True)
                        mm(po[0:64, :], ptile[:, j * 64:(j + 1) * 64], vt_, pg == 0, False, [ptn, vtn], [pon])
                        mm(pden[0:64, 100:101], ptile[:, j * 64:(j + 1) * 64], ones[:, 0:1], pg == 0, False, [ptn, "ones"], [pdenn])
                pso, pson = PS()
                for hp in range(4):
                    mm(pso[0:8, :64], knT[:, hp, cs], Qbd[:, hp, s_, :], hp == 0, hp == 3, ["vst", "Bq"], [pson])
                dve(lambda e, pso=pso: e.scalar_tensor_tensor(out=tmpf[0:8, :64], in0=pso[0:8, :64], scalar=0.125, in1=cm8, op0=ALU.mult, op1=ALU.add),
                    [pson, "vst"], ["tmpf"])
                act(pT[0][0:8, :64], tmpf[0:8, :64], AF.Exp, ["tmpf"], ["pT0"])
                mm(po[0:64, :], pT[0][0:8, :64], vnew[:, s_, :], False, True, ["pT0", "qTa"], [pon])
                mm(pden[0:64, 100:101], pT[0][0:8, :64], ones[0:8, 0:1], False, True, ["pT0", "ones"], [pdenn])
                act(stg[0:64, :], po[0:64, :], AF.Copy, [pon], ["stg"])
                dve(lambda e, pden=pden: e.reciprocal(out=rd1, in_=pden[0:64, 100:101]), [pdenn], ["vst"])
                dve(lambda e: e.tensor_scalar(out=stg[0:64, :], in0=stg[0:64, :], scalar1=rd1[:, 0:1], scalar2=None, op0=ALU.mult),
                    ["stg", "vst"], ["stg"])
                for h in range(8):
                    hp = h // 2
                    rs = slice((h % 2) * 64, (h % 2) * 64 + 64)
                    pz, pzn = PS()
                    mm(pz[:, 0:8], stg[0:64, hp * 128:(hp + 1) * 128], EtA[:, h, :], True, True, ["stg", "vst"], [pzn])
                    act(omT[rs, hp, cs], pz[rs, 0:8], AF.Copy, [pzn], ["omT"])
            dve(lambda e: e.memset(m8[0:1, 0, 0:1], 0.0), [n for n, _ in kslot + vslot], ["wt0", "wt1", "m8"])
            gla_proj(N)
            for s_ in range(4):
                cs = slice(s_ * 8, (s_ + 1) * 8)
                p, pn = PS()
                for k in range(8):
                    mm(p[:16, :8], wf[:, k, 1792:1808], xn[:, k, cs], k == 0, k == 7, ["xn", "wf"], [pn])
                dve(lambda e, p=p: e.tensor_copy(out=agT[:, :8], in_=p[:16, :8]), [pn], ["agT"])
                p2, pn2 = PS()
                mm(p2[0:8, :256], agT[:, :8], wa2[:], True, False, ["agT", "wa2"], [pn2])
                mm(p2[0:8, :256], ones[0:1, 0:8], ba[:], False, True, ["ones", "ba"], [pn2])
                act(nl[0:8, :], p2[0:8, :256], AF.Exp, [pn2], ["nl"], scale=-1.0)
                act(nl[0:8, :], nl[0:8, :], AF.Ln, ["nl"], ["nl"], bias=1.0)
                p3, pn3 = PS()
                mm(p3[0:8, :256], tri8U, nl[0:8, :], True, True, ["vst", "nl"], [pn3])
                act(eR[0:8, :], p3[0:8, :256], AF.Exp, [pn3], ["eR"])
                pk, pkn = PS()
                for k in range(8):
                    mm(pk[0:8, :256], xn[:, k, cs], wf[:, k, 1024:1280], k == 0, k == 7, ["xn", "wf"], [pkn])
                dve(lambda e, pk=pk: e.tensor_tensor(out=ktil[0:8, :], in0=pk[0:8, :256], in1=eR[0:8, :], op=ALU.mult), [pkn, "eR"], ["ktil"])
                pv, pvn = PS()
                for k in range(8):
                    mm(pv[0:8, :], xn[:, k, cs], wf[:, k, 1280:1792], k == 0, k == 7, ["xn", "wf"], [pvn])
                act(vgb[0:8, :], pv[0:8, :], AF.Copy, [pvn], ["vgb"])
                pd, pdn = PS()
                for half in range(2):
                    mm(pd[:, half:half + 1], nl[0:8, half * 128:(half + 1) * 128], m16c, True, True, ["nl", "vst"], [pdn])
                act(dch[:, :, 0], pd[:, 0:2], AF.Exp, [pdn], ["dch"])
                for hp in range(2):
                    pgm, pgn = PS()
                    mm(pgm[:, :8], nl[0:8, hp * 128:(hp + 1) * 128], tri8I, True, True, ["nl", "vst"], [pgn])
                    act(egp[:, :8], pgm[:, :8], AF.Exp, [pgn], ["egp"])
                    act(egn[:, :8], pgm[:, :8], AF.Exp, [pgn], ["egn"], scale=-1.0)
                    for par in range(2):
                        prs = slice(par * 64, par * 64 + 64)
                        dve(lambda e, hp=hp, cs=cs, prs=prs, par=par: e.scalar_tensor_tensor(
                            out=qtl[prs, hp * 2 + par, cs], in0=qgT[prs, hp, cs], scalar=0.125, in1=egp[prs, :8],
                            op0=ALU.mult, op1=ALU.mult), ["qgT", "egp"], ["qtl"])
                    dve(lambda e, hp=hp, cs=cs: e.tensor_tensor(out=ktl[:, hp, cs], in0=kgT[:, hp, cs], in1=egn[:, :8], op=ALU.mult),
                        ["kgT", "egn"], ["ktl"])
                dma(Sst[:], sgla[s_], [], ["Sst"])
                dve(lambda e: e.tensor_copy(out=Sbf[0][:], in_=Sst[:]), ["Sst"], ["Sbf0"])
                for h in range(4):
                    hp = h // 2
                    pa, pan = PS()
                    mm(pa[0:8, 0:8], ktl[:, hp, cs], qtl[:, h, cs], True, True, ["ktl", "qtl"], [pan])
                    dve(lambda e, pa=pa: e.tensor_tensor(out=ATf[0:8, 0:8], in0=pa[0:8, 0:8], in1=tri8m, op=ALU.mult), [pan, "vst"], ["ATf"])
                    po, pon = PSA(0)
                    mm(po[:, 0:8], vgb[0:8, h * 128:(h + 1) * 128], ATf[0:8, 0:8], True, False, ["vgb", "ATf"], [pon])
                    mm(po[:, 0:8], Sbf[0][:, hp, :], qtl[:, h, cs], False, True, ["Sbf0", "qtl"], [pon])
                    act(ogf[:, h, cs], po[:, 0:8], AF.Copy, [pon], ["ogf"])
                for hp in range(2):
                    pu, pun = PS()
                    mm(pu[:, :256], ktil[0:8, hp * 128:(hp + 1) * 128], vgb[0:8, hp * 256:(hp + 1) * 256], True, True, ["ktil", "vgb"], [pun])
                    for hh in range(2):
                        rs = slice(hh * 64, (hh + 1) * 64)
                        dve(lambda e, pu=pu, hp=hp, hh=hh, rs=rs: e.scalar_tensor_tensor(
                            out=Sst[rs, hp, :], in0=Sst[rs, hp, :], scalar=dch[rs, hp, 0:1], in1=pu[rs, hh * 128:(hh + 1) * 128],
                            op0=ALU.mult, op1=ALU.add), [pun, "dch", "Sst"], ["Sst"])
                dma(gla_s[s_], Sst[:], ["Sst"], [])
            gla_norm(N)
            cross_q(N)
            for s_ in range(4):
                dma(mkT[:], mkTs[s_].rearrange("h d m -> d h m"), [], ["mkT"], q="pool")
                dma(mvb[:], mvs[s_].rearrange("(t p) c -> p t c", p=128), [], ["mvb"], q="pool")
                cross_att(s_ * 8, 8)
            tail_own(99, 0, N, smp=True)

        for gi, (c0, N) in enumerate(GROUPS[:ngroups]):
            ntile = N // 128
            dma(xo[:, :, :N], xT_own[:, c0:c0 + N].rearrange("(k p) t -> p k t", p=128), [], ["xTt0"])
            norm(xo, "xTt0", N, g_mix, xn, "xn")
            oc0 = c0 - HALO

            if "moba" in phases:
                w, wn = loadw(w_in[:, 0:512], 512)
                for c in range(4):
                    p, pn = PS()
                    for k in range(8):
                        mm(p[:, :N], w[:, k, c * 128:(c + 1) * 128], xn[:, k, :N], k == 0, k == 7, ["xn", wn], [pn])
                    act(qT[:, c, :N], p[:, :N], AF.Copy, [pn], ["qT"])
                for tt in range(ntile):
                    p, pn = PS()
                    for k in range(8):
                        mm(p[:], xn[:, k, tt * 128:(tt + 1) * 128], w[:, k, :], k == 0, k == 7, ["xn", wn], [pn])
                    act(Bq[:, tt, :, 0:64], p[:].rearrange("p (h d) -> p h d", h=8), AF.Copy, [pn], ["Bq"])
                w, wn = loadw(w_in[:, 512:1024], 512)
                for h in range(8):
                    p, pn = PS()
                    for k in range(8):
                        mm(p[:64, :N], w[:, k, h * 64:(h + 1) * 64], xn[:, k, :N], k == 0, k == 7, ["xn", wn], [pn])
                    act(kla[0:64, h, :N], p[:64, :N], AF.Copy, [pn], ["kla"])
                    if gi > 0:
                        dve(lambda e, p=p, h=h: e.tensor_copy(out=kstg[0:64, h, :N], in_=p[:64, :N]), [pn], ["hT"])
                if gi > 0:
                    dma(kT_o.rearrange("(h d) t -> d h t", d=64)[:, :, oc0:oc0 + N], kstg[0:64, :, :N], ["hT"], [])
                w, wn = loadw(w_in[:, 1024:1536], 512)
                for tt in range(ntile):
                    p, pn = PS()
                    for k in range(8):
                        mm(p[:], xn[:, k, tt * 128:(tt + 1) * 128], w[:, k, :], k == 0, k == 7, ["xn", wn], [pn])
                    act(vloc[:, tt, :], p[:], AF.Copy, [pn], ["vloc"])
                    if gi > 0:
                        dve(lambda e, p=p: e.tensor_copy(out=stg[:], in_=p[:]), [pn], ["stg"])
                        dma(v_o[oc0 + tt * 128:oc0 + (tt + 1) * 128, :], stg[:], ["stg"], [])
                for tt in range(ntile):
                    qb = (c0 + tt * 128) // 256
                    pg, pgn = PS()
                    for hp in range(4):
                        mm(pg[:, hp * 64:(hp + 1) * 64], qT[:, hp, tt * 128:(tt + 1) * 128], kmbd[:, hp, :], True, True, ["qT", "kmbd"], [pgn])
                    dve(lambda e, pg=pg, qb=qb: e.tensor_tensor(out=gsel[:], in0=pg[:, :256].rearrange("p (h b) -> p h b", h=8),
                                                                in1=bv[:, qb:qb + 1, :].to_broadcast([128, 8, 32]), op=ALU.add), [pgn, "bv"], ["gsel"])
                    for h in range(8):
                        dve(lambda e, h=h: e.max(out=m8[:, h, :], in_=gsel[:, h, :]), ["gsel"], ["m8"])
                    dve(lambda e: e.tensor_scalar_max(out=m8[:, :, 2:3], in0=m8[:, :, 2:3], scalar1=-1e29), ["m8"], ["m8"])
                    dve(lambda e: e.tensor_tensor(out=gsel[:], in0=gsel[:], in1=m8[:, :, 2:3].to_broadcast([128, 8, 32]), op=ALU.is_ge), ["gsel", "m8"], ["gsel"])
                    dve(lambda e, tt=tt: e.tensor_scalar(out=Bq[:, tt, :, 64:96], in0=gsel[:], scalar1=-1.0, scalar2=-NEG, op0=ALU.add, op1=ALU.mult),
                        ["gsel"], ["Bq"])
                    for hq in range(2):
                        pb, pbn = PSB()
                        for h4 in range(4):
                            h = hq * 4 + h4
                            S.op("pe", lambda e, pb=pb, h4=h4, h=h, tt=tt: e.transpose(out=pb[:96, h4 * 128:(h4 + 1) * 128], in_=Bq[:, tt, h, :], identity=ident),
                                 reads=["Bq", "cb"], writes=[pbn])
                        act(qTa[:, hq * 4:(hq + 1) * 4, tt * 128:(tt + 1) * 128], pb[:96, :512].rearrange("p (h q) -> p h q", h=4), AF.Copy, [pbn], ["qTa"])
                npast = min(32, 23 + gi)
                for h in range(8):
                    par = h % 2
                    hp = h // 2
                    ka = kaug0
                    kan = "kaug0"
                    L = npast * 256
                    nkt = npast * 2
                    dma(ka[0:64, :L], Ks[h * 64:(h + 1) * 64, :L], ["Ks"], [kan])
                    if par == 0:
                        dma(vaug0[:, :nkt, :], Vs[:L, hp * 128:(hp + 1) * 128].rearrange("(t p) d -> p t d", p=128), ["Vs"], ["vaug0"])
                    po, pon = PSA(0)
                    pd, pdn = PSA(1)
                    items = []
                    for kt in range(0, nkt, 2):
                        items.append([(ka[:, (kt + j) * 128:(kt + j + 1) * 128], kan, vaug0[:, kt + j, :], "vaug0", 0, 2, False, j * 256) for j in range(2)])
                    items.append([(kla[:, h, 0:128], "kla", vloc[:, 0, hp * 128:(hp + 1) * 128], "vloc", 0, 2, True, 0),
                                  (kla[:, h, 128:256], "kla", vloc[:, 1, hp * 128:(hp + 1) * 128], "vloc", 1, 1, True, 256)])
                    LA = 2
                    psl = {}

                    def issue_qk(i):
                        ps_, psn = PS()
                        for (kap, kn_, vap, vn_, q0, qn_, diag, col0) in items[i]:
                            mm(ps_[:, col0:col0 + qn_ * 128], kap, qTa[:, h, q0 * 128:(q0 + qn_) * 128], True, True, [kn_, "qTa"], [psn])
                        psl[i] = (ps_, psn)

                    for i in range(min(LA, len(items))):
                        issue_qk(i)
                    for si, it in enumerate(items):
                        if si + LA < len(items):
                            issue_qk(si + LA)
                        ps_, psn = psl.pop(si)
                        ptile = pT[si % 2]
                        ptn = "pT%d" % (si % 2)
                        wtot = it[-1][7] + it[-1][5] * 128
                        act(ptile[:, :wtot], ps_[:, :wtot], AF.Exp, [psn], [ptn], scale=0.125)
                        for (kap, kn_, vap, vn_, q0, qn_, diag, col0) in it:
                            if diag:
                                dve(lambda e, ptile=ptile, col0=col0: e.tensor_tensor(out=ptile[:, col0:col0 + 128], in0=ptile[:, col0:col0 + 128], in1=tri01, op=ALU.mult),
                                    [ptn, "cb"], [ptn])
                        for j, (kap, kn_, vap, vn_, q0, qn_, diag, col0) in enumerate(it):
                            qs = slice(q0 * 128, (q0 + qn_) * 128)
                            first = si == 0 and j == 0
                            last = si == len(items) - 1 and j == len(it) - 1
                            mm(po[:, qs], vap, ptile[:, col0:col0 + qn_ * 128], first, last, [vn_, ptn], [pon])
                            if first:
                                dve(lambda e, ptile=ptile, col0=col0, qs=qs, qn_=qn_: e.tensor_copy(out=tmpf[:, qs], in_=ptile[:, col0:col0 + qn_ * 128]),
                                    [ptn], ["tmpf"])
                            else:
                                dve(lambda e, ptile=ptile, col0=col0, qs=qs, qn_=qn_: e.tensor_tensor(out=tmpf[:, qs], in0=tmpf[:, qs], in1=ptile[:, col0:col0 + qn_ * 128], op=ALU.add),
                                    [ptn, "tmpf"], ["tmpf"])
                    mm(pd[:, :N], c32t[:, 388:516], tmpf[:, :N], True, True, ["c32t", "tmpf"], [pdn])
                    rs = slice(par * 64, par * 64 + 64)
                    dve(lambda e, pd=pd, rs=rs: e.reciprocal(out=rden[rs, :N], in_=pd[rs, :N]), [pdn], ["rden"])
                    dve(lambda e, rs=rs, hp=hp, po=po: e.tensor_tensor(out=omT[rs, hp, :N], in0=po[rs, :N], in1=rden[rs, :N], op=ALU.mult),
                        [pon, "rden"], ["omT"])

            if debug and gi == 1:
                dma(dbgo["om"][:, 0:4, :], omT[:], ["omT"], [])
                dma(dbgo["qta"][0:96, :, :], qTa[:], ["qTa"], [])
                dma(dbgo["xn"], xn[:], ["xn"], [])
            if "gla" in phases:
                gla_own(gi, c0, N)
            if "cross" in phases:
                cross_own(gi, c0, N)
            if debug and gi == 1:
                dma(dbgo["og"][:, 0:4, :], ogT[:], ["ogT"], [])
                dma(dbgo["oc"][:, 0:4, :], ocT[:], ["ocT"], [])
            if "tail" in phases:
                tail_own(gi, c0, N)
        if "smp" in phases:
            sample_phase()
        S.emit(nc)
    return nc


_NC_CACHE = {}


def _consts():
    p = np.arange(128)
    ident = np.eye(128, dtype=np.float32)
    tri01 = (p[:, None] <= p[None, :]).astype(np.float32)
    blk2 = ((p[:, None] // 64) == (p[None, :] // 64)).astype(np.float32)
    triB = tri01 * blk2
    cb = np.concatenate([ident, tri01, blk2, triB, ident, ident], axis=1).astype(np.float32)
    triU = ((p[:, None] > p[None, :]) & ((p[:, None] // 64) == (p[None, :] // 64))).astype(np.float32) * (-1.0 / 16)
    triI = triB * (-1.0 / 16)
    chk = np.zeros((128, 2), np.float32)
    chk[:64, 0] = -1.0 / 16
    chk[64:, 1] = -1.0 / 16
    sel = np.zeros((2, 128, 128), np.float32)
    sel[0, 64, 0:64] = 1.0
    sel[1, :, :] = 1.0
    c32 = np.concatenate([triU, triI, chk, np.zeros((128, 2), np.float32), sel[0], sel[1]], axis=1)
    oh = np.zeros((32, SEQ), np.float32)
    for j in range(32):
        oh[j, j * 256:(j + 1) * 256] = 1.0
    return cb, c32, oh


def _csm():
    c = np.zeros((128, 192), np.float32)
    c[:, 0] = np.arange(128)
    i = np.arange(8)[:, None]
    t = np.tile(np.arange(8), 8)[None, :]
    c[0:8, 1:65] = np.where(i <= t, 0.0, NEG)
    c[0:64, 65:129] = np.eye(64)
    tri = (np.arange(8)[:, None] <= np.arange(8)[None, :]).astype(np.float32)
    c[0:8, 129:137] = tri * (-1.0 / 16)
    c[0:8, 137:145] = (np.arange(8)[:, None] > np.arange(8)[None, :]).astype(np.float32) * (-1.0 / 16)
    c[0:8, 145:153] = tri
    c[0:8, 153] = -1.0 / 16
    return c


def pool_layouts(cache_k, cache_v):
    n = cache_k.shape[1]
    kt = np.ascontiguousarray(np.asarray(cache_k[0], np.float32).reshape(n, 128, 4, 2, 64).transpose(0, 3, 4, 2, 1)).reshape(n * 128, 512)
    v = np.ascontiguousarray(np.asarray(cache_v[0], np.float32)).reshape(n * 128, 512)
    return kt, v


def make_in_maps(inp, pools=None):
    f = lambda a: np.ascontiguousarray(np.asarray(a, dtype=np.float32))
    if pools is None:
        pools = pool_layouts(inp["cache_moba_k"], inp["cache_moba_v"])
    poolKT, poolV = pools
    csm = _csm()
    x_sample = f(inp["x_sample"])
    page_table = np.asarray(inp["page_table"]).astype(np.int32)
    state_gla = f(inp["state_gla"])
    state_conv = f(inp["state_conv"])
    cmk = f(inp["cache_mem_k"])
    cmv = f(inp["cache_mem_v"])
    x_prompt = f(inp["x_prompt"])
    cb, c32, oh = _consts()
    fm = lambda v: f(v).reshape(-1, 128).T
    w_conv = inp["w_conv"]
    vecs = np.concatenate([fm(inp["norm_mix"][0]), fm(inp["norm_ffn"][0]), fm(inp["norm_final"]), fm(inp["norm_mem"][0]),
                           fm(inp["b_gate"][0]), fm(inp["norm_gla"][0]), fm(w_conv[0, 0]), fm(w_conv[0, 1]), fm(w_conv[0, 2]),
                           fm(inp["b_conv"][0])], axis=1)
    vecs = np.ascontiguousarray(vecs)
    shared = dict(w_in=f(inp["w_in"][0]), w_a2=f(inp["w_gla_a2"][0]), b_a=f(inp["b_gla_a"][0]).reshape(1, 256),
                  w_mem=f(inp["w_mem_kv"][0]), w_brm=f(inp["w_br_moba"][0]), w_brg=f(inp["w_br_gla"][0]),
                  w_brc=f(inp["w_br_cross"][0]), w_gate=f(inp["w_gate"][0]), w_out=f(inp["w_out"][0]),
                  w_up=f(inp["w_up"][0]), w_down=f(inp["w_down"][0]), vecs=vecs, consts=cb, c32=c32, ohrows=oh)
    in_maps = []
    for c in range(8):
        b, r = c // 4, c % 4
        xT = np.ascontiguousarray(x_prompt[b].T)
        own = np.zeros((1024, NCOL), np.float32)
        lo = r * NOWN - HALO
        if lo < 0:
            own[:, HALO:] = xT[:, 0:NOWN]
        else:
            own[:] = xT[:, lo:lo + NCOL]
        bvld = np.zeros((128, 9, 32), np.float32)
        for i in range(9):
            cur = 8 * r - 1 + i
            bvld[:, i, :] = np.where(np.arange(32) < cur, 0.0, -1e30)[None, :]
        selr = np.zeros((128, 4), np.float32)
        selr[:, r] = 1.0
        m = dict(shared)
        m.update(xT_full=xT, xT_own=own, memT=np.ascontiguousarray(f(inp["mem_prompt"][b]).T), blkvalid=bvld, selr=selr)
        sq = slice(4 * c, 4 * c + 4)
        m.update(xT_smp=np.ascontiguousarray(x_sample[sq].reshape(32, 1024).T), ptab=np.ascontiguousarray(page_table[sq]),
                 poolKT=poolKT, poolV=poolV,
                 sgla=np.ascontiguousarray(state_gla[0, sq].reshape(4, 2, 2, 64, 128).transpose(0, 2, 3, 1, 4).reshape(4, 128, 2, 128)),
                 sconv=np.ascontiguousarray(state_conv[0, sq].reshape(4, 2, NF, 128).transpose(3, 2, 0, 1)),
                 mkTs=np.ascontiguousarray(cmk[0, sq].transpose(0, 2, 3, 1)), mvs=np.ascontiguousarray(cmv[0, sq].reshape(4, 256, 512)),
                 csm=csm)
        in_maps.append(m)
    return in_maps


def kernel(**inp):
    n_phys = int(np.asarray(inp["cache_moba_k"]).shape[1])
    if n_phys not in _NC_CACHE:
        _NC_CACHE[n_phys] = build_nc(n_phys=n_phys)
    nc = _NC_CACHE[n_phys]
    in_maps = make_in_maps(inp)
    res = run_bass_kernel_spmd(nc, in_maps, core_ids=list(range(8))).results
    B = 2
    y_prompt = np.zeros((B, SEQ, 1024), np.float32)
    nk = np.zeros((1, B, SEQ, 8, 64), np.float32)
    nv = np.zeros((1, B, SEQ, 8, 64), np.float32)
    gp = np.zeros((1, B, 4, 64, 128), np.float32)
    cp = np.zeros((1, B, 2, DFF), np.float32)
    mkp = np.zeros((1, B, 256, 4, 128), np.float32)
    mvp = np.zeros((1, B, 256, 4, 128), np.float32)
    for c in range(8):
        b, r = c // 4, c % 4
        o = res[c]
        sl = slice(r * NOWN, (r + 1) * NOWN)
        y_prompt[b, sl] = o["yT"].T
        nk[0, b, sl] = o["kT_o"].T.reshape(NOWN, 8, 64)
        nv[0, b, sl] = o["v_o"].reshape(NOWN, 8, 64)
        if r == 3:
            g = o["gla_o"]
            gp[0, b] = g.reshape(2, 64, 2, 128).transpose(2, 0, 1, 3).reshape(4, 64, 128)
            cp[0, b] = o["conv_o"].transpose(2, 1, 0).reshape(2, DFF)
        if r == 0:
            mkp[0, b] = o["mk_o"].reshape(256, 4, 128)
            mvp[0, b] = o["mv_o"].reshape(256, 4, 128)
    DB = 32
    y_sample = np.zeros((DB, 8, 1024), np.float32)
    nks = np.zeros((1, DB, 8, 8, 64), np.float32)
    nvs = np.zeros((1, DB, 8, 8, 64), np.float32)
    gs = np.zeros((1, DB, 4, 64, 128), np.float32)
    cs = np.zeros((1, DB, 2, DFF), np.float32)
    for c in range(8):
        o = res[c]
        sq = slice(4 * c, 4 * c + 4)
        y_sample[sq] = o["yT_s"].T.reshape(4, 8, 1024)
        nks[0, sq] = o["kT_s"].T.reshape(4, 8, 8, 64)
        nvs[0, sq] = o["v_s"].reshape(4, 8, 8, 64)
        gs[0, sq] = o["gla_s"].reshape(4, 2, 64, 2, 128).transpose(0, 3, 1, 2, 4).reshape(4, 4, 64, 128)
        cs[0, sq] = o["conv_s"].transpose(2, 3, 1, 0).reshape(4, 2, DFF)
    return (y_prompt, y_sample, nk, nv, nks, nvs, gp, gs, cp, cs, mkp, mvp)
```

```python
import contextlib
import numpy as np
import concourse.bass as bass
import concourse.mybir as mybir
from concourse.bass_utils import run_bass_kernel_spmd

F32 = mybir.dt.float32
BF16 = mybir.dt.bfloat16
I32 = mybir.dt.int32
AF = mybir.ActivationFunctionType
ALU = mybir.AluOpType
AX = mybir.AxisListType
ENG = ("pe", "act", "dve", "pool", "sp")
NEG = -30000.0


class _Buf:
    __slots__ = ("lw", "rd")

    def __init__(self):
        self.lw = None
        self.rd = []


class _Op:
    __slots__ = ("eng", "fn", "deps", "dma", "sig", "sem", "val", "idx", "guard")

    def __init__(self, eng, fn, dma):
        self.eng, self.fn, self.dma = eng, fn, dma
        self.deps = set()
        self.sig = False
        self.sem = None
        self.val = 0
        self.guard = None


class Sched:
    NDMA = 6

    def __init__(self):
        self.ops = []
        self.bufs = {}
        self.bar = None

    def barrier(self, fn):
        o = _Op("dve", fn, False)
        o.idx = len(self.ops)
        for b in self.bufs.values():
            if b.lw is not None:
                o.deps.add(b.lw)
            o.deps.update(b.rd)
        self.ops.append(o)
        self.bar = o.idx
        self.bufs = {}

    limit = 10 ** 9

    def _last_per_engine(self, idxs):
        last = {}
        out = []
        for i in idxs:
            o = self.ops[i]
            if o.dma:
                out.append(i)
            elif last.get(o.eng, -1) < i:
                last[o.eng] = i
        out.extend(last.values())
        return out

    def op(self, eng, fn, reads=(), writes=(), dma=False):
        if len(self.ops) >= self.limit:
            return None
        o = _Op(eng, fn, dma)
        o.idx = len(self.ops)
        if self.bar is not None:
            o.deps.add(self.bar)
        bufs = self.bufs
        for r in reads:
            b = bufs.get(r)
            if b is None:
                b = bufs[r] = _Buf()
            if b.lw is not None:
                o.deps.add(b.lw)
            if r.startswith("ps"):
                o.deps.update(self._last_per_engine(b.rd))
        for w in writes:
            b = bufs.get(w)
            if b is None:
                b = bufs[w] = _Buf()
            if b.lw is not None:
                o.deps.add(b.lw)
            o.deps.update(self._last_per_engine(b.rd))
        for r in reads:
            bufs[r].rd.append(o.idx)
        for w in writes:
            b = bufs[w]
            b.lw = o.idx
            b.rd = []
        o.deps.discard(o.idx)
        self.ops.append(o)
        return o

    def emit(self, nc):
        ops = self.ops
        for o in ops:
            for d in o.deps:
                p = ops[d]
                if p.eng == "pe" and o.eng == "pe" and not p.dma and not o.dma:
                    continue
                p.sig = True
        alld = [o for o in ops if o.dma]
        for o in alld:
            o.sig = True
        with contextlib.ExitStack() as st:
            csem = {e: st.enter_context(nc.semaphore("c_" + e)) for e in ENG}
            dsem = {e: [st.enter_context(nc.semaphore("d_%s%d" % (e, i))) for i in range(self.NDMA)]
                    for e in ("sp", "pool")}
            ccount = {e: 0 for e in ENG}
            dcount = {e: 0 for e in ENG}
            lastd = {}
            for o in ops:
                if not o.sig:
                    continue
                if o.dma:
                    i = dcount[o.eng]
                    dcount[o.eng] += 1
                    o.sem = dsem[o.eng][i % self.NDMA]
                    o.val = 16 * (i // self.NDMA + 1)
                    o.guard = lastd.get(id(o.sem))
                    lastd[id(o.sem)] = o
                else:
                    ccount[o.eng] += 1
                    o.sem = csem[o.eng]
                    o.val = ccount[o.eng]
            block = st.enter_context(nc.Block())
            per = {e: [o for o in ops if o.eng == e] for e in ENG}

            def run(e, engobj):
                waited = {}
                for o in per[e]:
                    need = {}
                    for d in o.deps:
                        p = ops[d]
                        if not p.sig:
                            continue
                        k = id(p.sem)
                        if need.get(k, (None, 0))[1] < p.val:
                            need[k] = (p.sem, p.val)
                    if o.guard is not None:
                        g = o.guard
                        k = id(g.sem)
                        if need.get(k, (None, 0))[1] < g.val:
                            need[k] = (g.sem, g.val)
                    for k, (s, v) in need.items():
                        if waited.get(k, 0) < v:
                            engobj.wait_ge(s, v)
                            waited[k] = v
                    ins = o.fn(engobj)
                    if o.sig:
                        ins.then_inc(o.sem, 16 if o.dma else 1)
                if e == "sp":
                    fin = {}
                    for o in alld:
                        k = id(o.sem)
                        if fin.get(k, (None, 0))[1] < o.val:
                            fin[k] = (o.sem, o.val)
                    for k, (s, v) in fin.items():
                        if waited.get(k, 0) < v:
                            engobj.wait_ge(s, v)
                    for ee in ENG:
                        if ccount[ee] > 0:
                            engobj.wait_ge(csem[ee], ccount[ee])

            @block.sync
            def _(eng):
                run("sp", eng)

            @block.scalar
            def _(eng):
                run("act", eng)

            @block.vector
            def _(eng):
                run("dve", eng)

            @block.gpsimd
            def _(eng):
                run("pool", eng)

            @block.tensor
            def _(eng):
                run("pe", eng)


SEQ = 8192
NOWN = 2048
HALO = 256
GROUPS = [(i * 256, 256) for i in range(9)]
NCOL = HALO + NOWN
DFF = 2816
NF = 22
INC = 3600


def build_nc(phases=("full", "moba", "gla", "cross", "tail", "smp"), ngroups=9, nfull=32, debug=False, n_phys=5120):
    nc = bass.Bass("TRN2", target_bir_lowering=False)

    def din(name, shape, dt=F32):
        return nc.dram_tensor(name, list(shape), dt, kind="ExternalInput").ap()

    def dout(name, shape, dt=F32):
        return nc.dram_tensor(name, list(shape), dt, kind="ExternalOutput").ap()

    xT_full = din("xT_full", [1024, SEQ])
    xT_own = din("xT_own", [1024, NCOL])
    memT = din("memT", [1024, 256])
    w_in = din("w_in", [1024, INC])
    w_a2 = din("w_a2", [16, 256])
    b_a = din("b_a", [1, 256])
    w_mem = din("w_mem", [1024, 1024])
    w_brm = din("w_brm", [512, 1024])
    w_brg = din("w_brg", [512, 1024])
    w_brc = din("w_brc", [512, 1024])
    w_gate = din("w_gate", [1024, 3072])
    w_out = din("w_out", [1024, 1024])
    w_up = din("w_up", [1024, 2 * DFF])
    w_down = din("w_down", [DFF, 1024])
    vecs = din("vecs", [128, 8 * 4 + 24 + 4 + NF * 4])
    blkvalid = din("blkvalid", [128, 9, 32])
    selr = din("selr", [128, 4])
    consts = din("consts", [128, 128 * 6])
    c32 = din("c32", [128, 516])
    ohrows = din("ohrows", [32, SEQ])
    xT_smp = din("xT_smp", [1024, 32])
    ptab = din("ptab", [4, 128], I32)
    poolKT = din("poolKT", [n_phys * 128, 512])
    poolV = din("poolV", [n_phys * 128, 512])
    sgla = din("sgla", [4, 128, 2, 128])
    sconv = din("sconv", [128, NF, 4, 2])
    mkTs = din("mkTs", [4, 4, 128, 256])
    mvs = din("mvs", [4, 256, 512])
    csm_d = din("csm", [128, 192])
    yT_s = dout("yT_s", [1024, 32])
    kT_s = dout("kT_s", [512, 32])
    v_s = dout("v_s", [32, 512])
    gla_s = dout("gla_s", [4, 128, 2, 128])
    conv_s = dout("conv_s", [128, NF, 4, 2])
    Ks = nc.dram_tensor("Ks", [512, SEQ], BF16, kind="Internal").ap()
    wsrc = dict(w_in=(w_in, [1024, INC]), w_brm=(w_brm, [512, 1024]), w_brg=(w_brg, [512, 1024]), w_brc=(w_brc, [512, 1024]),
                w_gate=(w_gate, [1024, 3072]), w_out=(w_out, [1024, 1024]), w_up=(w_up, [1024, 2 * DFF]), w_down=(w_down, [DFF, 1024]),
                w_mem=(w_mem, [1024, 1024]))
    wbf = {k: nc.dram_tensor(k + "_bf", shp, BF16, kind="Internal").ap() for k, (_, shp) in wsrc.items()}
    Vs = nc.dram_tensor("Vs", [SEQ, 512], BF16, kind="Internal").ap()

    yT = dout("yT", [1024, NOWN])
    kT_o = dout("kT_o", [512, NOWN])
    v_o = dout("v_o", [NOWN, 512])
    gla_o = dout("gla_o", [128, 2, 128])
    conv_o = dout("conv_o", [128, NF, 2])
    mk_o = dout("mk_o", [256, 512])
    mv_o = dout("mv_o", [256, 512])

    dbgo = {}
    if debug:
        for nm in ("om", "og", "oc", "mrg"):
            dbgo[nm] = nc.dram_tensor("d_" + nm, [128, 8, 256], BF16, kind="ExternalOutput").ap()
        for nm in ("h1", "h2", "qta", "xn"):
            dbgo[nm] = nc.dram_tensor("d_" + nm, [128, 8, 256], F32 if nm.startswith("h") else BF16, kind="ExternalOutput").ap()
    S = Sched()
    st = contextlib.ExitStack()
    with st:
        def sb(name, shape, dt=F32):
            return st.enter_context(nc.sbuf_tensor(name, list(shape), dt))

        psf = [st.enter_context(nc.psum_tensor("psf%d" % i, [128, 512], F32)) for i in range(6)]
        psb = [st.enter_context(nc.psum_tensor("psb%d" % i, [128, 1024], BF16)) for i in range(2)]
        pctr = [0, 0]

        def PS():
            i = pctr[0] % 4
            pctr[0] += 1
            return psf[i], "psf%d" % i

        def PSA(i):
            return psf[4 + i], "psf%d" % (4 + i)

        def PSB():
            i = pctr[1] % 2
            pctr[1] += 1
            return psb[i], "psb%d" % i

        def mm(out, lhsT, rhs, start, stop, R, W):
            S.op("pe", lambda e: e.matmul(out, lhsT=lhsT, rhs=rhs, start=start, stop=stop), reads=R, writes=W)

        def act(out, in_, func, R, W, **kw):
            S.op("act", lambda e: e.activation(out=out, in_=in_, func=func, **kw), reads=R, writes=W)

        def dve(fn, R, W):
            S.op("dve", fn, reads=R, writes=W)

        def pool(fn, R, W):
            S.op("pool", fn, reads=R, writes=W)

        def dma(out, in_, R, W, q="sp", **kw):
            S.op(q, lambda e: e.dma_start(out=out, in_=in_, **kw), reads=R, writes=W, dma=True)

        for k_, (src_, shp_) in wsrc.items():
            rows = shp_[0]
            for r0 in range(0, rows, 256):
                r1 = min(rows, r0 + 256)
                dma(wbf[k_][r0:r1, :], src_[r0:r1, :], [], ["wscr"], q="pool")
        w_in, w_brm, w_brg, w_brc, w_gate, w_out, w_up, w_down, w_mem = (wbf[k_] for k_ in (
            "w_in", "w_brm", "w_brg", "w_brc", "w_gate", "w_out", "w_up", "w_down", "w_mem"))
        cb = sb("cb", [128, 6, 128], BF16)
        c32t = sb("c32t", [128, 516], F32)
        vt = sb("vt", [128, 8 * 4 + 24 + 4 + NF * 4], F32)
        bv = sb("bv", [128, 9, 32], F32)
        selt = sb("selt", [128, 4], F32)
        ones = sb("ones", [128, 128], BF16)
        dma(cb[:], consts.rearrange("p (a b) -> p a b", a=6), [], ["cb"], q="pool")
        dma(c32t[:], c32, [], ["c32t"])
        dma(vt[:], vecs, [], ["vt"])
        dma(bv[:], blkvalid, [], ["bv"])
        dma(selt[:], selr, [], ["selt"])
        dve(lambda e: e.memset(ones[:], 1.0), [], ["ones"])
        keep = sb("keep", [128, 1], F32)
        dve(lambda e: e.tensor_scalar(out=keep[:], in0=selt[:, 0:1], scalar1=-1.0, scalar2=1.0, op0=ALU.mult, op1=ALU.add), ["selt"], ["keep"])
        ident = cb[:, 0, :]
        tri01 = cb[:, 1, :]
        blk2 = cb[:, 2, :]
        triU32 = c32t[:, 0:128]
        triI32 = c32t[:, 128:256]
        chk32 = c32t[:, 256:258]
        g_mix = lambda k: vt[:, k:k + 1]
        g_ffn = lambda k: vt[:, 8 + k:9 + k]
        g_fin = lambda k: vt[:, 16 + k:17 + k]
        g_mem = lambda k: vt[:, 24 + k:25 + k]
        b_gate = lambda j: vt[:, 32 + j:33 + j]
        g_gla = lambda h: vt[:, 56 + h:57 + h]
        wc = lambda i, f: vt[:, 60 + i * NF + f:61 + i * NF + f]

        xTt0_ = sb("xTt0", [128, 8, 256], F32)
        xTt = [xTt0_, xTt0_]
        sqt = sb("sqt", [128, 8, 256], BF16)
        xn = sb("xn", [128, 8, 256], BF16)
        rstd = sb("rstd", [128, 256], F32)
        wt = [sb("wt%d" % i, [128, 8, 512], BF16) for i in range(2)]
        wctr = [0]

        def norm(src, srcname, N, gfn, dst, dstname):
            act(sqt[:, :, :N], src[:, :, :N], AF.Square, [srcname], ["sqt"])
            p, pn = PS()
            for k in range(8):
                mm(p[:, :N], ones[:], sqt[:, k, :N], k == 0, k == 7, ["ones", "sqt"], [pn])
            act(rstd[:, :N], p[:, :N], AF.Sqrt, [pn], ["rstd"], scale=1.0 / 1024, bias=1e-6)
            dve(lambda e: e.reciprocal(out=rstd[:, :N], in_=rstd[:, :N]), ["rstd"], ["rstd"])
            for k in range(8):
                dve(lambda e, k=k: e.scalar_tensor_tensor(out=dst[:, k, :N], in0=src[:, k, :N], scalar=gfn(k), in1=rstd[:, :N],
                                                          op0=ALU.mult, op1=ALU.mult),
                    [srcname, "vt", "rstd"], [dstname])

        def loadw(src_ap, ncols, krows=8):
            i = wctr[0] % 2
            wctr[0] += 1
            t = wt[i]
            dma(t[:, :krows, :ncols], src_ap.rearrange("(k p) c -> p k c", p=128), ["wscr"], ["wt%d" % i])
            return t, "wt%d" % i

        mkT = sb("mkT", [128, 4, 256], BF16)
        mvb = sb("mvb", [128, 2, 512], BF16)
        stg = sb("stg", [128, 512], F32)
        dma(xTt[0][:, :, :256], memT.rearrange("(k p) t -> p k t", p=128), [], ["xTt0"])
        norm(xTt[0], "xTt0", 256, g_mem, xn, "xn")
        for half, (dst_o,) in enumerate([(mk_o,), (mv_o,)]):
            w, wn = loadw(w_mem[:, half * 512:(half + 1) * 512], 512)
            for tt in range(2):
                p, pn = PS()
                for k in range(8):
                    mm(p[:], xn[:, k, tt * 128:(tt + 1) * 128], w[:, k, :], k == 0, k == 7, ["xn", wn], [pn])
                act(stg[:], p[:], AF.Copy, [pn], ["stg"])
                if half == 1:
                    dve(lambda e, p=p, tt=tt: e.tensor_copy(out=mvb[:, tt, :], in_=p[:]), [pn], ["mvb", pn])
                dma(dst_o[tt * 128:(tt + 1) * 128, :], stg[:], ["stg"], [])
            if half == 0:
                for h in range(4):
                    p, pn = PS()
                    for k in range(8):
                        mm(p[:, :256], w[:, k, h * 128:(h + 1) * 128], xn[:, k, :256], k == 0, k == 7, ["xn", wn], [pn])
                    dve(lambda e, p=p, h=h: e.tensor_copy(out=mkT[:, h, :], in_=p[:, :256]), [pn], ["mkT"])

        wf = sb("wf", [128, 8, 1808], BF16)
        dma(wf[:, :, 0:1024], w_in[:, 512:1536].rearrange("(k p) c -> p k c", p=128), ["wscr"], ["wf"])
        dma(wf[:, :, 1024:1280], w_in[:, 1792:2048].rearrange("(k p) c -> p k c", p=128), ["wscr"], ["wf"])
        dma(wf[:, :, 1280:1808], w_in[:, 2048:2576].rearrange("(k p) c -> p k c", p=128), ["wscr"], ["wf"])
        dma(wf[:, :, 1792:1808], w_in[:, 3072:3088].rearrange("(k p) c -> p k c", p=128), ["wf", "wscr"], ["wf"])
        wa2 = sb("wa2", [16, 256], BF16)
        ba = sb("ba", [1, 256], BF16)
        dma(wa2[:], w_a2, [], ["wa2"], q="pool")
        dma(ba[:], b_a, [], ["ba"], q="pool")
        kmT = sb("kmT", [128, 4, 32], F32)
        kst = sb("kst", [128, 4, 256], BF16)
        vst = sb("vst", [128, 2, 512], BF16)
        Sst = sb("Sst", [128, 2, 128], F32)
        Ssave = sb("Ssave", [128, 3, 2, 128], F32)
        agT = sb("agT", [16, 512], BF16)
        nl = sb("nl", [128, 256], F32)
        eR = sb("eR", [128, 256], F32)
        ktil = sb("ktil", [128, 256], BF16)
        vgb = sb("vgb", [128, 512], BF16)
        dch = sb("dch", [128, 2, 2], F32)
        dve(lambda e: e.memset(Sst[:], 0.0), [], ["Sst"])

        def gla_prep(xnt, xnname, col0, wk_ap, wv_ap, wa_ap, wname):
            p, pn = PS()
            for k in range(8):
                mm(p[:16, :128], wa_ap(k), xnt[:, k, col0:col0 + 128], k == 0, k == 7, [xnname, wname], [pn])
            dve(lambda e, p=p: e.tensor_copy(out=agT[:, :128], in_=p[:16, :128]), [pn], ["agT"])
            p2, pn2 = PS()
            mm(p2[:, :256], agT[:, :128], wa2[:], True, False, ["agT", "wa2"], [pn2])
            mm(p2[:, :256], ones[0:1, :], ba[:], False, True, ["ones", "ba"], [pn2])
            act(nl[:], p2[:, :256], AF.Exp, [pn2], ["nl"], scale=-1.0)
            act(nl[:], nl[:], AF.Ln, ["nl"], ["nl"], bias=1.0)
            p3, pn3 = PS()
            mm(p3[:, :256], triU32, nl[:], True, True, ["c32t", "nl"], [pn3])
            act(eR[:], p3[:, :256], AF.Exp, [pn3], ["eR"])
            pk, pkn = PS()
            for k in range(8):
                mm(pk[:, :256], xnt[:, k, col0:col0 + 128], wk_ap(k), k == 0, k == 7, [xnname, wname], [pkn])
            dve(lambda e, pk=pk: e.tensor_tensor(out=ktil[:], in0=pk[:, :256], in1=eR[:], op=ALU.mult), [pkn, "eR"], ["ktil"])
            pv, pvn = PS()
            for k in range(8):
                mm(pv[:], xnt[:, k, col0:col0 + 128], wv_ap(k), k == 0, k == 7, [xnname, wname], [pvn])
            act(vgb[:], pv[:], AF.Copy, [pvn], ["vgb"])
            pd, pdn = PS()
            for half in range(2):
                mm(pd[:, half * 2:half * 2 + 2], nl[:, half * 128:(half + 1) * 128], chk32, True, True, ["nl", "c32t"], [pdn])
            act(dch[:], pd[:, 0:4].rearrange("p (a b) -> p a b", a=2), AF.Exp, [pdn], ["dch"])

        def gla_state_update(ci):
            for hp in range(2):
                pu, pun = PS()
                mm(pu[:, :256], ktil[ci * 64:(ci + 1) * 64, hp * 128:(hp + 1) * 128], vgb[ci * 64:(ci + 1) * 64, hp * 256:(hp + 1) * 256],
                   True, True, ["ktil", "vgb"], [pun])
                for hh in range(2):
                    rs = slice(hh * 64, (hh + 1) * 64)
                    dve(lambda e, pu=pu, hp=hp, hh=hh, rs=rs: e.scalar_tensor_tensor(
                        out=Sst[rs, hp, :], in0=Sst[rs, hp, :], scalar=dch[rs, hp, ci:ci + 1], in1=pu[rs, hh * 128:(hh + 1) * 128],
                        op0=ALU.mult, op1=ALU.add), [pun, "dch", "Sst"], ["Sst"])

        for g in range(nfull if "full" in phases else 0):
            xt = xTt[0]
            xname = "xTt0"
            dma(xt[:], xT_full[:, g * 256:(g + 1) * 256].rearrange("(k p) t -> p k t", p=128), [], [xname])
            norm(xt, xname, 256, g_mix, xn, "xn")
            for c in range(4):
                p, pn = PS()
                for k in range(8):
                    mm(p[:, :256], wf[:, k, c * 128:(c + 1) * 128], xn[:, k, :], k == 0, k == 7, ["xn", "wf"], [pn])
                act(kst[:, c, :], p[:, :256], AF.Copy, [pn], ["kst"])
                dve(lambda e, p=p, c=c, g=g: e.reduce_sum(out=kmT[:, c, g:g + 1], in_=p[:, :256], axis=AX.X), [pn], ["kmT"])
            dma(Ks.rearrange("(c p) t -> p c t", p=128)[:, :, g * 256:(g + 1) * 256], kst[:], ["kst"], ["Ks"])
            for tt in range(2):
                p, pn = PS()
                for k in range(8):
                    mm(p[:], xn[:, k, tt * 128:(tt + 1) * 128], wf[:, k, 512:1024], k == 0, k == 7, ["xn", "wf"], [pn])
                act(vst[:, tt, :], p[:], AF.Copy, [pn], ["vst"])
            dma(Vs[g * 256:(g + 1) * 256, :].rearrange("(t p) c -> p t c", p=128), vst[:], ["vst"], ["Vs"])
            for tt in range(2):
                gla_prep(xn, "xn", tt * 128, lambda k: wf[:, k, 1024:1280], lambda k: wf[:, k, 1280:1792],
                         lambda k: wf[:, k, 1792:1808], "wf")
                for ci in range(2):
                    chunk = g * 4 + tt * 2 + ci
                    for i, cb_ in enumerate((28, 60, 92)):
                        if chunk == cb_:
                            dve(lambda e, i=i: e.tensor_copy(out=Ssave[:, i, :, :], in_=Sst[:]), ["Sst"], ["Ssave"])
                    gla_state_update(ci)
        dma(gla_o, Sst[:], ["Sst"], [])
        kmbd = sb("kmbd", [128, 4, 64], BF16)
        dve(lambda e: e.memset(kmbd[:], 0.0), [], ["kmbd"])
        dve(lambda e: e.tensor_copy(out=kmbd[0:64, :, 0:32], in_=kmT[0:64, :, :]), ["kmT", "kmbd"], ["kmbd"])
        dve(lambda e: e.tensor_copy(out=kmbd[64:128, :, 32:64], in_=kmT[64:128, :, :]), ["kmT", "kmbd"], ["kmbd"])

        Srun = sb("Srun", [128, 2, 128], F32)
        dve(lambda e: e.tensor_scalar(out=Srun[:], in0=Ssave[:, 0, :, :], scalar1=selt[:, 1:2], scalar2=None, op0=ALU.mult), ["Ssave", "selt"], ["Srun"])
        for i in (1, 2):
            dve(lambda e, i=i: e.scalar_tensor_tensor(out=Srun[:], in0=Ssave[:, i, :, :], scalar=selt[:, i + 1:i + 2], in1=Srun[:],
                                                      op0=ALU.mult, op1=ALU.add), ["Ssave", "selt", "Srun"], ["Srun"])

        xo = xTt[0]
        hT = sb("hT", [128, 8, 256], F32)
        qT = sb("qT", [128, 4, 256], BF16)
        Bq = sb("Bq", [128, 2, 8, 96], BF16)
        qTa = sb("qTa", [96, 8, 256], BF16)
        kla = sb("kla", [96, 8, 256], BF16)
        vloc = sb("vloc", [128, 2, 512], BF16)
        gsel = sb("gsel", [128, 8, 32], F32)
        m8 = sb("m8", [128, 8, 8], F32)
        kaug0 = sb("kaug0", [96, SEQ], BF16)
        kaug = [kaug0, kaug0]
        vaug0 = sb("vaug0", [128, 64, 128], BF16)
        pT = [sb("pT%d" % i, [128, 512], BF16) for i in range(2)]
        rden = sb("rden", [128, 256], F32)
        omT = sb("omT", [128, 4, 256], BF16)
        ogT = sb("ogT", [128, 4, 256], BF16)
        ocT = sb("ocT", [128, 4, 256], BF16)
        mrg = sb("mrg", [128, 8, 256], BF16)
        hn = mrg
        sg = sb("sg", [128, 3, 256], F32)
        tmpf = sb("tmpf", [128, 256], F32)
        uext2 = sb("uext2", [128, 258], F32)
        tmpf2 = uext2
        aT = sb("aT", [128, NF, 256], BF16)
        uext = sb("uext", [128, 258], F32)
        uprev = sb("uprev", [128, NF, 2], F32)
        yst = xTt[0]
        kstg = hT
        qgT = sb("qgT", [128, 2, 256], F32)
        kgT = sb("kgT", [128, 2, 256], F32)
        qtl = sb("qtl", [128, 4, 256], BF16)
        ktl = sb("ktl", [128, 2, 256], BF16)
        egp = sb("egp", [128, 128], F32)
        egn = sb("egn", [128, 128], F32)
        rsil = sb("rsil", [128, 4, 256], F32)
        ogf = sb("ogf", [128, 4, 256], F32)
        qcT = sb("qcT", [128, 4, 256], BF16)

        dma(kaug0[64:96, :], ohrows, [], ["kaug0"], q="pool")
        dve(lambda e: e.memset(kla[:], 0.0), [], ["kla"])
        dve(lambda e: e.memset(uprev[:], 0.0), [], ["uprev"])
        dve(lambda e: e.memset(Bq[:], 0.0), [], ["Bq"])
        dve(lambda e: e.memset(qtl[:], 0.0), [], ["qtl"])
        selmat = [c32t[:, 260 + i * 128:260 + (i + 1) * 128] for i in range(2)]
        triB = cb[:, 3, :]

        def st_update(ci, St, Sn):
            for hp in range(2):
                pu, pun = PS()
                mm(pu[:, :256], ktil[ci * 64:(ci + 1) * 64, hp * 128:(hp + 1) * 128], vgb[ci * 64:(ci + 1) * 64, hp * 256:(hp + 1) * 256],
                   True, True, ["ktil", "vgb"], [pun])
                for hh in range(2):
                    rs = slice(hh * 64, (hh + 1) * 64)
                    dve(lambda e, pu=pu, hp=hp, hh=hh, rs=rs: e.scalar_tensor_tensor(
                        out=St[rs, hp, :], in0=St[rs, hp, :], scalar=dch[rs, hp, ci:ci + 1], in1=pu[rs, hh * 128:(hh + 1) * 128],
                        op0=ALU.mult, op1=ALU.add), [pun, "dch", Sn], [Sn])

        Sbf = [sb("Sbf%d" % i, [128, 2, 128], BF16) for i in range(2)]
        ATf = sb("ATf", [128, 128], BF16)

        def gla_own(gi, c0, N):
            ntile = N // 128
            w, wn = loadw(w_in[:, 1536:2048], 512)
            for j, dst, dn in ((0, qgT, "qgT"), (1, kgT, "kgT")):
                for c in range(2):
                    p, pn = PS()
                    for k in range(8):
                        mm(p[:, :N], w[:, k, j * 256 + c * 128:j * 256 + (c + 1) * 128], xn[:, k, :N], k == 0, k == 7, ["xn", wn], [pn])
                    act(dst[:, c, :N], p[:, :N], AF.Copy, [pn], [dn])
            w, wn = loadw(w_in[:, 2560:3072], 512)
            for c in range(4):
                p, pn = PS()
                for k in range(8):
                    mm(p[:, :N], w[:, k, c * 128:(c + 1) * 128], xn[:, k, :N], k == 0, k == 7, ["xn", wn], [pn])
                act(rsil[:, c, :N], p[:, :N], AF.Silu, [pn], ["rsil"])
            for tt in range(ntile):
                ts_ = slice(tt * 128, (tt + 1) * 128)
                gla_prep(xn, "xn", tt * 128, lambda k: wf[:, k, 1024:1280], lambda k: wf[:, k, 1280:1792],
                         lambda k: wf[:, k, 1792:1808], "wf")
                for hp in range(2):
                    pgm, pgn = PS()
                    mm(pgm[:, :128], nl[:, hp * 128:(hp + 1) * 128], triI32, True, True, ["nl", "c32t"], [pgn])
                    act(egp[:], pgm[:, :128], AF.Exp, [pgn], ["egp"])
                    act(egn[:], pgm[:, :128], AF.Exp, [pgn], ["egn"], scale=-1.0)
                    for par in range(2):
                        prs = slice(par * 64, par * 64 + 64)
                        dve(lambda e, hp=hp, ts_=ts_, prs=prs, par=par: e.scalar_tensor_tensor(
                            out=qtl[prs, hp * 2 + par, ts_], in0=qgT[prs, hp, ts_], scalar=0.125, in1=egp[prs, :],
                            op0=ALU.mult, op1=ALU.mult), ["qgT", "egp"], ["qtl"])
                    dve(lambda e, hp=hp, ts_=ts_: e.tensor_tensor(out=ktl[:, hp, ts_], in0=kgT[:, hp, ts_], in1=egn[:], op=ALU.mult),
                        ["kgT", "egn"], ["ktl"])
                dve(lambda e: e.tensor_copy(out=Sbf[0][:], in_=Srun[:]), ["Srun"], ["Sbf0"])
                st_update(0, Srun, "Srun")
                dve(lambda e: e.tensor_copy(out=Sbf[1][:], in_=Srun[:]), ["Srun"], ["Sbf1"])
                st_update(1, Srun, "Srun")
                for h in range(4):
                    rs = slice((h % 2) * 64, (h % 2) * 64 + 64)
                    hp = h // 2
                    pa, pan = PS()
                    mm(pa[:, :128], ktl[:, hp, ts_], qtl[:, h, ts_], True, True, ["ktl", "qtl"], [pan])
                    dve(lambda e, pa=pa: e.tensor_tensor(out=ATf[:], in0=pa[:, :128], in1=triB, op=ALU.mult), [pan, "cb"], ["ATf"])
                    po, pon = PSA(0)
                    mm(po[:, :128], vgb[:, h * 128:(h + 1) * 128], ATf[:], True, False, ["vgb", "ATf"], [pon])
                    for ci in range(2):
                        mm(po[:, ci * 64:(ci + 1) * 64], Sbf[ci][:, hp, :], qtl[:, h, tt * 128 + ci * 64:tt * 128 + (ci + 1) * 64],
                           False, ci == 1, ["Sbf%d" % ci, "qtl"], [pon])
                    act(ogf[:, h, ts_], po[:, :128], AF.Copy, [pon], ["ogf"])
            gla_norm(N)

        def gla_norm(N):
            for h in range(4):
                act(sqt[:, h, :N], ogf[:, h, :N], AF.Square, ["ogf"], ["sqt"])
                p, pn = PS()
                mm(p[:, :N], ones[:], sqt[:, h, :N], True, True, ["ones", "sqt"], [pn])
                act(rstd[:, :N], p[:, :N], AF.Sqrt, [pn], ["rstd"], scale=1.0 / 128, bias=1e-6)
                dve(lambda e: e.reciprocal(out=rstd[:, :N], in_=rstd[:, :N]), ["rstd"], ["rstd"])
                dve(lambda e, h=h: e.scalar_tensor_tensor(out=ogf[:, h, :N], in0=ogf[:, h, :N], scalar=g_gla(h), in1=rstd[:, :N],
                                                          op0=ALU.mult, op1=ALU.mult), ["ogf", "vt", "rstd"], ["ogf"])
                dve(lambda e, h=h: e.tensor_tensor(out=ogT[:, h, :N], in0=ogf[:, h, :N], in1=rsil[:, h, :N], op=ALU.mult), ["ogf", "rsil"], ["ogT"])

        def gla_proj(N):
            w, wn = loadw(w_in[:, 1536:2048], 512)
            for j, dst, dn in ((0, qgT, "qgT"), (1, kgT, "kgT")):
                for c in range(2):
                    p, pn = PS()
                    for k in range(8):
                        mm(p[:, :N], w[:, k, j * 256 + c * 128:j * 256 + (c + 1) * 128], xn[:, k, :N], k == 0, k == 7, ["xn", wn], [pn])
                    act(dst[:, c, :N], p[:, :N], AF.Copy, [pn], [dn])
            w, wn = loadw(w_in[:, 2560:3072], 512)
            for c in range(4):
                p, pn = PS()
                for k in range(8):
                    mm(p[:, :N], w[:, k, c * 128:(c + 1) * 128], xn[:, k, :N], k == 0, k == 7, ["xn", wn], [pn])
                act(rsil[:, c, :N], p[:, :N], AF.Silu, [pn], ["rsil"])

        def cross_q(N):
            w, wn = loadw(w_in[:, 3088:3600], 512)
            for h in range(4):
                p, pn = PS()
                for k in range(8):
                    mm(p[:, :N], w[:, k, h * 128:(h + 1) * 128], xn[:, k, :N], k == 0, k == 7, ["xn", wn], [pn])
                act(qcT[:, h, :N], p[:, :N], AF.Copy, [pn], ["qcT"])

        def cross_att(c0_, N):
            cs = slice(c0_, c0_ + N)
            for h in range(4):
                po, pon = PSA(0)
                pd, pdn = PSA(1)
                for mt in range(2):
                    ps_, psn = PS()
                    mm(ps_[:, :N], mkT[:, h, mt * 128:(mt + 1) * 128], qcT[:, h, cs], True, True, ["mkT", "qcT"], [psn])
                    ptile = pT[mt]
                    ptn = "pT%d" % mt
                    act(ptile[:, :N], ps_[:, :N], AF.Exp, [psn], [ptn], scale=128 ** -0.5)
                    mm(po[:, :N], mvb[:, mt, h * 128:(h + 1) * 128], ptile[:, :N], mt == 0, mt == 1, ["mvb", ptn], [pon])
                    mm(pd[:, :N], ones[:], ptile[:, :N], mt == 0, mt == 1, ["ones", ptn], [pdn])
                dve(lambda e, pd=pd: e.reciprocal(out=rden[:, :N], in_=pd[:, :N]), [pdn], ["rden"])
                dve(lambda e, po=po, h=h: e.tensor_tensor(out=ocT[:, h, cs], in0=po[:, :N], in1=rden[:, :N], op=ALU.mult), [pon, "rden"], ["ocT"])

        def cross_own(gi, c0, N):
            cross_q(N)
            cross_att(0, N)

        def tail_own(gi, c0, N, smp=False):
            oc0 = c0 - HALO
            brs = ((w_brm, omT, "omT"), (w_brg, ogT, "ogT"), (w_brc, ocT, "ocT"))
            wbr = []
            for (wd, _, _) in brs:
                wbr.append(None)
            for half in range(8):
                wg_t = []
                for i in range(3):
                    wg_t.append(loadw3[i](w_gate[:, i * 1024 + half * 128:i * 1024 + (half + 1) * 128], 128))
                wb_t = []
                for i, (wd, _, _) in enumerate(brs):
                    wb_t.append(loadw3b[i](wd[:, half * 128:(half + 1) * 128], 128, 4))
                for c4 in range(1):
                    c8 = half
                    cs = slice(c4 * 128, (c4 + 1) * 128)
                    for i in range(3):
                        w, wn = wg_t[i]
                        p, pn = PS()
                        for k in range(8):
                            mm(p[:, :N], w[:, k, cs], xn[:, k, :N], k == 0, k == 7, ["xn", wn], [pn])
                        act(sg[:, i, :N], p[:, :N], AF.Sigmoid, [pn, "vt"], ["sg%d" % i], bias=b_gate(i * 8 + c8))
                    for i, (wd, oT_, on_) in enumerate(brs):
                        w, wn = wb_t[i]
                        p, pn = PS()
                        for k in range(4):
                            mm(p[:, :N], w[:, k, cs], oT_[:, k, :N], k == 0, k == 3, [on_, wn], [pn])
                        if i == 0:
                            dve(lambda e, p=p: e.tensor_tensor(out=tmpf[:, :N], in0=p[:, :N], in1=sg[:, 0, :N], op=ALU.mult), [pn, "sg0"], ["tmpf"])
                        else:
                            dve(lambda e, p=p, i=i: e.tensor_tensor(out=tmpf2[:, :N], in0=p[:, :N], in1=sg[:, i, :N], op=ALU.mult), [pn, "sg%d" % i], ["uext2"])
                            if i == 1:
                                dve(lambda e: e.tensor_tensor(out=tmpf[:, :N], in0=tmpf[:, :N], in1=tmpf2[:, :N], op=ALU.add), ["tmpf", "uext2"], ["tmpf"])
                            else:
                                dve(lambda e, c8=c8: e.tensor_tensor(out=mrg[:, c8, :N], in0=tmpf[:, :N], in1=tmpf2[:, :N], op=ALU.add),
                                    ["tmpf", "uext2"], ["mrg"])
            for half in range(2):
                w, wn = loadw(w_out[:, half * 512:(half + 1) * 512], 512)
                for c4 in range(4):
                    c8 = half * 4 + c4
                    p, pn = PS()
                    for k in range(8):
                        mm(p[:, :N], w[:, k, c4 * 128:(c4 + 1) * 128], mrg[:, k, :N], k == 0, k == 7, ["mrg", wn], [pn])
                    dve(lambda e, p=p, c8=c8: e.tensor_tensor(out=hT[:, c8, :N], in0=p[:, :N], in1=xo[:, c8, :N], op=ALU.add), [pn, "xTt0"], ["hT"])
            if debug and gi == 1:
                dma(dbgo["mrg"], mrg[:], ["mrg"], [])
                dma(dbgo["h1"], hT[:], ["hT"], [])
            norm(hT, "hT", N, g_ffn, hn, "mrg")
            for f4 in range(0, NF, 4):
                nf = min(4, NF - f4)
                wu, wun = loadw(w_up[:, f4 * 128:(f4 + nf) * 128], nf * 128)
                for fi in range(nf):
                    f = f4 + fi
                    pu, pun = PS()
                    for k in range(8):
                        mm(pu[:, :N], wu[:, k, fi * 128:(fi + 1) * 128], hn[:, k, :N], k == 0, k == 7, ["mrg", wun], [pun])
                    wgt, wgn = loadw3[f % 3](w_up[:, DFF + f * 128:DFF + (f + 1) * 128], 128)
                    pg_, pgn_ = PS()
                    for k in range(8):
                        mm(pg_[:, :N], wgt[:, k, :128], hn[:, k, :N], k == 0, k == 7, ["mrg", wgn], [pgn_])
                    ux, uxn = (uext, "uext") if f % 2 == 0 else (uext2, "uext2")
                    tf, tfn = (tmpf, "tmpf") if f % 2 == 0 else (sg[:, 0, :], "sg0")
                    if smp:
                        u3 = ux[:, 0:40].rearrange("p (s c) -> p s c", c=10)
                        t3 = tf[:, 0:32].rearrange("p (s c) -> p s c", c=8)
                        dve(lambda e, f=f, u3=u3: e.tensor_copy(out=u3[:, :, 0:2], in_=sconvt[:, f, :, :]), ["kst"], [uxn])
                        act(u3[:, :, 2:10], pu[:, :32].rearrange("p (s c) -> p s c", c=8), AF.Copy, [pun, uxn], [uxn])
                        dve(lambda e, f=f, u3=u3: e.tensor_copy(out=ulasts[:, f, :, :], in_=u3[:, :, 8:10]), [uxn], ["kst"])
                        dve(lambda e, f=f, u3=u3, t3=t3: e.tensor_scalar(out=t3, in0=u3[:, :, 2:10], scalar1=wc(2, f), scalar2=wc(3, f), op0=ALU.mult, op1=ALU.add),
                            [uxn, "vt"], [tfn])
                        dve(lambda e, f=f, u3=u3, t3=t3: e.scalar_tensor_tensor(out=t3, in0=u3[:, :, 1:9], scalar=wc(1, f), in1=t3, op0=ALU.mult, op1=ALU.add),
                            [uxn, "vt", tfn], [tfn])
                        dve(lambda e, f=f, u3=u3, t3=t3: e.scalar_tensor_tensor(out=t3, in0=u3[:, :, 0:8], scalar=wc(0, f), in1=t3, op0=ALU.mult, op1=ALU.add),
                            [uxn, "vt", tfn], [tfn])
                    else:
                        if gi == 1:
                            dve(lambda e, f=f, ux=ux: e.tensor_scalar(out=ux[:, 0:2], in0=uprev[:, f, :], scalar1=keep[:, 0:1], scalar2=None, op0=ALU.mult),
                                ["uprev", "keep"], [uxn])
                        else:
                            dve(lambda e, f=f, ux=ux: e.tensor_copy(out=ux[:, 0:2], in_=uprev[:, f, :]), ["uprev"], [uxn])
                        act(ux[:, 2:2 + N], pu[:, :N], AF.Copy, [pun, uxn], [uxn])
                        dve(lambda e, f=f, ux=ux: e.tensor_copy(out=uprev[:, f, :], in_=ux[:, N:N + 2]), [uxn], ["uprev"])
                        dve(lambda e, f=f, ux=ux, tf=tf: e.tensor_scalar(out=tf[:, :N], in0=ux[:, 2:2 + N], scalar1=wc(2, f), scalar2=wc(3, f), op0=ALU.mult, op1=ALU.add),
                            [uxn, "vt"], [tfn])
                        dve(lambda e, f=f, ux=ux, tf=tf: e.scalar_tensor_tensor(out=tf[:, :N], in0=ux[:, 1:1 + N], scalar=wc(1, f), in1=tf[:, :N], op0=ALU.mult, op1=ALU.add),
                            [uxn, "vt", tfn], [tfn])
                        dve(lambda e, f=f, ux=ux, tf=tf: e.scalar_tensor_tensor(out=tf[:, :N], in0=ux[:, 0:N], scalar=wc(0, f), in1=tf[:, :N], op0=ALU.mult, op1=ALU.add),
                            [uxn, "vt", tfn], [tfn])
                    act(tf[:, :N], tf[:, :N], AF.Gelu_apprx_tanh, [tfn], [tfn])
                    dve(lambda e, pg_=pg_, f=f, tf=tf: e.tensor_tensor(out=aT[:, f, :N], in0=tf[:, :N], in1=pg_[:, :N], op=ALU.mult), [tfn, pgn_], ["aT"])
            if smp:
                dma(conv_s, ulasts, ["kst"], [])
            elif gi == len(GROUPS) - 1:
                dma(conv_o, uprev[:], ["uprev"], [])
            for c8 in range(8):
                p, pn = PSA(c8 % 2)
                for f4 in range(0, NF, 4):
                    nf = min(4, NF - f4)
                    w, wn = loadwd(w_down[f4 * 128:(f4 + nf) * 128, c8 * 128:(c8 + 1) * 128], 128, nf)
                    for fi in range(nf):
                        f = f4 + fi
                        mm(p[:, :N], w[:, fi, :128], aT[:, f, :N], f == 0, f == NF - 1, ["aT", wn], [pn])
                dve(lambda e, p=p, c8=c8: e.tensor_tensor(out=hT[:, c8, :N], in0=p[:, :N], in1=hT[:, c8, :N], op=ALU.add), [pn, "hT"], ["hT"])
            if debug and gi == 1:
                dma(dbgo["h2"], hT[:], ["hT"], [])
            if gi > 0 or smp:
                act(sqt[:, :, :N], hT[:, :, :N], AF.Square, ["hT"], ["sqt"])
                p, pn = PS()
                for k in range(8):
                    mm(p[:, :N], ones[:], sqt[:, k, :N], k == 0, k == 7, ["ones", "sqt"], [pn])
                act(rstd[:, :N], p[:, :N], AF.Sqrt, [pn], ["rstd"], scale=1.0 / 1024, bias=1e-6)
                dve(lambda e: e.reciprocal(out=rstd[:, :N], in_=rstd[:, :N]), ["rstd"], ["rstd"])
                for k in range(8):
                    dve(lambda e, k=k: e.scalar_tensor_tensor(out=yst[:, k, :N], in0=hT[:, k, :N], scalar=g_fin(k), in1=rstd[:, :N],
                                                              op0=ALU.mult, op1=ALU.mult), ["hT", "vt", "rstd"], ["xTt0"])
                if smp:
                    dma(yT_s.rearrange("(k p) t -> p k t", p=128), yst[:, :, :N], ["xTt0"], [])
                else:
                    dma(yT.rearrange("(k p) t -> p k t", p=128)[:, :, oc0:oc0 + N], yst[:, :, :N], ["xTt0"], [])

        wg3 = [sb("wg3_%d" % i, [128, 8, 128], BF16) for i in range(3)]
        wb3 = [sb("wb3_%d" % i, [128, 4, 128], BF16) for i in range(3)]
        wdt = [sb("wdt%d" % i, [128, 4, 128], BF16) for i in range(2)]
        wdctr = [0]

        def mk_loadw3(i):
            def f(src_ap, ncols):
                dma(wg3[i][:, :, :ncols], src_ap.rearrange("(k p) c -> p k c", p=128), ["wscr"], ["wg3_%d" % i])
                return wg3[i], "wg3_%d" % i
            return f

        def mk_loadw3b(i):
            def f(src_ap, ncols, kr):
                dma(wb3[i][:, :kr, :ncols], src_ap.rearrange("(k p) c -> p k c", p=128), ["wscr"], ["wb3_%d" % i])
                return wb3[i], "wb3_%d" % i
            return f

        loadw3 = [mk_loadw3(i) for i in range(3)]
        loadw3b = [mk_loadw3b(i) for i in range(3)]

        def loadwd(src_ap, ncols, kr):
            i = wdctr[0] % 2
            wdctr[0] += 1
            dma(wdt[i][:, :kr, :ncols], src_ap.rearrange("(k p) c -> p k c", p=128), ["wscr"], ["wdt%d" % i])
            return wdt[i], "wdt%d" % i

        vstf = vst[:, :, :].rearrange("p a b -> p (a b)")
        kstf = kst[:, :, :].rearrange("p a b -> p (a b)").bitcast(F32)
        csm = vstf[:, 128:512].bitcast(F32)
        knT = vstf[:, 0:128].rearrange("p (c t) -> p c t", t=32)
        gs2 = vstf[0:64, 512:640].bitcast(F32)
        biasb = vstf[0:64, 640:704]
        biasT = vstf[0:64, 704:768]
        m8s = vstf[0:64, 768:784].bitcast(F32)
        rd1 = vstf[0:64, 784:786].bitcast(F32)
        sconvt = kstf[:, 0:176].rearrange("p (f s i) -> p f s i", s=4, i=2)
        ulasts = kstf[:, 176:352].rearrange("p (f s i) -> p f s i", s=4, i=2)
        Qbd = Bq[:, :, :, :].rearrange("p a b c -> p (a b c)")[:, 0:1024].rearrange("p (a b c) -> p a b c", a=4, b=4)
        vnew = qTa[0:8, :, :].rearrange("p a b -> p (a b)").rearrange("p (s c) -> p s c", c=512)
        gself = gsel[:, :, :].rearrange("p a b -> p (a b)")
        ptf = gself[:, 0:128]
        pti = gself[:, 128:256].bitcast(I32)
        idxu = rden[:, 0:128].bitcast(I32)
        cm8 = csm[0:8, 1:65]
        EtA = csm[0:64, 65:129].rearrange("p (h t) -> p h t", t=8)
        tri8I = csm[0:8, 129:137]
        tri8U = csm[0:8, 137:145]
        tri8m = csm[0:8, 145:153]
        m16c = csm[0:8, 153:154]

        def sample_phase():
            N = 32
            dma(csm, csm_d, [], ["vst"])
            dma(sconvt, sconv, [], ["kst"])
            dma(xo[:, :, :N], xT_smp.rearrange("(k p) t -> p k t", p=128), [], ["xTt0"])
            norm(xo, "xTt0", N, g_mix, xn, "xn")
            w, wn = loadw(w_in[:, 0:512], 512)
            for c in range(4):
                p, pn = PS()
                for k in range(8):
                    mm(p[:, :N], w[:, k, c * 128:(c + 1) * 128], xn[:, k, :N], k == 0, k == 7, ["xn", wn], [pn])
                act(qT[:, c, :N], p[:, :N], AF.Copy, [pn], ["qT"])
            w, wn = loadw(w_in[:, 512:1024], 512)
            for c in range(4):
                p, pn = PS()
                for k in range(8):
                    mm(p[:, :N], w[:, k, c * 128:(c + 1) * 128], xn[:, k, :N], k == 0, k == 7, ["xn", wn], [pn])
                act(knT[:, c, :], p[:, :N], AF.Copy, [pn], ["vst"])
                dve(lambda e, p=p, c=c: e.tensor_copy(out=kstg[:, c, :N], in_=p[:, :N]), [pn], ["hT"])
            dma(kT_s.rearrange("(c p) t -> p c t", p=128), kstg[:, 0:4, :N], ["hT"], [])
            w, wn = loadw(w_in[:, 1024:1536], 512)
            for s_ in range(4):
                p, pn = PS()
                for k in range(8):
                    mm(p[0:8, :], xn[:, k, s_ * 8:(s_ + 1) * 8], w[:, k, :], k == 0, k == 7, ["xn", wn], [pn])
                act(vnew[:, s_, :], p[0:8, :], AF.Copy, [pn], ["qTa"])
                dve(lambda e, p=p: e.tensor_copy(out=stg[0:8, :], in_=p[0:8, :]), [pn], ["stg"])
                dma(v_s[s_ * 8:(s_ + 1) * 8, :], stg[0:8, :], ["stg"], [])
            dve(lambda e: e.memset(Qbd, 0.0), [], ["Bq"])
            for hp in range(4):
                for h2 in range(2):
                    h = 2 * hp + h2
                    prs = slice(h2 * 64, h2 * 64 + 64)
                    dve(lambda e, hp=hp, h=h, prs=prs: e.tensor_copy(out=Qbd[prs, hp, :, h * 8:(h + 1) * 8],
                                                                     in_=qT[prs, hp, 0:32].rearrange("p (s t) -> p s t", t=8)), ["qT", "Bq"], ["Bq"])
            Sraw = vaug0[:, :, :].rearrange("p a (b c) -> p (a b) c", c=64)
            NS = 8
            kslot = [("wt0s%d" % j, wt[0][:, j, :]) for j in range(NS)]
            vslot = [("wt1s%d" % j, wt[1][:, j, :]) for j in range(NS)]
            dve(lambda e: e.memset(m8[0:1, 0, 0:1], 0.0), ["wt0", "wt1"], [n for n, _ in kslot + vslot] + ["m8"])
            for s_ in range(4):
                cs = slice(s_ * 8, (s_ + 1) * 8)
                dma(pti, ptab[s_:s_ + 1, :].to_broadcast([128, 128]), [], ["gsel"])
                dve(lambda e: e.tensor_copy(out=ptf, in_=pti), ["gsel"], ["gsel"])
                dve(lambda e: e.tensor_scalar(out=ptf, in0=ptf, scalar1=128.0, scalar2=csm[:, 0:1], op0=ALU.mult, op1=ALU.add), ["gsel", "vst"], ["gsel"])
                dve(lambda e: e.tensor_copy(out=idxu, in_=ptf), ["gsel"], ["rden"])
                pgate, pgaten = PSA(1)
                for pg in range(128):
                    ktn, ktf = kslot[pg % NS]
                    kt = ktf.rearrange("p (a b) -> p a b", b=128)
                    S.op("pool", lambda e, ktf=ktf, pg=pg: e.indirect_dma_start(
                        out=ktf, out_offset=None, in_=poolKT,
                        in_offset=bass.IndirectOffsetOnAxis(ap=idxu[:, pg:pg + 1], axis=0)), reads=["rden"], writes=[ktn], dma=True)
                    ps_, psn = PS()
                    for hp in range(4):
                        mm(ps_[:, :64], kt[:, hp, :], Qbd[:, hp, s_, :], hp == 0, hp == 3, [ktn, "Bq"], [psn])
                    act(Sraw[:, pg, :], ps_[:, :64], AF.Copy, [psn], ["vaug0"])
                    if pg >= 1:
                        q_ = pg - 1
                        mm(pgate[0:64, q_ // 2:q_ // 2 + 1], Sraw[:, q_, :], ones[:, 0:1], q_ % 2 == 0, q_ % 2 == 1, ["vaug0", "ones"], [pgaten])
                mm(pgate[0:64, 63:64], Sraw[:, 127, :], ones[:, 0:1], False, True, ["vaug0", "ones"], [pgaten])
                dve(lambda e, pgate=pgate: e.tensor_copy(out=gs2, in_=pgate[0:64, 0:64]), [pgaten], ["vst"])
                dve(lambda e: e.max(out=m8s, in_=gs2), ["vst"], ["vst"])
                dve(lambda e: e.tensor_scalar(out=gs2, in0=gs2, scalar1=m8s[:, 2:3], scalar2=None, op0=ALU.is_ge), ["vst", "vst"], ["vst"])
                dve(lambda e: e.tensor_scalar(out=biasb, in0=gs2, scalar1=-1.0, scalar2=-NEG, op0=ALU.add, op1=ALU.mult), ["vst"], ["vst"])
                pb, pbn = PSB()
                S.op("pe", lambda e, pb=pb: e.transpose(out=pb[0:64, 0:64], in_=biasb, identity=cb[0:64, 0, 0:64]), reads=["vst", "cb"], writes=[pbn])
                act(biasT, pb[0:64, 0:64], AF.Copy, [pbn], ["vst"])
                po, pon = PSA(0)
                pden, pdenn = PSA(1)
                pbl = {}

                def issue_bias(b4):
                    pb_, pb_n = PS()
                    for j in range(4):
                        blk = (b4 * 4 + j) // 2
                        mm(pb_[:, j * 64:(j + 1) * 64], cb[0:64, 0, blk:blk + 1].to_broadcast([64, 128]), biasT, True, True, ["cb", "vst"], [pb_n])
                    pbl[b4] = (pb_, pb_n)

                issue_bias(0)
                for b4 in range(32):
                    if b4 + 1 < 32:
                        issue_bias(b4 + 1)
                    pb_, pb_n = pbl.pop(b4)
                    dve(lambda e, pb_=pb_, b4=b4: e.scalar_tensor_tensor(
                        out=tmpf[:, :256], in0=Sraw[:, b4 * 4:(b4 + 1) * 4, :].rearrange("p a b -> p (a b)"), scalar=0.125, in1=pb_[:, :256],
                        op0=ALU.mult, op1=ALU.add), ["vaug0", pb_n], ["tmpf"])
                    ptile = pT[b4 % 2]
                    ptn = "pT%d" % (b4 % 2)
                    act(ptile[:, :256], tmpf[:, :256], AF.Exp, ["tmpf"], [ptn])
                    for j in range(4):
                        pg = b4 * 4 + j
                        vtn, vt_ = vslot[pg % NS]
                        S.op("pool", lambda e, vt_=vt_, pg=pg: e.indirect_dma_start(
                            out=vt_, out_offset=None, in_=poolV,
                            in_offset=bass.IndirectOffsetOnAxis(ap=idxu[:, pg:pg + 1], axis=0)), reads=["rden"], writes=[vtn], dma=True)
                        mm(po[0:64, :], ptile[:, j * 64:(j + 1) * 64], vt_, pg == 0, False, [ptn, vtn], [pon])
                        mm(pden[0:64, 100:101], ptile[:, j * 64:(j + 1) * 64], ones[:, 0:1], pg == 0, False, [ptn, "ones"], [pdenn])
                pso, pson = PS()
                for hp in range(4):
                    mm(pso[0:8, :64], knT[:, hp, cs], Qbd[:, hp, s_, :], hp == 0, hp == 3, ["vst", "Bq"], [pson])
                dve(lambda e, pso=pso: e.scalar_tensor_tensor(out=tmpf[0:8, :64], in0=pso[0:8, :64], scalar=0.125, in1=cm8, op0=ALU.mult, op1=ALU.add),
                    [pson, "vst"], ["tmpf"])
                act(pT[0][0:8, :64], tmpf[0:8, :64], AF.Exp, ["tmpf"], ["pT0"])
                mm(po[0:64, :], pT[0][0:8, :64], vnew[:, s_, :], False, True, ["pT0", "qTa"], [pon])
                mm(pden[0:64, 100:101], pT[0][0:8, :64], ones[0:8, 0:1], False, True, ["pT0", "ones"], [pdenn])
                act(stg[0:64, :], po[0:64, :], AF.Copy, [pon], ["stg"])
                dve(lambda e, pden=pden: e.reciprocal(out=rd1, in_=pden[0:64, 100:101]), [pdenn], ["vst"])
                dve(lambda e: e.tensor_scalar(out=stg[0:64, :], in0=stg[0:64, :], scalar1=rd1[:, 0:1], scalar2=None, op0=ALU.mult),
                    ["stg", "vst"], ["stg"])
                for h in range(8):
                    hp = h // 2
                    rs = slice((h % 2) * 64, (h % 2) * 64 + 64)
                    pz, pzn = PS()
                    mm(pz[:, 0:8], stg[0:64, hp * 128:(hp + 1) * 128], EtA[:, h, :], True, True, ["stg", "vst"], [pzn])
                    act(omT[rs, hp, cs], pz[rs, 0:8], AF.Copy, [pzn], ["omT"])
            dve(lambda e: e.memset(m8[0:1, 0, 0:1], 0.0), [n for n, _ in kslot + vslot], ["wt0", "wt1", "m8"])
            gla_proj(N)
            for s_ in range(4):
                cs = slice(s_ * 8, (s_ + 1) * 8)
                p, pn = PS()
                for k in range(8):
                    mm(p[:16, :8], wf[:, k, 1792:1808], xn[:, k, cs], k == 0, k == 7, ["xn", "wf"], [pn])
                dve(lambda e, p=p: e.tensor_copy(out=agT[:, :8], in_=p[:16, :8]), [pn], ["agT"])
                p2, pn2 = PS()
                mm(p2[0:8, :256], agT[:, :8], wa2[:], True, False, ["agT", "wa2"], [pn2])
                mm(p2[0:8, :256], ones[0:1, 0:8], ba[:], False, True, ["ones", "ba"], [pn2])
                act(nl[0:8, :], p2[0:8, :256], AF.Exp, [pn2], ["nl"], scale=-1.0)
                act(nl[0:8, :], nl[0:8, :], AF.Ln, ["nl"], ["nl"], bias=1.0)
                p3, pn3 = PS()
                mm(p3[0:8, :256], tri8U, nl[0:8, :], True, True, ["vst", "nl"], [pn3])
                act(eR[0:8, :], p3[0:8, :256], AF.Exp, [pn3], ["eR"])
                pk, pkn = PS()
                for k in range(8):
                    mm(pk[0:8, :256], xn[:, k, cs], wf[:, k, 1024:1280], k == 0, k == 7, ["xn", "wf"], [pkn])
                dve(lambda e, pk=pk: e.tensor_tensor(out=ktil[0:8, :], in0=pk[0:8, :256], in1=eR[0:8, :], op=ALU.mult), [pkn, "eR"], ["ktil"])
                pv, pvn = PS()
                for k in range(8):
                    mm(pv[0:8, :], xn[:, k, cs], wf[:, k, 1280:1792], k == 0, k == 7, ["xn", "wf"], [pvn])
                act(vgb[0:8, :], pv[0:8, :], AF.Copy, [pvn], ["vgb"])
                pd, pdn = PS()
                for half in range(2):
                    mm(pd[:, half:half + 1], nl[0:8, half * 128:(half + 1) * 128], m16c, True, True, ["nl", "vst"], [pdn])
                act(dch[:, :, 0], pd[:, 0:2], AF.Exp, [pdn], ["dch"])
                for hp in range(2):
                    pgm, pgn = PS()
                    mm(pgm[:, :8], nl[0:8, hp * 128:(hp + 1) * 128], tri8I, True, True, ["nl", "vst"], [pgn])
                    act(egp[:, :8], pgm[:, :8], AF.Exp, [pgn], ["egp"])
                    act(egn[:, :8], pgm[:, :8], AF.Exp, [pgn], ["egn"], scale=-1.0)
                    for par in range(2):
                        prs = slice(par * 64, par * 64 + 64)
                        dve(lambda e, hp=hp, cs=cs, prs=prs, par=par: e.scalar_tensor_tensor(
                            out=qtl[prs, hp * 2 + par, cs], in0=qgT[prs, hp, cs], scalar=0.125, in1=egp[prs, :8],
                            op0=ALU.mult, op1=ALU.mult), ["qgT", "egp"], ["qtl"])
                    dve(lambda e, hp=hp, cs=cs: e.tensor_tensor(out=ktl[:, hp, cs], in0=kgT[:, hp, cs], in1=egn[:, :8], op=ALU.mult),
                        ["kgT", "egn"], ["ktl"])
                dma(Sst[:], sgla[s_], [], ["Sst"])
                dve(lambda e: e.tensor_copy(out=Sbf[0][:], in_=Sst[:]), ["Sst"], ["Sbf0"])
                for h in range(4):
                    hp = h // 2
                    pa, pan = PS()
                    mm(pa[0:8, 0:8], ktl[:, hp, cs], qtl[:, h, cs], True, True, ["ktl", "qtl"], [pan])
                    dve(lambda e, pa=pa: e.tensor_tensor(out=ATf[0:8, 0:8], in0=pa[0:8, 0:8], in1=tri8m, op=ALU.mult), [pan, "vst"], ["ATf"])
                    po, pon = PSA(0)
                    mm(po[:, 0:8], vgb[0:8, h * 128:(h + 1) * 128], ATf[0:8, 0:8], True, False, ["vgb", "ATf"], [pon])
                    mm(po[:, 0:8], Sbf[0][:, hp, :], qtl[:, h, cs], False, True, ["Sbf0", "qtl"], [pon])
                    act(ogf[:, h, cs], po[:, 0:8], AF.Copy, [pon], ["ogf"])
                for hp in range(2):
                    pu, pun = PS()
                    mm(pu[:, :256], ktil[0:8, hp * 128:(hp + 1) * 128], vgb[0:8, hp * 256:(hp + 1) * 256], True, True, ["ktil", "vgb"], [pun])
                    for hh in range(2):
                        rs = slice(hh * 64, (hh + 1) * 64)
                        dve(lambda e, pu=pu, hp=hp, hh=hh, rs=rs: e.scalar_tensor_tensor(
                            out=Sst[rs, hp, :], in0=Sst[rs, hp, :], scalar=dch[rs, hp, 0:1], in1=pu[rs, hh * 128:(hh + 1) * 128],
                            op0=ALU.mult, op1=ALU.add), [pun, "dch", "Sst"], ["Sst"])
                dma(gla_s[s_], Sst[:], ["Sst"], [])
            gla_norm(N)
            cross_q(N)
            for s_ in range(4):
                dma(mkT[:], mkTs[s_].rearrange("h d m -> d h m"), [], ["mkT"], q="pool")
                dma(mvb[:], mvs[s_].rearrange("(t p) c -> p t c", p=128), [], ["mvb"], q="pool")
                cross_att(s_ * 8, 8)
            tail_own(99, 0, N, smp=True)

        for gi, (c0, N) in enumerate(GROUPS[:ngroups]):
            ntile = N // 128
            dma(xo[:, :, :N], xT_own[:, c0:c0 + N].rearrange("(k p) t -> p k t", p=128), [], ["xTt0"])
            norm(xo, "xTt0", N, g_mix, xn, "xn")
            oc0 = c0 - HALO

            if "moba" in phases:
                w, wn = loadw(w_in[:, 0:512], 512)
                for c in range(4):
                    p, pn = PS()
                    for k in range(8):
                        mm(p[:, :N], w[:, k, c * 128:(c + 1) * 128], xn[:, k, :N], k == 0, k == 7, ["xn", wn], [pn])
                    act(qT[:, c, :N], p[:, :N], AF.Copy, [pn], ["qT"])
                for tt in range(ntile):
                    p, pn = PS()
                    for k in range(8):
                        mm(p[:], xn[:, k, tt * 128:(tt + 1) * 128], w[:, k, :], k == 0, k == 7, ["xn", wn], [pn])
                    act(Bq[:, tt, :, 0:64], p[:].rearrange("p (h d) -> p h d", h=8), AF.Copy, [pn], ["Bq"])
                w, wn = loadw(w_in[:, 512:1024], 512)
                for h in range(8):
                    p, pn = PS()
                    for k in range(8):
                        mm(p[:64, :N], w[:, k, h * 64:(h + 1) * 64], xn[:, k, :N], k == 0, k == 7, ["xn", wn], [pn])
                    act(kla[0:64, h, :N], p[:64, :N], AF.Copy, [pn], ["kla"])
                    if gi > 0:
                        dve(lambda e, p=p, h=h: e.tensor_copy(out=kstg[0:64, h, :N], in_=p[:64, :N]), [pn], ["hT"])
                if gi > 0:
                    dma(kT_o.rearrange("(h d) t -> d h t", d=64)[:, :, oc0:oc0 + N], kstg[0:64, :, :N], ["hT"], [])
                w, wn = loadw(w_in[:, 1024:1536], 512)
                for tt in range(ntile):
                    p, pn = PS()
                    for k in range(8):
                        mm(p[:], xn[:, k, tt * 128:(tt + 1) * 128], w[:, k, :], k == 0, k == 7, ["xn", wn], [pn])
                    act(vloc[:, tt, :], p[:], AF.Copy, [pn], ["vloc"])
                    if gi > 0:
                        dve(lambda e, p=p: e.tensor_copy(out=stg[:], in_=p[:]), [pn], ["stg"])
                        dma(v_o[oc0 + tt * 128:oc0 + (tt + 1) * 128, :], stg[:], ["stg"], [])
                for tt in range(ntile):
                    qb = (c0 + tt * 128) // 256
                    pg, pgn = PS()
                    for hp in range(4):
                        mm(pg[:, hp * 64:(hp + 1) * 64], qT[:, hp, tt * 128:(tt + 1) * 128], kmbd[:, hp, :], True, True, ["qT", "kmbd"], [pgn])
                    dve(lambda e, pg=pg, qb=qb: e.tensor_tensor(out=gsel[:], in0=pg[:, :256].rearrange("p (h b) -> p h b", h=8),
                                                                in1=bv[:, qb:qb + 1, :].to_broadcast([128, 8, 32]), op=ALU.add), [pgn, "bv"], ["gsel"])
                    for h in range(8):
                        dve(lambda e, h=h: e.max(out=m8[:, h, :], in_=gsel[:, h, :]), ["gsel"], ["m8"])
                    dve(lambda e: e.tensor_scalar_max(out=m8[:, :, 2:3], in0=m8[:, :, 2:3], scalar1=-1e29), ["m8"], ["m8"])
                    dve(lambda e: e.tensor_tensor(out=gsel[:], in0=gsel[:], in1=m8[:, :, 2:3].to_broadcast([128, 8, 32]), op=ALU.is_ge), ["gsel", "m8"], ["gsel"])
                    dve(lambda e, tt=tt: e.tensor_scalar(out=Bq[:, tt, :, 64:96], in0=gsel[:], scalar1=-1.0, scalar2=-NEG, op0=ALU.add, op1=ALU.mult),
                        ["gsel"], ["Bq"])
                    for hq in range(2):
                        pb, pbn = PSB()
                        for h4 in range(4):
                            h = hq * 4 + h4
                            S.op("pe", lambda e, pb=pb, h4=h4, h=h, tt=tt: e.transpose(out=pb[:96, h4 * 128:(h4 + 1) * 128], in_=Bq[:, tt, h, :], identity=ident),
                                 reads=["Bq", "cb"], writes=[pbn])
                        act(qTa[:, hq * 4:(hq + 1) * 4, tt * 128:(tt + 1) * 128], pb[:96, :512].rearrange("p (h q) -> p h q", h=4), AF.Copy, [pbn], ["qTa"])
                npast = min(32, 23 + gi)
                for h in range(8):
                    par = h % 2
                    hp = h // 2
                    ka = kaug0
                    kan = "kaug0"
                    L = npast * 256
                    nkt = npast * 2
                    dma(ka[0:64, :L], Ks[h * 64:(h + 1) * 64, :L], ["Ks"], [kan])
                    if par == 0:
                        dma(vaug0[:, :nkt, :], Vs[:L, hp * 128:(hp + 1) * 128].rearrange("(t p) d -> p t d", p=128), ["Vs"], ["vaug0"])
                    po, pon = PSA(0)
                    pd, pdn = PSA(1)
                    items = []
                    for kt in range(0, nkt, 2):
                        items.append([(ka[:, (kt + j) * 128:(kt + j + 1) * 128], kan, vaug0[:, kt + j, :], "vaug0", 0, 2, False, j * 256) for j in range(2)])
                    items.append([(kla[:, h, 0:128], "kla", vloc[:, 0, hp * 128:(hp + 1) * 128], "vloc", 0, 2, True, 0),
                                  (kla[:, h, 128:256], "kla", vloc[:, 1, hp * 128:(hp + 1) * 128], "vloc", 1, 1, True, 256)])
                    LA = 2
                    psl = {}

                    def issue_qk(i):
                        ps_, psn = PS()
                        for (kap, kn_, vap, vn_, q0, qn_, diag, col0) in items[i]:
                            mm(ps_[:, col0:col0 + qn_ * 128], kap, qTa[:, h, q0 * 128:(q0 + qn_) * 128], True, True, [kn_, "qTa"], [psn])
                        psl[i] = (ps_, psn)

                    for i in range(min(LA, len(items))):
                        issue_qk(i)
                    for si, it in enumerate(items):
                        if si + LA < len(items):
                            issue_qk(si + LA)
                        ps_, psn = psl.pop(si)
                        ptile = pT[si % 2]
                        ptn = "pT%d" % (si % 2)
                        wtot = it[-1][7] + it[-1][5] * 128
                        act(ptile[:, :wtot], ps_[:, :wtot], AF.Exp, [psn], [ptn], scale=0.125)
                        for (kap, kn_, vap, vn_, q0, qn_, diag, col0) in it:
                            if diag:
                                dve(lambda e, ptile=ptile, col0=col0: e.tensor_tensor(out=ptile[:, col0:col0 + 128], in0=ptile[:, col0:col0 + 128], in1=tri01, op=ALU.mult),
                                    [ptn, "cb"], [ptn])
                        for j, (kap, kn_, vap, vn_, q0, qn_, diag, col0) in enumerate(it):
                            qs = slice(q0 * 128, (q0 + qn_) * 128)
                            first = si == 0 and j == 0
                            last = si == len(items) - 1 and j == len(it) - 1
                            mm(po[:, qs], vap, ptile[:, col0:col0 + qn_ * 128], first, last, [vn_, ptn], [pon])
                            if first:
                                dve(lambda e, ptile=ptile, col0=col0, qs=qs, qn_=qn_: e.tensor_copy(out=tmpf[:, qs], in_=ptile[:, col0:col0 + qn_ * 128]),
                                    [ptn], ["tmpf"])
                            else:
                                dve(lambda e, ptile=ptile, col0=col0, qs=qs, qn_=qn_: e.tensor_tensor(out=tmpf[:, qs], in0=tmpf[:, qs], in1=ptile[:, col0:col0 + qn_ * 128], op=ALU.add),
                                    [ptn, "tmpf"], ["tmpf"])
                    mm(pd[:, :N], c32t[:, 388:516], tmpf[:, :N], True, True, ["c32t", "tmpf"], [pdn])
                    rs = slice(par * 64, par * 64 + 64)
                    dve(lambda e, pd=pd, rs=rs: e.reciprocal(out=rden[rs, :N], in_=pd[rs, :N]), [pdn], ["rden"])
                    dve(lambda e, rs=rs, hp=hp, po=po: e.tensor_tensor(out=omT[rs, hp, :N], in0=po[rs, :N], in1=rden[rs, :N], op=ALU.mult),
                        [pon, "rden"], ["omT"])

            if debug and gi == 1:
                dma(dbgo["om"][:, 0:4, :], omT[:], ["omT"], [])
                dma(dbgo["qta"][0:96, :, :], qTa[:], ["qTa"], [])
                dma(dbgo["xn"], xn[:], ["xn"], [])
            if "gla" in phases:
                gla_own(gi, c0, N)
            if "cross" in phases:
                cross_own(gi, c0, N)
            if debug and gi == 1:
                dma(dbgo["og"][:, 0:4, :], ogT[:], ["ogT"], [])
                dma(dbgo["oc"][:, 0:4, :], ocT[:], ["ocT"], [])
            if "tail" in phases:
                tail_own(gi, c0, N)
        if "smp" in phases:
            sample_phase()
        S.emit(nc)
    return nc


_NC_CACHE = {}


def _consts():
    p = np.arange(128)
    ident = np.eye(128, dtype=np.float32)
    tri01 = (p[:, None] <= p[None, :]).astype(np.float32)
    blk2 = ((p[:, None] // 64) == (p[None, :] // 64)).astype(np.float32)
    triB = tri01 * blk2
    cb = np.concatenate([ident, tri01, blk2, triB, ident, ident], axis=1).astype(np.float32)
    triU = ((p[:, None] > p[None, :]) & ((p[:, None] // 64) == (p[None, :] // 64))).astype(np.float32) * (-1.0 / 16)
    triI = triB * (-1.0 / 16)
    chk = np.zeros((128, 2), np.float32)
    chk[:64, 0] = -1.0 / 16
    chk[64:, 1] = -1.0 / 16
    sel = np.zeros((2, 128, 128), np.float32)
    sel[0, 64, 0:64] = 1.0
    sel[1, :, :] = 1.0
    c32 = np.concatenate([triU, triI, chk, np.zeros((128, 2), np.float32), sel[0], sel[1]], axis=1)
    oh = np.zeros((32, SEQ), np.float32)
    for j in range(32):
        oh[j, j * 256:(j + 1) * 256] = 1.0
    return cb, c32, oh


def _csm():
    c = np.zeros((128, 192), np.float32)
    c[:, 0] = np.arange(128)
    i = np.arange(8)[:, None]
    t = np.tile(np.arange(8), 8)[None, :]
    c[0:8, 1:65] = np.where(i <= t, 0.0, NEG)
    c[0:64, 65:129] = np.eye(64)
    tri = (np.arange(8)[:, None] <= np.arange(8)[None, :]).astype(np.float32)
    c[0:8, 129:137] = tri * (-1.0 / 16)
    c[0:8, 137:145] = (np.arange(8)[:, None] > np.arange(8)[None, :]).astype(np.float32) * (-1.0 / 16)
    c[0:8, 145:153] = tri
    c[0:8, 153] = -1.0 / 16
    return c


def pool_layouts(cache_k, cache_v):
    n = cache_k.shape[1]
    kt = np.ascontiguousarray(np.asarray(cache_k[0], np.float32).reshape(n, 128, 4, 2, 64).transpose(0, 3, 4, 2, 1)).reshape(n * 128, 512)
    v = np.ascontiguousarray(np.asarray(cache_v[0], np.float32)).reshape(n * 128, 512)
    return kt, v


def make_in_maps(inp, pools=None):
    f = lambda a: np.ascontiguousarray(np.asarray(a, dtype=np.float32))
    if pools is None:
        pools = pool_layouts(inp["cache_moba_k"], inp["cache_moba_v"])
    poolKT, poolV = pools
    csm = _csm()
    x_sample = f(inp["x_sample"])
    page_table = np.asarray(inp["page_table"]).astype(np.int32)
    state_gla = f(inp["state_gla"])
    state_conv = f(inp["state_conv"])
    cmk = f(inp["cache_mem_k"])
    cmv = f(inp["cache_mem_v"])
    x_prompt = f(inp["x_prompt"])
    cb, c32, oh = _consts()
    fm = lambda v: f(v).reshape(-1, 128).T
    w_conv = inp["w_conv"]
    vecs = np.concatenate([fm(inp["norm_mix"][0]), fm(inp["norm_ffn"][0]), fm(inp["norm_final"]), fm(inp["norm_mem"][0]),
                           fm(inp["b_gate"][0]), fm(inp["norm_gla"][0]), fm(w_conv[0, 0]), fm(w_conv[0, 1]), fm(w_conv[0, 2]),
                           fm(inp["b_conv"][0])], axis=1)
    vecs = np.ascontiguousarray(vecs)
    shared = dict(w_in=f(inp["w_in"][0]), w_a2=f(inp["w_gla_a2"][0]), b_a=f(inp["b_gla_a"][0]).reshape(1, 256),
                  w_mem=f(inp["w_mem_kv"][0]), w_brm=f(inp["w_br_moba"][0]), w_brg=f(inp["w_br_gla"][0]),
                  w_brc=f(inp["w_br_cross"][0]), w_gate=f(inp["w_gate"][0]), w_out=f(inp["w_out"][0]),
                  w_up=f(inp["w_up"][0]), w_down=f(inp["w_down"][0]), vecs=vecs, consts=cb, c32=c32, ohrows=oh)
    in_maps = []
    for c in range(8):
        b, r = c // 4, c % 4
        xT = np.ascontiguousarray(x_prompt[b].T)
        own = np.zeros((1024, NCOL), np.float32)
        lo = r * NOWN - HALO
        if lo < 0:
            own[:, HALO:] = xT[:, 0:NOWN]
        else:
            own[:] = xT[:, lo:lo + NCOL]
        bvld = np.zeros((128, 9, 32), np.float32)
        for i in range(9):
            cur = 8 * r - 1 + i
            bvld[:, i, :] = np.where(np.arange(32) < cur, 0.0, -1e30)[None, :]
        selr = np.zeros((128, 4), np.float32)
        selr[:, r] = 1.0
        m = dict(shared)
        m.update(xT_full=xT, xT_own=own, memT=np.ascontiguousarray(f(inp["mem_prompt"][b]).T), blkvalid=bvld, selr=selr)
        sq = slice(4 * c, 4 * c + 4)
        m.update(xT_smp=np.ascontiguousarray(x_sample[sq].reshape(32, 1024).T), ptab=np.ascontiguousarray(page_table[sq]),
                 poolKT=poolKT, poolV=poolV,
                 sgla=np.ascontiguousarray(state_gla[0, sq].reshape(4, 2, 2, 64, 128).transpose(0, 2, 3, 1, 4).reshape(4, 128, 2, 128)),
                 sconv=np.ascontiguousarray(state_conv[0, sq].reshape(4, 2, NF, 128).transpose(3, 2, 0, 1)),
                 mkTs=np.ascontiguousarray(cmk[0, sq].transpose(0, 2, 3, 1)), mvs=np.ascontiguousarray(cmv[0, sq].reshape(4, 256, 512)),
                 csm=csm)
        in_maps.append(m)
    return in_maps


def kernel(**inp):
    n_phys = int(np.asarray(inp["cache_moba_k"]).shape[1])
    if n_phys not in _NC_CACHE:
        _NC_CACHE[n_phys] = build_nc(n_phys=n_phys)
    nc = _NC_CACHE[n_phys]
    in_maps = make_in_maps(inp)
    res = run_bass_kernel_spmd(nc, in_maps, core_ids=list(range(8))).results
    B = 2
    y_prompt = np.zeros((B, SEQ, 1024), np.float32)
    nk = np.zeros((1, B, SEQ, 8, 64), np.float32)
    nv = np.zeros((1, B, SEQ, 8, 64), np.float32)
    gp = np.zeros((1, B, 4, 64, 128), np.float32)
    cp = np.zeros((1, B, 2, DFF), np.float32)
    mkp = np.zeros((1, B, 256, 4, 128), np.float32)
    mvp = np.zeros((1, B, 256, 4, 128), np.float32)
    for c in range(8):
        b, r = c // 4, c % 4
        o = res[c]
        sl = slice(r * NOWN, (r + 1) * NOWN)
        y_prompt[b, sl] = o["yT"].T
        nk[0, b, sl] = o["kT_o"].T.reshape(NOWN, 8, 64)
        nv[0, b, sl] = o["v_o"].reshape(NOWN, 8, 64)
        if r == 3:
            g = o["gla_o"]
            gp[0, b] = g.reshape(2, 64, 2, 128).transpose(2, 0, 1, 3).reshape(4, 64, 128)
            cp[0, b] = o["conv_o"].transpose(2, 1, 0).reshape(2, DFF)
        if r == 0:
            mkp[0, b] = o["mk_o"].reshape(256, 4, 128)
            mvp[0, b] = o["mv_o"].reshape(256, 4, 128)
    DB = 32
    y_sample = np.zeros((DB, 8, 1024), np.float32)
    nks = np.zeros((1, DB, 8, 8, 64), np.float32)
    nvs = np.zeros((1, DB, 8, 8, 64), np.float32)
    gs = np.zeros((1, DB, 4, 64, 128), np.float32)
    cs = np.zeros((1, DB, 2, DFF), np.float32)
    for c in range(8):
        o = res[c]
        sq = slice(4 * c, 4 * c + 4)
        y_sample[sq] = o["yT_s"].T.reshape(4, 8, 1024)
        nks[0, sq] = o["kT_s"].T.reshape(4, 8, 8, 64)
        nvs[0, sq] = o["v_s"].reshape(4, 8, 8, 64)
        gs[0, sq] = o["gla_s"].reshape(4, 2, 64, 2, 128).transpose(0, 3, 1, 2, 4).reshape(4, 4, 64, 128)
        cs[0, sq] = o["conv_s"].transpose(2, 3, 1, 0).reshape(4, 2, DFF)
    return (y_prompt, y_sample, nk, nv, nks, nvs, gp, gs, cp, cs, mkp, mvp)
```

```python
import contextlib
import numpy as np
import concourse.bass as bass
import concourse.mybir as mybir
from concourse.bass_utils import run_bass_kernel_spmd

F32 = mybir.dt.float32
BF16 = mybir.dt.bfloat16
I32 = mybir.dt.int32
AF = mybir.ActivationFunctionType
ALU = mybir.AluOpType
AX = mybir.AxisListType
ENG = ("pe", "act", "dve", "pool", "sp")
NEG = -30000.0


class _Buf:
    __slots__ = ("lw", "rd")

    def __init__(self):
        self.lw = None
        self.rd = []


class _Op:
    __slots__ = ("eng", "fn", "deps", "dma", "sig", "sem", "val", "idx", "guard")

    def __init__(self, eng, fn, dma):
        self.eng, self.fn, self.dma = eng, fn, dma
        self.deps = set()
        self.sig = False
        self.sem = None
        self.val = 0
        self.guard = None


class Sched:
    NDMA = 6

    def __init__(self):
        self.ops = []
        self.bufs = {}
        self.bar = None

    def barrier(self, fn):
        o = _Op("dve", fn, False)
        o.idx = len(self.ops)
        for b in self.bufs.values():
            if b.lw is not None:
                o.deps.add(b.lw)
            o.deps.update(b.rd)
        self.ops.append(o)
        self.bar = o.idx
        self.bufs = {}

    limit = 10 ** 9

    def _last_per_engine(self, idxs):
        last = {}
        out = []
        for i in idxs:
            o = self.ops[i]
            if o.dma:
                out.append(i)
            elif last.get(o.eng, -1) < i:
                last[o.eng] = i
        out.extend(last.values())
        return out

    def op(self, eng, fn, reads=(), writes=(), dma=False):
        if len(self.ops) >= self.limit:
            return None
        o = _Op(eng, fn, dma)
        o.idx = len(self.ops)
        if self.bar is not None:
            o.deps.add(self.bar)
        bufs = self.bufs
        for r in reads:
            b = bufs.get(r)
            if b is None:
                b = bufs[r] = _Buf()
            if b.lw is not None:
                o.deps.add(b.lw)
            if r.startswith("ps"):
                o.deps.update(self._last_per_engine(b.rd))
        for w in writes:
            b = bufs.get(w)
            if b is None:
                b = bufs[w] = _Buf()
            if b.lw is not None:
                o.deps.add(b.lw)
            o.deps.update(self._last_per_engine(b.rd))
        for r in reads:
            bufs[r].rd.append(o.idx)
        for w in writes:
            b = bufs[w]
            b.lw = o.idx
            b.rd = []
        o.deps.discard(o.idx)
        self.ops.append(o)
        return o

    def emit(self, nc):
        ops = self.ops
        for o in ops:
            for d in o.deps:
                p = ops[d]
                if p.eng == "pe" and o.eng == "pe" and not p.dma and not o.dma:
                    continue
                p.sig = True
        alld = [o for o in ops if o.dma]
        for o in alld:
            o.sig = True
        with contextlib.ExitStack() as st:
            csem = {e: st.enter_context(nc.semaphore("c_" + e)) for e in ENG}
            dsem = {e: [st.enter_context(nc.semaphore("d_%s%d" % (e, i))) for i in range(self.NDMA)]
                    for e in ("sp", "pool")}
            ccount = {e: 0 for e in ENG}
            dcount = {e: 0 for e in ENG}
            lastd = {}
            for o in ops:
                if not o.sig:
                    continue
                if o.dma:
                    i = dcount[o.eng]
                    dcount[o.eng] += 1
                    o.sem = dsem[o.eng][i % self.NDMA]
                    o.val = 16 * (i // self.NDMA + 1)
                    o.guard = lastd.get(id(o.sem))
                    lastd[id(o.sem)] = o
                else:
                    ccount[o.eng] += 1
                    o.sem = csem[o.eng]
                    o.val = ccount[o.eng]
            block = st.enter_context(nc.Block())
            per = {e: [o for o in ops if o.eng == e] for e in ENG}

            def run(e, engobj):
                waited = {}
                for o in per[e]:
                    need = {}
                    for d in o.deps:
                        p = ops[d]
                        if not p.sig:
                            continue
                        k = id(p.sem)
                        if need.get(k, (None, 0))[1] < p.val:
                            need[k] = (p.sem, p.val)
                    if o.guard is not None:
                        g = o.guard
                        k = id(g.sem)
                        if need.get(k, (None, 0))[1] < g.val:
                            need[k] = (g.sem, g.val)
                    for k, (s, v) in need.items():
                        if waited.get(k, 0) < v:
                            engobj.wait_ge(s, v)
                            waited[k] = v
                    ins = o.fn(engobj)
                    if o.sig:
                        ins.then_inc(o.sem, 16 if o.dma else 1)
                if e == "sp":
                    fin = {}
                    for o in alld:
                        k = id(o.sem)
                        if fin.get(k, (None, 0))[1] < o.val:
                            fin[k] = (o.sem, o.val)
                    for k, (s, v) in fin.items():
                        if waited.get(k, 0) < v:
                            engobj.wait_ge(s, v)
                    for ee in ENG:
                        if ccount[ee] > 0:
                            engobj.wait_ge(csem[ee], ccount[ee])

            @block.sync
            def _(eng):
                run("sp", eng)

            @block.scalar
            def _(eng):
                run("act", eng)

            @block.vector
            def _(eng):
                run("dve", eng)

            @block.gpsimd
            def _(eng):
                run("pool", eng)

            @block.tensor
            def _(eng):
                run("pe", eng)


SEQ = 8192
NOWN = 2048
HALO = 256
GROUPS = [(i * 256, 256) for i in range(9)]
NCOL = HALO + NOWN
DFF = 2816
NF = 22
INC = 3600


def build_nc(phases=("full", "moba", "gla", "cross", "tail", "smp"), ngroups=9, nfull=32, debug=False, n_phys=5120):
    nc = bass.Bass("TRN2", target_bir_lowering=False)

    def din(name, shape, dt=F32):
        return nc.dram_tensor(name, list(shape), dt, kind="ExternalInput").ap()

    def dout(name, shape, dt=F32):
        return nc.dram_tensor(name, list(shape), dt, kind="ExternalOutput").ap()

    xT_full = din("xT_full", [1024, SEQ])
    xT_own = din("xT_own", [1024, NCOL])
    memT = din("memT", [1024, 256])
    w_in = din("w_in", [1024, INC])
    w_a2 = din("w_a2", [16, 256])
    b_a = din("b_a", [1, 256])
    w_mem = din("w_mem", [1024, 1024])
    w_brm = din("w_brm", [512, 1024])
    w_brg = din("w_brg", [512, 1024])
    w_brc = din("w_brc", [512, 1024])
    w_gate = din("w_gate", [1024, 3072])
    w_out = din("w_out", [1024, 1024])
    w_up = din("w_up", [1024, 2 * DFF])
    w_down = din("w_down", [DFF, 1024])
    vecs = din("vecs", [128, 8 * 4 + 24 + 4 + NF * 4])
    blkvalid = din("blkvalid", [128, 9, 32])
    selr = din("selr", [128, 4])
    consts = din("consts", [128, 128 * 6])
    c32 = din("c32", [128, 516])
    ohrows = din("ohrows", [32, SEQ])
    xT_smp = din("xT_smp", [1024, 32])
    ptab = din("ptab", [4, 128], I32)
    poolKT = din("poolKT", [n_phys * 128, 512])
    poolV = din("poolV", [n_phys * 128, 512])
    sgla = din("sgla", [4, 128, 2, 128])
    sconv = din("sconv", [128, NF, 4, 2])
    mkTs = din("mkTs", [4, 4, 128, 256])
    mvs = din("mvs", [4, 256, 512])
    csm_d = din("csm", [128, 192])
    yT_s = dout("yT_s", [1024, 32])
    kT_s = dout("kT_s", [512, 32])
    v_s = dout("v_s", [32, 512])
    gla_s = dout("gla_s", [4, 128, 2, 128])
    conv_s = dout("conv_s", [128, NF, 4, 2])
    Ks = nc.dram_tensor("Ks", [512, SEQ], BF16, kind="Internal").ap()
    wsrc = dict(w_mem=(w_mem, [1024, 1024]), w_in=(w_in, [1024, INC]), w_brm=(w_brm, [512, 1024]), w_brg=(w_brg, [512, 1024]),
                w_brc=(w_brc, [512, 1024]), w_gate=(w_gate, [1024, 3072]), w_out=(w_out, [1024, 1024]), w_up=(w_up, [1024, 2 * DFF]),
                w_down=(w_down, [DFF, 1024]))
    wbf = {k: nc.dram_tensor(k + "_bf", shp, BF16, kind="Internal").ap() for k, (_, shp) in wsrc.items()}
    Vs = nc.dram_tensor("Vs", [SEQ, 512], BF16, kind="Internal").ap()

    yT = dout("yT", [1024, NOWN])
    kT_o = dout("kT_o", [512, NOWN])
    v_o = dout("v_o", [NOWN, 512])
    gla_o = dout("gla_o", [128, 2, 128])
    conv_o = dout("conv_o", [128, NF, 2])
    mk_o = dout("mk_o", [256, 512])
    mv_o = dout("mv_o", [256, 512])

    dbgo = {}
    if debug:
        for nm in ("om", "og", "oc", "mrg"):
            dbgo[nm] = nc.dram_tensor("d_" + nm, [128, 8, 256], BF16, kind="ExternalOutput").ap()
        for nm in ("h1", "h2", "qta", "xn"):
            dbgo[nm] = nc.dram_tensor("d_" + nm, [128, 8, 256], F32 if nm.startswith("h") else BF16, kind="ExternalOutput").ap()
    S = Sched()
    st = contextlib.ExitStack()
    with st:
        def sb(name, shape, dt=F32):
            return st.enter_context(nc.sbuf_tensor(name, list(shape), dt))

        psf = [st.enter_context(nc.psum_tensor("psf%d" % i, [128, 512], F32)) for i in range(6)]
        psb = [st.enter_context(nc.psum_tensor("psb%d" % i, [128, 1024], BF16)) for i in range(2)]
        pctr = [0, 0]

        def PS():
            i = pctr[0] % 4
            pctr[0] += 1
            return psf[i], "psf%d" % i

        def PSA(i):
            return psf[4 + i], "psf%d" % (4 + i)

        def PSB():
            i = pctr[1] % 2
            pctr[1] += 1
            return psb[i], "psb%d" % i

        def mm(out, lhsT, rhs, start, stop, R, W):
            S.op("pe", lambda e: e.matmul(out, lhsT=lhsT, rhs=rhs, start=start, stop=stop), reads=R, writes=W)

        def act(out, in_, func, R, W, **kw):
            S.op("act", lambda e: e.activation(out=out, in_=in_, func=func, **kw), reads=R, writes=W)

        def dve(fn, R, W):
            S.op("dve", fn, reads=R, writes=W)

        def pool(fn, R, W):
            S.op("pool", fn, reads=R, writes=W)

        def dma(out, in_, R, W, q="sp", **kw):
            S.op(q, lambda e: e.dma_start(out=out, in_=in_, **kw), reads=R, writes=W, dma=True)

        for k_, (src_, shp_) in wsrc.items():
            rows = shp_[0]
            for r0 in range(0, rows, 256):
                r1 = min(rows, r0 + 256)
                dma(wbf[k_][r0:r1, :], src_[r0:r1, :], [], ["wscr_" + k_ + "_bf"], q="pool")
        w_in, w_brm, w_brg, w_brc, w_gate, w_out, w_up, w_down, w_mem = (wbf[k_] for k_ in (
            "w_in", "w_brm", "w_brg", "w_brc", "w_gate", "w_out", "w_up", "w_down", "w_mem"))
        cb = sb("cb", [128, 6, 128], BF16)
        c32t = sb("c32t", [128, 516], F32)
        vt = sb("vt", [128, 8 * 4 + 24 + 4 + NF * 4], F32)
        bv = sb("bv", [128, 9, 32], F32)
        selt = sb("selt", [128, 4], F32)
        ones = sb("ones", [128, 128], BF16)
        dma(cb[:], consts.rearrange("p (a b) -> p a b", a=6), [], ["cb"], q="pool")
        dma(c32t[:], c32, [], ["c32t"])
        dma(vt[:], vecs, [], ["vt"])
        dma(bv[:], blkvalid, [], ["bv"])
        dma(selt[:], selr, [], ["selt"])
        dve(lambda e: e.memset(ones[:], 1.0), [], ["ones"])
        keep = sb("keep", [128, 1], F32)
        dve(lambda e: e.tensor_scalar(out=keep[:], in0=selt[:, 0:1], scalar1=-1.0, scalar2=1.0, op0=ALU.mult, op1=ALU.add), ["selt"], ["keep"])
        ident = cb[:, 0, :]
        tri01 = cb[:, 1, :]
        blk2 = cb[:, 2, :]
        triU32 = c32t[:, 0:128]
        triI32 = c32t[:, 128:256]
        chk32 = c32t[:, 256:258]
        g_mix = lambda k: vt[:, k:k + 1]
        g_ffn = lambda k: vt[:, 8 + k:9 + k]
        g_fin = lambda k: vt[:, 16 + k:17 + k]
        g_mem = lambda k: vt[:, 24 + k:25 + k]
        b_gate = lambda j: vt[:, 32 + j:33 + j]
        g_gla = lambda h: vt[:, 56 + h:57 + h]
        wc = lambda i, f: vt[:, 60 + i * NF + f:61 + i * NF + f]

        xTt0_ = sb("xTt0", [128, 8, 256], F32)
        xTt = [xTt0_, xTt0_]
        sqt = sb("sqt", [128, 8, 256], BF16)
        xn = sb("xn", [128, 8, 256], BF16)
        rstd = sb("rstd", [128, 256], F32)
        wt = [sb("wt%d" % i, [128, 8, 512], BF16) for i in range(2)]
        wctr = [0]

        def norm(src, srcname, N, gfn, dst, dstname):
            act(sqt[:, :, :N], src[:, :, :N], AF.Square, [srcname], ["sqt"])
            p, pn = PS()
            for k in range(8):
                mm(p[:, :N], ones[:], sqt[:, k, :N], k == 0, k == 7, ["ones", "sqt"], [pn])
            act(rstd[:, :N], p[:, :N], AF.Sqrt, [pn], ["rstd"], scale=1.0 / 1024, bias=1e-6)
            dve(lambda e: e.reciprocal(out=rstd[:, :N], in_=rstd[:, :N]), ["rstd"], ["rstd"])
            for k in range(8):
                dve(lambda e, k=k: e.scalar_tensor_tensor(out=dst[:, k, :N], in0=src[:, k, :N], scalar=gfn(k), in1=rstd[:, :N],
                                                          op0=ALU.mult, op1=ALU.mult),
                    [srcname, "vt", "rstd"], [dstname])

        def loadw(src_ap, ncols, krows=8):
            i = wctr[0] % 2
            wctr[0] += 1
            t = wt[i]
            dma(t[:, :krows, :ncols], src_ap.rearrange("(k p) c -> p k c", p=128), ["wscr_" + src_ap.tensor.name], ["wt%d" % i])
            return t, "wt%d" % i

        mkT = sb("mkT", [128, 4, 256], BF16)
        mvb = sb("mvb", [128, 2, 512], BF16)
        stg = sb("stg", [128, 512], F32)
        dma(xTt[0][:, :, :256], memT.rearrange("(k p) t -> p k t", p=128), [], ["xTt0"])
        norm(xTt[0], "xTt0", 256, g_mem, xn, "xn")
        for half, (dst_o,) in enumerate([(mk_o,), (mv_o,)]):
            w, wn = loadw(w_mem[:, half * 512:(half + 1) * 512], 512)
            for tt in range(2):
                p, pn = PS()
                for k in range(8):
                    mm(p[:], xn[:, k, tt * 128:(tt + 1) * 128], w[:, k, :], k == 0, k == 7, ["xn", wn], [pn])
                act(stg[:], p[:], AF.Copy, [pn], ["stg"])
                if half == 1:
                    dve(lambda e, p=p, tt=tt: e.tensor_copy(out=mvb[:, tt, :], in_=p[:]), [pn], ["mvb", pn])
                dma(dst_o[tt * 128:(tt + 1) * 128, :], stg[:], ["stg"], [])
            if half == 0:
                for h in range(4):
                    p, pn = PS()
                    for k in range(8):
                        mm(p[:, :256], w[:, k, h * 128:(h + 1) * 128], xn[:, k, :256], k == 0, k == 7, ["xn", wn], [pn])
                    dve(lambda e, p=p, h=h: e.tensor_copy(out=mkT[:, h, :], in_=p[:, :256]), [pn], ["mkT"])

        wf = sb("wf", [128, 8, 1808], BF16)
        dma(wf[:, :, 0:1024], w_in[:, 512:1536].rearrange("(k p) c -> p k c", p=128), ["wscr_w_in_bf"], ["wf"])
        dma(wf[:, :, 1024:1280], w_in[:, 1792:2048].rearrange("(k p) c -> p k c", p=128), ["wscr_w_in_bf"], ["wf"])
        dma(wf[:, :, 1280:1808], w_in[:, 2048:2576].rearrange("(k p) c -> p k c", p=128), ["wscr_w_in_bf"], ["wf"])
        dma(wf[:, :, 1792:1808], w_in[:, 3072:3088].rearrange("(k p) c -> p k c", p=128), ["wf", "wscr_w_in_bf"], ["wf"])
        wa2 = sb("wa2", [16, 256], BF16)
        ba = sb("ba", [1, 256], BF16)
        dma(wa2[:], w_a2, [], ["wa2"], q="pool")
        dma(ba[:], b_a, [], ["ba"], q="pool")
        kmT = sb("kmT", [128, 4, 32], F32)
        kst = sb("kst", [128, 4, 256], BF16)
        vst = sb("vst", [128, 2, 512], BF16)
        Sst = sb("Sst", [128, 2, 128], F32)
        Ssave = sb("Ssave", [128, 3, 2, 128], F32)
        agT = sb("agT", [16, 512], BF16)
        nl = sb("nl", [128, 256], F32)
        eR = sb("eR", [128, 256], F32)
        ktil = sb("ktil", [128, 256], BF16)
        vgb = sb("vgb", [128, 512], BF16)
        dch = sb("dch", [128, 2, 2], F32)
        dve(lambda e: e.memset(Sst[:], 0.0), [], ["Sst"])

        def gla_prep(xnt, xnname, col0, wk_ap, wv_ap, wa_ap, wname):
            p, pn = PS()
            for k in range(8):
                mm(p[:16, :128], wa_ap(k), xnt[:, k, col0:col0 + 128], k == 0, k == 7, [xnname, wname], [pn])
            dve(lambda e, p=p: e.tensor_copy(out=agT[:, :128], in_=p[:16, :128]), [pn], ["agT"])
            p2, pn2 = PS()
            mm(p2[:, :256], agT[:, :128], wa2[:], True, False, ["agT", "wa2"], [pn2])
            mm(p2[:, :256], ones[0:1, :], ba[:], False, True, ["ones", "ba"], [pn2])
            act(nl[:], p2[:, :256], AF.Exp, [pn2], ["nl"], scale=-1.0)
            act(nl[:], nl[:], AF.Ln, ["nl"], ["nl"], bias=1.0)
            p3, pn3 = PS()
            mm(p3[:, :256], triU32, nl[:], True, True, ["c32t", "nl"], [pn3])
            act(eR[:], p3[:, :256], AF.Exp, [pn3], ["eR"])
            pk, pkn = PS()
            for k in range(8):
                mm(pk[:, :256], xnt[:, k, col0:col0 + 128], wk_ap(k), k == 0, k == 7, [xnname, wname], [pkn])
            dve(lambda e, pk=pk: e.tensor_tensor(out=ktil[:], in0=pk[:, :256], in1=eR[:], op=ALU.mult), [pkn, "eR"], ["ktil"])
            pv, pvn = PS()
            for k in range(8):
                mm(pv[:], xnt[:, k, col0:col0 + 128], wv_ap(k), k == 0, k == 7, [xnname, wname], [pvn])
            act(vgb[:], pv[:], AF.Copy, [pvn], ["vgb"])
            pd, pdn = PS()
            for half in range(2):
                mm(pd[:, half * 2:half * 2 + 2], nl[:, half * 128:(half + 1) * 128], chk32, True, True, ["nl", "c32t"], [pdn])
            act(dch[:], pd[:, 0:4].rearrange("p (a b) -> p a b", a=2), AF.Exp, [pdn], ["dch"])

        def gla_state_update(ci):
            for hp in range(2):
                pu, pun = PS()
                mm(pu[:, :256], ktil[ci * 64:(ci + 1) * 64, hp * 128:(hp + 1) * 128], vgb[ci * 64:(ci + 1) * 64, hp * 256:(hp + 1) * 256],
                   True, True, ["ktil", "vgb"], [pun])
                for hh in range(2):
                    rs = slice(hh * 64, (hh + 1) * 64)
                    dve(lambda e, pu=pu, hp=hp, hh=hh, rs=rs: e.scalar_tensor_tensor(
                        out=Sst[rs, hp, :], in0=Sst[rs, hp, :], scalar=dch[rs, hp, ci:ci + 1], in1=pu[rs, hh * 128:(hh + 1) * 128],
                        op0=ALU.mult, op1=ALU.add), [pun, "dch", "Sst"], ["Sst"])

        for g in range(nfull if "full" in phases else 0):
            xt = xTt[0]
            xname = "xTt0"
            dma(xt[:], xT_full[:, g * 256:(g + 1) * 256].rearrange("(k p) t -> p k t", p=128), [], [xname])
            norm(xt, xname, 256, g_mix, xn, "xn")
            for c in range(4):
                p, pn = PS()
                for k in range(8):
                    mm(p[:, :256], wf[:, k, c * 128:(c + 1) * 128], xn[:, k, :], k == 0, k == 7, ["xn", "wf"], [pn])
                act(kst[:, c, :], p[:, :256], AF.Copy, [pn], ["kst"])
                dve(lambda e, p=p, c=c, g=g: e.reduce_sum(out=kmT[:, c, g:g + 1], in_=p[:, :256], axis=AX.X), [pn], ["kmT"])
            dma(Ks.rearrange("(c p) t -> p c t", p=128)[:, :, g * 256:(g + 1) * 256], kst[:], ["kst"], ["Ks"])
            for tt in range(2):
                p, pn = PS()
                for k in range(8):
                    mm(p[:], xn[:, k, tt * 128:(tt + 1) * 128], wf[:, k, 512:1024], k == 0, k == 7, ["xn", "wf"], [pn])
                act(vst[:, tt, :], p[:], AF.Copy, [pn], ["vst"])
            dma(Vs[g * 256:(g + 1) * 256, :].rearrange("(t p) c -> p t c", p=128), vst[:], ["vst"], ["Vs"])
            for tt in range(2):
                gla_prep(xn, "xn", tt * 128, lambda k: wf[:, k, 1024:1280], lambda k: wf[:, k, 1280:1792],
                         lambda k: wf[:, k, 1792:1808], "wf")
                for ci in range(2):
                    chunk = g * 4 + tt * 2 + ci
                    for i, cb_ in enumerate((28, 60, 92)):
                        if chunk == cb_:
                            dve(lambda e, i=i: e.tensor_copy(out=Ssave[:, i, :, :], in_=Sst[:]), ["Sst"], ["Ssave"])
                    gla_state_update(ci)
        dma(gla_o, Sst[:], ["Sst"], [])
        kmbd = sb("kmbd", [128, 4, 64], BF16)
        dve(lambda e: e.memset(kmbd[:], 0.0), [], ["kmbd"])
        dve(lambda e: e.tensor_copy(out=kmbd[0:64, :, 0:32], in_=kmT[0:64, :, :]), ["kmT", "kmbd"], ["kmbd"])
        dve(lambda e: e.tensor_copy(out=kmbd[64:128, :, 32:64], in_=kmT[64:128, :, :]), ["kmT", "kmbd"], ["kmbd"])

        Srun = sb("Srun", [128, 2, 128], F32)
        dve(lambda e: e.tensor_scalar(out=Srun[:], in0=Ssave[:, 0, :, :], scalar1=selt[:, 1:2], scalar2=None, op0=ALU.mult), ["Ssave", "selt"], ["Srun"])
        for i in (1, 2):
            dve(lambda e, i=i: e.scalar_tensor_tensor(out=Srun[:], in0=Ssave[:, i, :, :], scalar=selt[:, i + 1:i + 2], in1=Srun[:],
                                                      op0=ALU.mult, op1=ALU.add), ["Ssave", "selt", "Srun"], ["Srun"])

        xo = xTt[0]
        hT = sb("hT", [128, 8, 256], F32)
        qT = sb("qT", [128, 4, 256], BF16)
        Bq = sb("Bq", [128, 2, 8, 96], BF16)
        qTa = sb("qTa", [96, 8, 256], BF16)
        kla = sb("kla", [96, 8, 256], BF16)
        vloc = sb("vloc", [128, 2, 512], BF16)
        gsel = sb("gsel", [128, 8, 32], F32)
        m8 = sb("m8", [128, 8, 8], F32)
        kaug0 = sb("kaug0", [96, SEQ], BF16)
        kaug = [kaug0, kaug0]
        vaug0 = sb("vaug0", [128, 64, 128], BF16)
        pT = [sb("pT%d" % i, [128, 512], BF16) for i in range(2)]
        rden = sb("rden", [128, 256], F32)
        omT = sb("omT", [128, 4, 256], BF16)
        ogT = sb("ogT", [128, 4, 256], BF16)
        ocT = sb("ocT", [128, 4, 256], BF16)
        mrg = sb("mrg", [128, 8, 256], BF16)
        hn = mrg
        sg = sb("sg", [128, 3, 256], F32)
        tmpf = sb("tmpf", [128, 256], F32)
        uext2 = sb("uext2", [128, 258], F32)
        tmpf2 = uext2
        aT = sb("aT", [128, NF, 256], BF16)
        uext = sb("uext", [128, 258], F32)
        uprev = sb("uprev", [128, NF, 2], F32)
        yst = xTt[0]
        kstg = hT
        qgT = sb("qgT", [128, 2, 256], F32)
        kgT = sb("kgT", [128, 2, 256], F32)
        qtl = sb("qtl", [128, 4, 256], BF16)
        ktl = sb("ktl", [128, 2, 256], BF16)
        egp = sb("egp", [128, 128], F32)
        egn = sb("egn", [128, 128], F32)
        rsil = sb("rsil", [128, 4, 256], F32)
        ogf = sb("ogf", [128, 4, 256], F32)
        qcT = sb("qcT", [128, 4, 256], BF16)

        dma(kaug0[64:96, :], ohrows, [], ["kaug0"], q="pool")
        dve(lambda e: e.memset(kla[:], 0.0), [], ["kla"])
        dve(lambda e: e.memset(uprev[:], 0.0), [], ["uprev"])
        dve(lambda e: e.memset(Bq[:], 0.0), [], ["Bq"])
        dve(lambda e: e.memset(qtl[:], 0.0), [], ["qtl"])
        selmat = [c32t[:, 260 + i * 128:260 + (i + 1) * 128] for i in range(2)]
        triB = cb[:, 3, :]

        def st_update(ci, St, Sn):
            for hp in range(2):
                pu, pun = PS()
                mm(pu[:, :256], ktil[ci * 64:(ci + 1) * 64, hp * 128:(hp + 1) * 128], vgb[ci * 64:(ci + 1) * 64, hp * 256:(hp + 1) * 256],
                   True, True, ["ktil", "vgb"], [pun])
                for hh in range(2):
                    rs = slice(hh * 64, (hh + 1) * 64)
                    dve(lambda e, pu=pu, hp=hp, hh=hh, rs=rs: e.scalar_tensor_tensor(
                        out=St[rs, hp, :], in0=St[rs, hp, :], scalar=dch[rs, hp, ci:ci + 1], in1=pu[rs, hh * 128:(hh + 1) * 128],
                        op0=ALU.mult, op1=ALU.add), [pun, "dch", Sn], [Sn])

        Sbf = [sb("Sbf%d" % i, [128, 2, 128], BF16) for i in range(2)]
        ATf = sb("ATf", [128, 128], BF16)

        def gla_own(gi, c0, N):
            ntile = N // 128
            w, wn = loadw(w_in[:, 1536:2048], 512)
            for j, dst, dn in ((0, qgT, "qgT"), (1, kgT, "kgT")):
                for c in range(2):
                    p, pn = PS()
                    for k in range(8):
                        mm(p[:, :N], w[:, k, j * 256 + c * 128:j * 256 + (c + 1) * 128], xn[:, k, :N], k == 0, k == 7, ["xn", wn], [pn])
                    act(dst[:, c, :N], p[:, :N], AF.Copy, [pn], [dn])
            w, wn = loadw(w_in[:, 2560:3072], 512)
            for c in range(4):
                p, pn = PS()
                for k in range(8):
                    mm(p[:, :N], w[:, k, c * 128:(c + 1) * 128], xn[:, k, :N], k == 0, k == 7, ["xn", wn], [pn])
                act(rsil[:, c, :N], p[:, :N], AF.Silu, [pn], ["rsil"])
            for tt in range(ntile):
                ts_ = slice(tt * 128, (tt + 1) * 128)
                gla_prep(xn, "xn", tt * 128, lambda k: wf[:, k, 1024:1280], lambda k: wf[:, k, 1280:1792],
                         lambda k: wf[:, k, 1792:1808], "wf")
                for hp in range(2):
                    pgm, pgn = PS()
                    mm(pgm[:, :128], nl[:, hp * 128:(hp + 1) * 128], triI32, True, True, ["nl", "c32t"], [pgn])
                    act(egp[:], pgm[:, :128], AF.Exp, [pgn], ["egp"])
                    act(egn[:], pgm[:, :128], AF.Exp, [pgn], ["egn"], scale=-1.0)
                    for par in range(2):
                        prs = slice(par * 64, par * 64 + 64)
                        dve(lambda e, hp=hp, ts_=ts_, prs=prs, par=par: e.scalar_tensor_tensor(
                            out=qtl[prs, hp * 2 + par, ts_], in0=qgT[prs, hp, ts_], scalar=0.125, in1=egp[prs, :],
                            op0=ALU.mult, op1=ALU.mult), ["qgT", "egp"], ["qtl"])
                    dve(lambda e, hp=hp, ts_=ts_: e.tensor_tensor(out=ktl[:, hp, ts_], in0=kgT[:, hp, ts_], in1=egn[:], op=ALU.mult),
                        ["kgT", "egn"], ["ktl"])
                dve(lambda e: e.tensor_copy(out=Sbf[0][:], in_=Srun[:]), ["Srun"], ["Sbf0"])
                st_update(0, Srun, "Srun")
                dve(lambda e: e.tensor_copy(out=Sbf[1][:], in_=Srun[:]), ["Srun"], ["Sbf1"])
                st_update(1, Srun, "Srun")
                for h in range(4):
                    rs = slice((h % 2) * 64, (h % 2) * 64 + 64)
                    hp = h // 2
                    pa, pan = PS()
                    mm(pa[:, :128], ktl[:, hp, ts_], qtl[:, h, ts_], True, True, ["ktl", "qtl"], [pan])
                    dve(lambda e, pa=pa: e.tensor_tensor(out=ATf[:], in0=pa[:, :128], in1=triB, op=ALU.mult), [pan, "cb"], ["ATf"])
                    po, pon = PSA(0)
                    mm(po[:, :128], vgb[:, h * 128:(h + 1) * 128], ATf[:], True, False, ["vgb", "ATf"], [pon])
                    for ci in range(2):
                        mm(po[:, ci * 64:(ci + 1) * 64], Sbf[ci][:, hp, :], qtl[:, h, tt * 128 + ci * 64:tt * 128 + (ci + 1) * 64],
                           False, ci == 1, ["Sbf%d" % ci, "qtl"], [pon])
                    act(ogf[:, h, ts_], po[:, :128], AF.Copy, [pon], ["ogf"])
            gla_norm(N)

        def gla_norm(N):
            for h in range(4):
                act(sqt[:, h, :N], ogf[:, h, :N], AF.Square, ["ogf"], ["sqt"])
                p, pn = PS()
                mm(p[:, :N], ones[:], sqt[:, h, :N], True, True, ["ones", "sqt"], [pn])
                act(rstd[:, :N], p[:, :N], AF.Sqrt, [pn], ["rstd"], scale=1.0 / 128, bias=1e-6)
                dve(lambda e: e.reciprocal(out=rstd[:, :N], in_=rstd[:, :N]), ["rstd"], ["rstd"])
                dve(lambda e, h=h: e.scalar_tensor_tensor(out=ogf[:, h, :N], in0=ogf[:, h, :N], scalar=g_gla(h), in1=rstd[:, :N],
                                                          op0=ALU.mult, op1=ALU.mult), ["ogf", "vt", "rstd"], ["ogf"])
                dve(lambda e, h=h: e.tensor_tensor(out=ogT[:, h, :N], in0=ogf[:, h, :N], in1=rsil[:, h, :N], op=ALU.mult), ["ogf", "rsil"], ["ogT"])

        def gla_proj(N):
            w, wn = loadw(w_in[:, 1536:2048], 512)
            for j, dst, dn in ((0, qgT, "qgT"), (1, kgT, "kgT")):
                for c in range(2):
                    p, pn = PS()
                    for k in range(8):
                        mm(p[:, :N], w[:, k, j * 256 + c * 128:j * 256 + (c + 1) * 128], xn[:, k, :N], k == 0, k == 7, ["xn", wn], [pn])
                    act(dst[:, c, :N], p[:, :N], AF.Copy, [pn], [dn])
            w, wn = loadw(w_in[:, 2560:3072], 512)
            for c in range(4):
                p, pn = PS()
                for k in range(8):
                    mm(p[:, :N], w[:, k, c * 128:(c + 1) * 128], xn[:, k, :N], k == 0, k == 7, ["xn", wn], [pn])
                act(rsil[:, c, :N], p[:, :N], AF.Silu, [pn], ["rsil"])

        def cross_q(N):
            w, wn = loadw(w_in[:, 3088:3600], 512)
            for h in range(4):
                p, pn = PS()
                for k in range(8):
                    mm(p[:, :N], w[:, k, h * 128:(h + 1) * 128], xn[:, k, :N], k == 0, k == 7, ["xn", wn], [pn])
                act(qcT[:, h, :N], p[:, :N], AF.Copy, [pn], ["qcT"])

        def cross_att(c0_, N):
            cs = slice(c0_, c0_ + N)
            for h in range(4):
                po, pon = PSA(0)
                pd, pdn = PSA(1)
                for mt in range(2):
                    ps_, psn = PS()
                    mm(ps_[:, :N], mkT[:, h, mt * 128:(mt + 1) * 128], qcT[:, h, cs], True, True, ["mkT", "qcT"], [psn])
                    ptile = pT[mt]
                    ptn = "pT%d" % mt
                    act(ptile[:, :N], ps_[:, :N], AF.Exp, [psn], [ptn], scale=128 ** -0.5)
                    mm(po[:, :N], mvb[:, mt, h * 128:(h + 1) * 128], ptile[:, :N], mt == 0, mt == 1, ["mvb", ptn], [pon])
                    mm(pd[:, :N], ones[:], ptile[:, :N], mt == 0, mt == 1, ["ones", ptn], [pdn])
                dve(lambda e, pd=pd: e.reciprocal(out=rden[:, :N], in_=pd[:, :N]), [pdn], ["rden"])
                dve(lambda e, po=po, h=h: e.tensor_tensor(out=ocT[:, h, cs], in0=po[:, :N], in1=rden[:, :N], op=ALU.mult), [pon, "rden"], ["ocT"])

        def cross_own(gi, c0, N):
            cross_q(N)
            cross_att(0, N)

        def tail_own(gi, c0, N, smp=False):
            oc0 = c0 - HALO
            brs = ((w_brm, omT, "omT"), (w_brg, ogT, "ogT"), (w_brc, ocT, "ocT"))
            wbr = []
            for (wd, _, _) in brs:
                wbr.append(None)
            for half in range(8):
                wg_t = []
                for i in range(3):
                    wg_t.append(loadw3[i](w_gate[:, i * 1024 + half * 128:i * 1024 + (half + 1) * 128], 128))
                wb_t = []
                for i, (wd, _, _) in enumerate(brs):
                    wb_t.append(loadw3b[i](wd[:, half * 128:(half + 1) * 128], 128, 4))
                for c4 in range(1):
                    c8 = half
                    cs = slice(c4 * 128, (c4 + 1) * 128)
                    for i in range(3):
                        w, wn = wg_t[i]
                        p, pn = PS()
                        for k in range(8):
                            mm(p[:, :N], w[:, k, cs], xn[:, k, :N], k == 0, k == 7, ["xn", wn], [pn])
                        act(sg[:, i, :N], p[:, :N], AF.Sigmoid, [pn, "vt"], ["sg%d" % i], bias=b_gate(i * 8 + c8))
                    for i, (wd, oT_, on_) in enumerate(brs):
                        w, wn = wb_t[i]
                        p, pn = PS()
                        for k in range(4):
                            mm(p[:, :N], w[:, k, cs], oT_[:, k, :N], k == 0, k == 3, [on_, wn], [pn])
                        if i == 0:
                            dve(lambda e, p=p: e.tensor_tensor(out=tmpf[:, :N], in0=p[:, :N], in1=sg[:, 0, :N], op=ALU.mult), [pn, "sg0"], ["tmpf"])
                        else:
                            dve(lambda e, p=p, i=i: e.tensor_tensor(out=tmpf2[:, :N], in0=p[:, :N], in1=sg[:, i, :N], op=ALU.mult), [pn, "sg%d" % i], ["uext2"])
                            if i == 1:
                                dve(lambda e: e.tensor_tensor(out=tmpf[:, :N], in0=tmpf[:, :N], in1=tmpf2[:, :N], op=ALU.add), ["tmpf", "uext2"], ["tmpf"])
                            else:
                                dve(lambda e, c8=c8: e.tensor_tensor(out=mrg[:, c8, :N], in0=tmpf[:, :N], in1=tmpf2[:, :N], op=ALU.add),
                                    ["tmpf", "uext2"], ["mrg"])
            for half in range(2):
                w, wn = loadw(w_out[:, half * 512:(half + 1) * 512], 512)
                for c4 in range(4):
                    c8 = half * 4 + c4
                    p, pn = PS()
                    for k in range(8):
                        mm(p[:, :N], w[:, k, c4 * 128:(c4 + 1) * 128], mrg[:, k, :N], k == 0, k == 7, ["mrg", wn], [pn])
                    dve(lambda e, p=p, c8=c8: e.tensor_tensor(out=hT[:, c8, :N], in0=p[:, :N], in1=xo[:, c8, :N], op=ALU.add), [pn, "xTt0"], ["hT"])
            if debug and gi == 1:
                dma(dbgo["mrg"], mrg[:], ["mrg"], [])
                dma(dbgo["h1"], hT[:], ["hT"], [])
            norm(hT, "hT", N, g_ffn, hn, "mrg")
            for f4 in range(0, NF, 4):
                nf = min(4, NF - f4)
                wu, wun = loadw(w_up[:, f4 * 128:(f4 + nf) * 128], nf * 128)
                for fi in range(nf):
                    f = f4 + fi
                    pu, pun = PS()
                    for k in range(8):
                        mm(pu[:, :N], wu[:, k, fi * 128:(fi + 1) * 128], hn[:, k, :N], k == 0, k == 7, ["mrg", wun], [pun])
                    wgt, wgn = loadw3[f % 3](w_up[:, DFF + f * 128:DFF + (f + 1) * 128], 128)
                    pg_, pgn_ = PS()
                    for k in range(8):
                        mm(pg_[:, :N], wgt[:, k, :128], hn[:, k, :N], k == 0, k == 7, ["mrg", wgn], [pgn_])
                    ux, uxn = (uext, "uext") if f % 2 == 0 else (uext2, "uext2")
                    tf, tfn = (tmpf, "tmpf") if f % 2 == 0 else (sg[:, 0, :], "sg0")
                    if smp:
                        u3 = ux[:, 0:40].rearrange("p (s c) -> p s c", c=10)
                        t3 = tf[:, 0:32].rearrange("p (s c) -> p s c", c=8)
                        dve(lambda e, f=f, u3=u3: e.tensor_copy(out=u3[:, :, 0:2], in_=sconvt[:, f, :, :]), ["kst"], [uxn])
                        act(u3[:, :, 2:10], pu[:, :32].rearrange("p (s c) -> p s c", c=8), AF.Copy, [pun, uxn], [uxn])
                        dve(lambda e, f=f, u3=u3: e.tensor_copy(out=ulasts[:, f, :, :], in_=u3[:, :, 8:10]), [uxn], ["kst"])
                        dve(lambda e, f=f, u3=u3, t3=t3: e.tensor_scalar(out=t3, in0=u3[:, :, 2:10], scalar1=wc(2, f), scalar2=wc(3, f), op0=ALU.mult, op1=ALU.add),
                            [uxn, "vt"], [tfn])
                        dve(lambda e, f=f, u3=u3, t3=t3: e.scalar_tensor_tensor(out=t3, in0=u3[:, :, 1:9], scalar=wc(1, f), in1=t3, op0=ALU.mult, op1=ALU.add),
                            [uxn, "vt", tfn], [tfn])
                        dve(lambda e, f=f, u3=u3, t3=t3: e.scalar_tensor_tensor(out=t3, in0=u3[:, :, 0:8], scalar=wc(0, f), in1=t3, op0=ALU.mult, op1=ALU.add),
                            [uxn, "vt", tfn], [tfn])
                    else:
                        if gi == 1:
                            dve(lambda e, f=f, ux=ux: e.tensor_scalar(out=ux[:, 0:2], in0=uprev[:, f, :], scalar1=keep[:, 0:1], scalar2=None, op0=ALU.mult),
                                ["uprev", "keep"], [uxn])
                        else:
                            dve(lambda e, f=f, ux=ux: e.tensor_copy(out=ux[:, 0:2], in_=uprev[:, f, :]), ["uprev"], [uxn])
                        act(ux[:, 2:2 + N], pu[:, :N], AF.Copy, [pun, uxn], [uxn])
                        dve(lambda e, f=f, ux=ux: e.tensor_copy(out=uprev[:, f, :], in_=ux[:, N:N + 2]), [uxn], ["uprev"])
                        dve(lambda e, f=f, ux=ux, tf=tf: e.tensor_scalar(out=tf[:, :N], in0=ux[:, 2:2 + N], scalar1=wc(2, f), scalar2=wc(3, f), op0=ALU.mult, op1=ALU.add),
                            [uxn, "vt"], [tfn])
                        dve(lambda e, f=f, ux=ux, tf=tf: e.scalar_tensor_tensor(out=tf[:, :N], in0=ux[:, 1:1 + N], scalar=wc(1, f), in1=tf[:, :N], op0=ALU.mult, op1=ALU.add),
                            [uxn, "vt", tfn], [tfn])
                        dve(lambda e, f=f, ux=ux, tf=tf: e.scalar_tensor_tensor(out=tf[:, :N], in0=ux[:, 0:N], scalar=wc(0, f), in1=tf[:, :N], op0=ALU.mult, op1=ALU.add),
                            [uxn, "vt", tfn], [tfn])
                    act(tf[:, :N], tf[:, :N], AF.Gelu_apprx_tanh, [tfn], [tfn])
                    dve(lambda e, pg_=pg_, f=f, tf=tf: e.tensor_tensor(out=aT[:, f, :N], in0=tf[:, :N], in1=pg_[:, :N], op=ALU.mult), [tfn, pgn_], ["aT"])
            if smp:
                dma(conv_s, ulasts, ["kst"], [])
            elif gi == len(GROUPS) - 1:
                dma(conv_o, uprev[:], ["uprev"], [])
            for c8 in range(8):
                p, pn = PSA(c8 % 2)
                for f4 in range(0, NF, 4):
                    nf = min(4, NF - f4)
                    w, wn = loadwd(w_down[f4 * 128:(f4 + nf) * 128, c8 * 128:(c8 + 1) * 128], 128, nf)
                    for fi in range(nf):
                        f = f4 + fi
                        mm(p[:, :N], w[:, fi, :128], aT[:, f, :N], f == 0, f == NF - 1, ["aT", wn], [pn])
                dve(lambda e, p=p, c8=c8: e.tensor_tensor(out=hT[:, c8, :N], in0=p[:, :N], in1=hT[:, c8, :N], op=ALU.add), [pn, "hT"], ["hT"])
            if debug and gi == 1:
                dma(dbgo["h2"], hT[:], ["hT"], [])
            if gi > 0 or smp:
                act(sqt[:, :, :N], hT[:, :, :N], AF.Square, ["hT"], ["sqt"])
                p, pn = PS()
                for k in range(8):
                    mm(p[:, :N], ones[:], sqt[:, k, :N], k == 0, k == 7, ["ones", "sqt"], [pn])
                act(rstd[:, :N], p[:, :N], AF.Sqrt, [pn], ["rstd"], scale=1.0 / 1024, bias=1e-6)
                dve(lambda e: e.reciprocal(out=rstd[:, :N], in_=rstd[:, :N]), ["rstd"], ["rstd"])
                for k in range(8):
                    dve(lambda e, k=k: e.scalar_tensor_tensor(out=yst[:, k, :N], in0=hT[:, k, :N], scalar=g_fin(k), in1=rstd[:, :N],
                                                              op0=ALU.mult, op1=ALU.mult), ["hT", "vt", "rstd"], ["xTt0"])
                if smp:
                    dma(yT_s.rearrange("(k p) t -> p k t", p=128), yst[:, :, :N], ["xTt0"], [])
                else:
                    dma(yT.rearrange("(k p) t -> p k t", p=128)[:, :, oc0:oc0 + N], yst[:, :, :N], ["xTt0"], [])

        wg3 = [sb("wg3_%d" % i, [128, 8, 128], BF16) for i in range(3)]
        wb3 = [sb("wb3_%d" % i, [128, 4, 128], BF16) for i in range(3)]
        wdt = [sb("wdt%d" % i, [128, 4, 128], BF16) for i in range(2)]
        wdctr = [0]

        def mk_loadw3(i):
            def f(src_ap, ncols):
                dma(wg3[i][:, :, :ncols], src_ap.rearrange("(k p) c -> p k c", p=128), ["wscr_" + src_ap.tensor.name], ["wg3_%d" % i])
                return wg3[i], "wg3_%d" % i
            return f

        def mk_loadw3b(i):
            def f(src_ap, ncols, kr):
                dma(wb3[i][:, :kr, :ncols], src_ap.rearrange("(k p) c -> p k c", p=128), ["wscr_" + src_ap.tensor.name], ["wb3_%d" % i])
                return wb3[i], "wb3_%d" % i
            return f

        loadw3 = [mk_loadw3(i) for i in range(3)]
        loadw3b = [mk_loadw3b(i) for i in range(3)]

        def loadwd(src_ap, ncols, kr):
            i = wdctr[0] % 2
            wdctr[0] += 1
            dma(wdt[i][:, :kr, :ncols], src_ap.rearrange("(k p) c -> p k c", p=128), ["wscr_" + src_ap.tensor.name], ["wdt%d" % i])
            return wdt[i], "wdt%d" % i

        vstf = vst[:, :, :].rearrange("p a b -> p (a b)")
        kstf = kst[:, :, :].rearrange("p a b -> p (a b)").bitcast(F32)
        csm = vstf[:, 128:512].bitcast(F32)
        knT = vstf[:, 0:128].rearrange("p (c t) -> p c t", t=32)
        gs2 = vstf[0:64, 512:640].bitcast(F32)
        biasb = vstf[0:64, 640:704]
        biasT = vstf[0:64, 704:768]
        m8s = vstf[0:64, 768:784].bitcast(F32)
        rd1 = vstf[0:64, 784:786].bitcast(F32)
        sconvt = kstf[:, 0:176].rearrange("p (f s i) -> p f s i", s=4, i=2)
        ulasts = kstf[:, 176:352].rearrange("p (f s i) -> p f s i", s=4, i=2)
        Qbd = Bq[:, :, :, :].rearrange("p a b c -> p (a b c)")[:, 0:1024].rearrange("p (a b c) -> p a b c", a=4, b=4)
        vnew = qTa[0:8, :, :].rearrange("p a b -> p (a b)").rearrange("p (s c) -> p s c", c=512)
        gself = gsel[:, :, :].rearrange("p a b -> p (a b)")
        ptf = gself[:, 0:128]
        pti = gself[:, 128:256].bitcast(I32)
        idxu = rden[:, 0:128].bitcast(I32)
        cm8 = csm[0:8, 1:65]
        EtA = csm[0:64, 65:129].rearrange("p (h t) -> p h t", t=8)
        tri8I = csm[0:8, 129:137]
        tri8U = csm[0:8, 137:145]
        tri8m = csm[0:8, 145:153]
        m16c = csm[0:8, 153:154]

        def sample_phase():
            N = 32
            dma(csm, csm_d, [], ["vst"])
            dma(sconvt, sconv, [], ["kst"])
            dma(xo[:, :, :N], xT_smp.rearrange("(k p) t -> p k t", p=128), [], ["xTt0"])
            norm(xo, "xTt0", N, g_mix, xn, "xn")
            w, wn = loadw(w_in[:, 0:512], 512)
            for c in range(4):
                p, pn = PS()
                for k in range(8):
                    mm(p[:, :N], w[:, k, c * 128:(c + 1) * 128], xn[:, k, :N], k == 0, k == 7, ["xn", wn], [pn])
                act(qT[:, c, :N], p[:, :N], AF.Copy, [pn], ["qT"])
            w, wn = loadw(w_in[:, 512:1024], 512)
            for c in range(4):
                p, pn = PS()
                for k in range(8):
                    mm(p[:, :N], w[:, k, c * 128:(c + 1) * 128], xn[:, k, :N], k == 0, k == 7, ["xn", wn], [pn])
                act(knT[:, c, :], p[:, :N], AF.Copy, [pn], ["vst"])
                dve(lambda e, p=p, c=c: e.tensor_copy(out=kstg[:, c, :N], in_=p[:, :N]), [pn], ["hT"])
            dma(kT_s.rearrange("(c p) t -> p c t", p=128), kstg[:, 0:4, :N], ["hT"], [])
            w, wn = loadw(w_in[:, 1024:1536], 512)
            for s_ in range(4):
                p, pn = PS()
                for k in range(8):
                    mm(p[0:8, :], xn[:, k, s_ * 8:(s_ + 1) * 8], w[:, k, :], k == 0, k == 7, ["xn", wn], [pn])
                act(vnew[:, s_, :], p[0:8, :], AF.Copy, [pn], ["qTa"])
                dve(lambda e, p=p: e.tensor_copy(out=stg[0:8, :], in_=p[0:8, :]), [pn], ["stg"])
                dma(v_s[s_ * 8:(s_ + 1) * 8, :], stg[0:8, :], ["stg"], [])
            dve(lambda e: e.memset(Qbd, 0.0), [], ["Bq"])
            for hp in range(4):
                for h2 in range(2):
                    h = 2 * hp + h2
                    prs = slice(h2 * 64, h2 * 64 + 64)
                    dve(lambda e, hp=hp, h=h, prs=prs: e.tensor_copy(out=Qbd[prs, hp, :, h * 8:(h + 1) * 8],
                                                                     in_=qT[prs, hp, 0:32].rearrange("p (s t) -> p s t", t=8)), ["qT", "Bq"], ["Bq"])
            Sraw = vaug0[:, :, :].rearrange("p a (b c) -> p (a b) c", c=64)
            NS = 8
            kslot = [("wt0s%d" % j, wt[0][:, j, :]) for j in range(NS)]
            vslot = [("wt1s%d" % j, wt[1][:, j, :]) for j in range(NS)]
            dve(lambda e: e.memset(m8[0:1, 0, 0:1], 0.0), ["wt0", "wt1"], [n for n, _ in kslot + vslot] + ["m8"])
            for s_ in range(4):
                cs = slice(s_ * 8, (s_ + 1) * 8)
                dma(pti, ptab[s_:s_ + 1, :].to_broadcast([128, 128]), [], ["gsel"])
                dve(lambda e: e.tensor_copy(out=ptf, in_=pti), ["gsel"], ["gsel"])
                dve(lambda e: e.tensor_scalar(out=ptf, in0=ptf, scalar1=128.0, scalar2=csm[:, 0:1], op0=ALU.mult, op1=ALU.add), ["gsel", "vst"], ["gsel"])
                dve(lambda e: e.tensor_copy(out=idxu, in_=ptf), ["gsel"], ["rden"])
                pgate, pgaten = PSA(1)
                for pg in range(128):
                    ktn, ktf = kslot[pg % NS]
                    kt = ktf.rearrange("p (a b) -> p a b", b=128)
                    S.op("pool", lambda e, ktf=ktf, pg=pg: e.indirect_dma_start(
                        out=ktf, out_offset=None, in_=poolKT,
                        in_offset=bass.IndirectOffsetOnAxis(ap=idxu[:, pg:pg + 1], axis=0)), reads=["rden"], writes=[ktn], dma=True)
                    ps_, psn = PS()
                    for hp in range(4):
                        mm(ps_[:, :64], kt[:, hp, :], Qbd[:, hp, s_, :], hp == 0, hp == 3, [ktn, "Bq"], [psn])
                    act(Sraw[:, pg, :], ps_[:, :64], AF.Copy, [psn], ["vaug0"])
                    if pg >= 1:
                        q_ = pg - 1
                        mm(pgate[0:64, q_ // 2:q_ // 2 + 1], Sraw[:, q_, :], ones[:, 0:1], q_ % 2 == 0, q_ % 2 == 1, ["vaug0", "ones"], [pgaten])
                mm(pgate[0:64, 63:64], Sraw[:, 127, :], ones[:, 0:1], False, True, ["vaug0", "ones"], [pgaten])
                dve(lambda e, pgate=pgate: e.tensor_copy(out=gs2, in_=pgate[0:64, 0:64]), [pgaten], ["vst"])
                dve(lambda e: e.max(out=m8s, in_=gs2), ["vst"], ["vst"])
                dve(lambda e: e.tensor_scalar(out=gs2, in0=gs2, scalar1=m8s[:, 2:3], scalar2=None, op0=ALU.is_ge), ["vst", "vst"], ["vst"])
                dve(lambda e: e.tensor_scalar(out=biasb, in0=gs2, scalar1=-1.0, scalar2=-NEG, op0=ALU.add, op1=ALU.mult), ["vst"], ["vst"])
                pb, pbn = PSB()
                S.op("pe", lambda e, pb=pb: e.transpose(out=pb[0:64, 0:64], in_=biasb, identity=cb[0:64, 0, 0:64]), reads=["vst", "cb"], writes=[pbn])
                act(biasT, pb[0:64, 0:64], AF.Copy, [pbn], ["vst"])
                po, pon = PSA(0)
                pden, pdenn = PSA(1)
                pbl = {}

                def issue_bias(b4):
                    pb_, pb_n = PS()
                    for j in range(4):
                        blk = (b4 * 4 + j) // 2
                        mm(pb_[:, j * 64:(j + 1) * 64], cb[0:64, 0, blk:blk + 1].to_broadcast([64, 128]), biasT, True, True, ["cb", "vst"], [pb_n])
                    pbl[b4] = (pb_, pb_n)

                issue_bias(0)
                for b4 in range(32):
                    if b4 + 1 < 32:
                        issue_bias(b4 + 1)
                    pb_, pb_n = pbl.pop(b4)
                    dve(lambda e, pb_=pb_, b4=b4: e.scalar_tensor_tensor(
                        out=tmpf[:, :256], in0=Sraw[:, b4 * 4:(b4 + 1) * 4, :].rearrange("p a b -> p (a b)"), scalar=0.125, in1=pb_[:, :256],
                        op0=ALU.mult, op1=ALU.add), ["vaug0", pb_n], ["tmpf"])
                    ptile = pT[b4 % 2]
                    ptn = "pT%d" % (b4 % 2)
                    act(ptile[:, :256], tmpf[:, :256], AF.Exp, ["tmpf"], [ptn])
                    for j in range(4):
                        pg = b4 * 4 + j
                        vtn, vt_ = vslot[pg % NS]
                        S.op("pool", lambda e, vt_=vt_, pg=pg: e.indirect_dma_start(
                            out=vt_, out_offset=None, in_=poolV,
                            in_offset=bass.IndirectOffsetOnAxis(ap=idxu[:, pg:pg + 1], axis=0)), reads=["rden"], writes=[vtn], dma=True)
                        mm(po[0:64, :], ptile[:, j * 64:(j + 1) * 64], vt_, pg == 0, False, [ptn, vtn], [pon])
                        mm(pden[0:64, 100:101], ptile[:, j * 64:(j + 1) * 64], ones[:, 0:1], pg == 0, False, [ptn, "ones"], [pdenn])
                pso, pson = PS()
                for hp in range(4):
                    mm(pso[0:8, :64], knT[:, hp, cs], Qbd[:, hp, s_, :], hp == 0, hp == 3, ["vst", "Bq"], [pson])
                dve(lambda e, pso=pso: e.scalar_tensor_tensor(out=tmpf[0:8, :64], in0=pso[0:8, :64], scalar=0.125, in1=cm8, op0=ALU.mult, op1=ALU.add),
                    [pson, "vst"], ["tmpf"])
                act(pT[0][0:8, :64], tmpf[0:8, :64], AF.Exp, ["tmpf"], ["pT0"])
                mm(po[0:64, :], pT[0][0:8, :64], vnew[:, s_, :], False, True, ["pT0", "qTa"], [pon])
                mm(pden[0:64, 100:101], pT[0][0:8, :64], ones[0:8, 0:1], False, True, ["pT0", "ones"], [pdenn])
                act(stg[0:64, :], po[0:64, :], AF.Copy, [pon], ["stg"])
                dve(lambda e, pden=pden: e.reciprocal(out=rd1, in_=pden[0:64, 100:101]), [pdenn], ["vst"])
                dve(lambda e: e.tensor_scalar(out=stg[0:64, :], in0=stg[0:64, :], scalar1=rd1[:, 0:1], scalar2=None, op0=ALU.mult),
                    ["stg", "vst"], ["stg"])
                for h in range(8):
                    hp = h // 2
                    rs = slice((h % 2) * 64, (h % 2) * 64 + 64)
                    pz, pzn = PS()
                    mm(pz[:, 0:8], stg[0:64, hp * 128:(hp + 1) * 128], EtA[:, h, :], True, True, ["stg", "vst"], [pzn])
                    act(omT[rs, hp, cs], pz[rs, 0:8], AF.Copy, [pzn], ["omT"])
            dve(lambda e: e.memset(m8[0:1, 0, 0:1], 0.0), [n for n, _ in kslot + vslot], ["wt0", "wt1", "m8"])
            gla_proj(N)
            for s_ in range(4):
                cs = slice(s_ * 8, (s_ + 1) * 8)
                p, pn = PS()
                for k in range(8):
                    mm(p[:16, :8], wf[:, k, 1792:1808], xn[:, k, cs], k == 0, k == 7, ["xn", "wf"], [pn])
                dve(lambda e, p=p: e.tensor_copy(out=agT[:, :8], in_=p[:16, :8]), [pn], ["agT"])
                p2, pn2 = PS()
                mm(p2[0:8, :256], agT[:, :8], wa2[:], True, False, ["agT", "wa2"], [pn2])
                mm(p2[0:8, :256], ones[0:1, 0:8], ba[:], False, True, ["ones", "ba"], [pn2])
                act(nl[0:8, :], p2[0:8, :256], AF.Exp, [pn2], ["nl"], scale=-1.0)
                act(nl[0:8, :], nl[0:8, :], AF.Ln, ["nl"], ["nl"], bias=1.0)
                p3, pn3 = PS()
                mm(p3[0:8, :256], tri8U, nl[0:8, :], True, True, ["vst", "nl"], [pn3])
                act(eR[0:8, :], p3[0:8, :256], AF.Exp, [pn3], ["eR"])
                pk, pkn = PS()
                for k in range(8):
                    mm(pk[0:8, :256], xn[:, k, cs], wf[:, k, 1024:1280], k == 0, k == 7, ["xn", "wf"], [pkn])
                dve(lambda e, pk=pk: e.tensor_tensor(out=ktil[0:8, :], in0=pk[0:8, :256], in1=eR[0:8, :], op=ALU.mult), [pkn, "eR"], ["ktil"])
                pv, pvn = PS()
                for k in range(8):
                    mm(pv[0:8, :], xn[:, k, cs], wf[:, k, 1280:1792], k == 0, k == 7, ["xn", "wf"], [pvn])
                act(vgb[0:8, :], pv[0:8, :], AF.Copy, [pvn], ["vgb"])
                pd, pdn = PS()
                for half in range(2):
                    mm(pd[:, half:half + 1], nl[0:8, half * 128:(half + 1) * 128], m16c, True, True, ["nl", "vst"], [pdn])
                act(dch[:, :, 0], pd[:, 0:2], AF.Exp, [pdn], ["dch"])
                for hp in range(2):
                    pgm, pgn = PS()
                    mm(pgm[:, :8], nl[0:8, hp * 128:(hp + 1) * 128], tri8I, True, True, ["nl", "vst"], [pgn])
                    act(egp[:, :8], pgm[:, :8], AF.Exp, [pgn], ["egp"])
                    act(egn[:, :8], pgm[:, :8], AF.Exp, [pgn], ["egn"], scale=-1.0)
                    for par in range(2):
                        prs = slice(par * 64, par * 64 + 64)
                        dve(lambda e, hp=hp, cs=cs, prs=prs, par=par: e.scalar_tensor_tensor(
                            out=qtl[prs, hp * 2 + par, cs], in0=qgT[prs, hp, cs], scalar=0.125, in1=egp[prs, :8],
                            op0=ALU.mult, op1=ALU.mult), ["qgT", "egp"], ["qtl"])
                    dve(lambda e, hp=hp, cs=cs: e.tensor_tensor(out=ktl[:, hp, cs], in0=kgT[:, hp, cs], in1=egn[:, :8], op=ALU.mult),
                        ["kgT", "egn"], ["ktl"])
                dma(Sst[:], sgla[s_], [], ["Sst"])
                dve(lambda e: e.tensor_copy(out=Sbf[0][:], in_=Sst[:]), ["Sst"], ["Sbf0"])
                for h in range(4):
                    hp = h // 2
                    pa, pan = PS()
                    mm(pa[0:8, 0:8], ktl[:, hp, cs], qtl[:, h, cs], True, True, ["ktl", "qtl"], [pan])
                    dve(lambda e, pa=pa: e.tensor_tensor(out=ATf[0:8, 0:8], in0=pa[0:8, 0:8], in1=tri8m, op=ALU.mult), [pan, "vst"], ["ATf"])
                    po, pon = PSA(0)
                    mm(po[:, 0:8], vgb[0:8, h * 128:(h + 1) * 128], ATf[0:8, 0:8], True, False, ["vgb", "ATf"], [pon])
                    mm(po[:, 0:8], Sbf[0][:, hp, :], qtl[:, h, cs], False, True, ["Sbf0", "qtl"], [pon])
                    act(ogf[:, h, cs], po[:, 0:8], AF.Copy, [pon], ["ogf"])
                for hp in range(2):
                    pu, pun = PS()
                    mm(pu[:, :256], ktil[0:8, hp * 128:(hp + 1) * 128], vgb[0:8, hp * 256:(hp + 1) * 256], True, True, ["ktil", "vgb"], [pun])
                    for hh in range(2):
                        rs = slice(hh * 64, (hh + 1) * 64)
                        dve(lambda e, pu=pu, hp=hp, hh=hh, rs=rs: e.scalar_tensor_tensor(
                            out=Sst[rs, hp, :], in0=Sst[rs, hp, :], scalar=dch[rs, hp, 0:1], in1=pu[rs, hh * 128:(hh + 1) * 128],
                            op0=ALU.mult, op1=ALU.add), [pun, "dch", "Sst"], ["Sst"])
                dma(gla_s[s_], Sst[:], ["Sst"], [])
            gla_norm(N)
            cross_q(N)
            for s_ in range(4):
                dma(mkT[:], mkTs[s_].rearrange("h d m -> d h m"), [], ["mkT"], q="pool")
                dma(mvb[:], mvs[s_].rearrange("(t p) c -> p t c", p=128), [], ["mvb"], q="pool")
                cross_att(s_ * 8, 8)
            tail_own(99, 0, N, smp=True)

        for gi, (c0, N) in enumerate(GROUPS[:ngroups]):
            ntile = N // 128
            dma(xo[:, :, :N], xT_own[:, c0:c0 + N].rearrange("(k p) t -> p k t", p=128), [], ["xTt0"])
            norm(xo, "xTt0", N, g_mix, xn, "xn")
            oc0 = c0 - HALO

            if "moba" in phases:
                w, wn = loadw(w_in[:, 0:512], 512)
                for c in range(4):
                    p, pn = PS()
                    for k in range(8):
                        mm(p[:, :N], w[:, k, c * 128:(c + 1) * 128], xn[:, k, :N], k == 0, k == 7, ["xn", wn], [pn])
                    act(qT[:, c, :N], p[:, :N], AF.Copy, [pn], ["qT"])
                for tt in range(ntile):
                    p, pn = PS()
                    for k in range(8):
                        mm(p[:], xn[:, k, tt * 128:(tt + 1) * 128], w[:, k, :], k == 0, k == 7, ["xn", wn], [pn])
                    act(Bq[:, tt, :, 0:64], p[:].rearrange("p (h d) -> p h d", h=8), AF.Copy, [pn], ["Bq"])
                w, wn = loadw(w_in[:, 512:1024], 512)
                for h in range(8):
                    p, pn = PS()
                    for k in range(8):
                        mm(p[:64, :N], w[:, k, h * 64:(h + 1) * 64], xn[:, k, :N], k == 0, k == 7, ["xn", wn], [pn])
                    act(kla[0:64, h, :N], p[:64, :N], AF.Copy, [pn], ["kla"])
                    if gi > 0:
                        dve(lambda e, p=p, h=h: e.tensor_copy(out=kstg[0:64, h, :N], in_=p[:64, :N]), [pn], ["hT"])
                if gi > 0:
                    dma(kT_o.rearrange("(h d) t -> d h t", d=64)[:, :, oc0:oc0 + N], kstg[0:64, :, :N], ["hT"], [])
                w, wn = loadw(w_in[:, 1024:1536], 512)
                for tt in range(ntile):
                    p, pn = PS()
                    for k in range(8):
                        mm(p[:], xn[:, k, tt * 128:(tt + 1) * 128], w[:, k, :], k == 0, k == 7, ["xn", wn], [pn])
                    act(vloc[:, tt, :], p[:], AF.Copy, [pn], ["vloc"])
                    if gi > 0:
                        dve(lambda e, p=p: e.tensor_copy(out=stg[:], in_=p[:]), [pn], ["stg"])
                        dma(v_o[oc0 + tt * 128:oc0 + (tt + 1) * 128, :], stg[:], ["stg"], [])
                for tt in range(ntile):
                    qb = (c0 + tt * 128) // 256
                    pg, pgn = PS()
                    for hp in range(4):
                        mm(pg[:, hp * 64:(hp + 1) * 64], qT[:, hp, tt * 128:(tt + 1) * 128], kmbd[:, hp, :], True, True, ["qT", "kmbd"], [pgn])
                    dve(lambda e, pg=pg, qb=qb: e.tensor_tensor(out=gsel[:], in0=pg[:, :256].rearrange("p (h b) -> p h b", h=8),
                                                                in1=bv[:, qb:qb + 1, :].to_broadcast([128, 8, 32]), op=ALU.add), [pgn, "bv"], ["gsel"])
                    for h in range(8):
                        dve(lambda e, h=h: e.max(out=m8[:, h, :], in_=gsel[:, h, :]), ["gsel"], ["m8"])
                    dve(lambda e: e.tensor_scalar_max(out=m8[:, :, 2:3], in0=m8[:, :, 2:3], scalar1=-1e29), ["m8"], ["m8"])
                    dve(lambda e: e.tensor_tensor(out=gsel[:], in0=gsel[:], in1=m8[:, :, 2:3].to_broadcast([128, 8, 32]), op=ALU.is_ge), ["gsel", "m8"], ["gsel"])
                    dve(lambda e, tt=tt: e.tensor_scalar(out=Bq[:, tt, :, 64:96], in0=gsel[:], scalar1=-1.0, scalar2=-NEG, op0=ALU.add, op1=ALU.mult),
                        ["gsel"], ["Bq"])
                    for hq in range(2):
                        pb, pbn = PSB()
                        for h4 in range(4):
                            h = hq * 4 + h4
                            S.op("pe", lambda e, pb=pb, h4=h4, h=h, tt=tt: e.transpose(out=pb[:96, h4 * 128:(h4 + 1) * 128], in_=Bq[:, tt, h, :], identity=ident),
                                 reads=["Bq", "cb"], writes=[pbn])
                        act(qTa[:, hq * 4:(hq + 1) * 4, tt * 128:(tt + 1) * 128], pb[:96, :512].rearrange("p (h q) -> p h q", h=4), AF.Copy, [pbn], ["qTa"])
                npast = min(32, 23 + gi)
                for h in range(8):
                    par = h % 2
                    hp = h // 2
                    ka = kaug0
                    kan = "kaug0"
                    L = npast * 256
                    nkt = npast * 2
                    dma(ka[0:64, :L], Ks[h * 64:(h + 1) * 64, :L], ["Ks"], [kan])
                    if par == 0:
                        dma(vaug0[:, :nkt, :], Vs[:L, hp * 128:(hp + 1) * 128].rearrange("(t p) d -> p t d", p=128), ["Vs"], ["vaug0"])
                    po, pon = PSA(0)
                    pd, pdn = PSA(1)
                    items = []
                    for kt in range(0, nkt, 2):
                        items.append([(ka[:, (kt + j) * 128:(kt + j + 1) * 128], kan, vaug0[:, kt + j, :], "vaug0", 0, 2, False, j * 256) for j in range(2)])
                    items.append([(kla[:, h, 0:128], "kla", vloc[:, 0, hp * 128:(hp + 1) * 128], "vloc", 0, 2, True, 0),
                                  (kla[:, h, 128:256], "kla", vloc[:, 1, hp * 128:(hp + 1) * 128], "vloc", 1, 1, True, 256)])
                    LA = 2
                    psl = {}

                    def issue_qk(i):
                        ps_, psn = PS()
                        for (kap, kn_, vap, vn_, q0, qn_, diag, col0) in items[i]:
                            mm(ps_[:, col0:col0 + qn_ * 128], kap, qTa[:, h, q0 * 128:(q0 + qn_) * 128], True, True, [kn_, "qTa"], [psn])
                        psl[i] = (ps_, psn)

                    for i in range(min(LA, len(items))):
                        issue_qk(i)
                    for si, it in enumerate(items):
                        if si + LA < len(items):
                            issue_qk(si + LA)
                        ps_, psn = psl.pop(si)
                        ptile = pT[si % 2]
                        ptn = "pT%d" % (si % 2)
                        wtot = it[-1][7] + it[-1][5] * 128
                        act(ptile[:, :wtot], ps_[:, :wtot], AF.Exp, [psn], [ptn], scale=0.125)
                        for (kap, kn_, vap, vn_, q0, qn_, diag, col0) in it:
                            if diag:
                                dve(lambda e, ptile=ptile, col0=col0: e.tensor_tensor(out=ptile[:, col0:col0 + 128], in0=ptile[:, col0:col0 + 128], in1=tri01, op=ALU.mult),
                                    [ptn, "cb"], [ptn])
                        for j, (kap, kn_, vap, vn_, q0, qn_, diag, col0) in enumerate(it):
                            qs = slice(q0 * 128, (q0 + qn_) * 128)
                            first = si == 0 and j == 0
                            last = si == len(items) - 1 and j == len(it) - 1
                            mm(po[:, qs], vap, ptile[:, col0:col0 + qn_ * 128], first, last, [vn_, ptn], [pon])
                            acc, accn, fn_ = (tmpf, "tmpf", dve) if j == 0 else (uext2, "uext2", pool)
                            if si == 0:
                                fn_(lambda e, ptile=ptile, col0=col0, qs=qs, qn_=qn_, acc=acc: e.tensor_copy(out=acc[:, qs], in_=ptile[:, col0:col0 + qn_ * 128]),
                                    [ptn], [accn])
                            else:
                                fn_(lambda e, ptile=ptile, col0=col0, qs=qs, qn_=qn_, acc=acc: e.tensor_tensor(out=acc[:, qs], in0=acc[:, qs], in1=ptile[:, col0:col0 + qn_ * 128], op=ALU.add),
                                    [ptn, accn], [accn])
                    mm(pd[:, :N], c32t[:, 388:516], tmpf[:, :N], True, False, ["c32t", "tmpf"], [pdn])
                    mm(pd[:, :N], c32t[:, 388:516], uext2[:, :N], False, True, ["c32t", "uext2"], [pdn])
                    rs = slice(par * 64, par * 64 + 64)
                    dve(lambda e, pd=pd, rs=rs: e.reciprocal(out=rden[rs, :N], in_=pd[rs, :N]), [pdn], ["rden"])
                    dve(lambda e, rs=rs, hp=hp, po=po: e.tensor_tensor(out=omT[rs, hp, :N], in0=po[rs, :N], in1=rden[rs, :N], op=ALU.mult),
                        [pon, "rden"], ["omT"])

            if debug and gi == 1:
                dma(dbgo["om"][:, 0:4, :], omT[:], ["omT"], [])
                dma(dbgo["qta"][0:96, :, :], qTa[:], ["qTa"], [])
                dma(dbgo["xn"], xn[:], ["xn"], [])
            if "gla" in phases:
                gla_own(gi, c0, N)
            if "cross" in phases:
                cross_own(gi, c0, N)
            if debug and gi == 1:
                dma(dbgo["og"][:, 0:4, :], ogT[:], ["ogT"], [])
                dma(dbgo["oc"][:, 0:4, :], ocT[:], ["ocT"], [])
            if "tail" in phases:
                tail_own(gi, c0, N)
        if "smp" in phases:
            sample_phase()
        S.emit(nc)
    return nc


_NC_CACHE = {}


def _consts():
    p = np.arange(128)
    ident = np.eye(128, dtype=np.float32)
    tri01 = (p[:, None] <= p[None, :]).astype(np.float32)
    blk2 = ((p[:, None] // 64) == (p[None, :] // 64)).astype(np.float32)
    triB = tri01 * blk2
    cb = np.concatenate([ident, tri01, blk2, triB, ident, ident], axis=1).astype(np.float32)
    triU = ((p[:, None] > p[None, :]) & ((p[:, None] // 64) == (p[None, :] // 64))).astype(np.float32) * (-1.0 / 16)
    triI = triB * (-1.0 / 16)
    chk = np.zeros((128, 2), np.float32)
    chk[:64, 0] = -1.0 / 16
    chk[64:, 1] = -1.0 / 16
    sel = np.zeros((2, 128, 128), np.float32)
    sel[0, 64, 0:64] = 1.0
    sel[1, :, :] = 1.0
    c32 = np.concatenate([triU, triI, chk, np.zeros((128, 2), np.float32), sel[0], sel[1]], axis=1)
    oh = np.zeros((32, SEQ), np.float32)
    for j in range(32):
        oh[j, j * 256:(j + 1) * 256] = 1.0
    return cb, c32, oh


def _csm():
    c = np.zeros((128, 192), np.float32)
    c[:, 0] = np.arange(128)
    i = np.arange(8)[:, None]
    t = np.tile(np.arange(8), 8)[None, :]
    c[0:8, 1:65] = np.where(i <= t, 0.0, NEG)
    c[0:64, 65:129] = np.eye(64)
    tri = (np.arange(8)[:, None] <= np.arange(8)[None, :]).astype(np.float32)
    c[0:8, 129:137] = tri * (-1.0 / 16)
    c[0:8, 137:145] = (np.arange(8)[:, None] > np.arange(8)[None, :]).astype(np.float32) * (-1.0 / 16)
    c[0:8, 145:153] = tri
    c[0:8, 153] = -1.0 / 16
    return c


def pool_layouts(cache_k, cache_v):
    n = cache_k.shape[1]
    kt = np.ascontiguousarray(np.asarray(cache_k[0], np.float32).reshape(n, 128, 4, 2, 64).transpose(0, 3, 4, 2, 1)).reshape(n * 128, 512)
    v = np.ascontiguousarray(np.asarray(cache_v[0], np.float32)).reshape(n * 128, 512)
    return kt, v


def make_in_maps(inp, pools=None):
    f = lambda a: np.ascontiguousarray(np.asarray(a, dtype=np.float32))
    if pools is None:
        pools = pool_layouts(inp["cache_moba_k"], inp["cache_moba_v"])
    poolKT, poolV = pools
    csm = _csm()
    x_sample = f(inp["x_sample"])
    page_table = np.asarray(inp["page_table"]).astype(np.int32)
    state_gla = f(inp["state_gla"])
    state_conv = f(inp["state_conv"])
    cmk = f(inp["cache_mem_k"])
    cmv = f(inp["cache_mem_v"])
    x_prompt = f(inp["x_prompt"])
    cb, c32, oh = _consts()
    fm = lambda v: f(v).reshape(-1, 128).T
    w_conv = inp["w_conv"]
    vecs = np.concatenate([fm(inp["norm_mix"][0]), fm(inp["norm_ffn"][0]), fm(inp["norm_final"]), fm(inp["norm_mem"][0]),
                           fm(inp["b_gate"][0]), fm(inp["norm_gla"][0]), fm(w_conv[0, 0]), fm(w_conv[0, 1]), fm(w_conv[0, 2]),
                           fm(inp["b_conv"][0])], axis=1)
    vecs = np.ascontiguousarray(vecs)
    shared = dict(w_in=f(inp["w_in"][0]), w_a2=f(inp["w_gla_a2"][0]), b_a=f(inp["b_gla_a"][0]).reshape(1, 256),
                  w_mem=f(inp["w_mem_kv"][0]), w_brm=f(inp["w_br_moba"][0]), w_brg=f(inp["w_br_gla"][0]),
                  w_brc=f(inp["w_br_cross"][0]), w_gate=f(inp["w_gate"][0]), w_out=f(inp["w_out"][0]),
                  w_up=f(inp["w_up"][0]), w_down=f(inp["w_down"][0]), vecs=vecs, consts=cb, c32=c32, ohrows=oh)
    in_maps = []
    for c in range(8):
        b, r = c // 4, c % 4
        xT = np.ascontiguousarray(x_prompt[b].T)
        own = np.zeros((1024, NCOL), np.float32)
        lo = r * NOWN - HALO
        if lo < 0:
            own[:, HALO:] = xT[:, 0:NOWN]
        else:
            own[:] = xT[:, lo:lo + NCOL]
        bvld = np.zeros((128, 9, 32), np.float32)
        for i in range(9):
            cur = 8 * r - 1 + i
            bvld[:, i, :] = np.where(np.arange(32) < cur, 0.0, -1e30)[None, :]
        selr = np.zeros((128, 4), np.float32)
        selr[:, r] = 1.0
        m = dict(shared)
        m.update(xT_full=xT, xT_own=own, memT=np.ascontiguousarray(f(inp["mem_prompt"][b]).T), blkvalid=bvld, selr=selr)
        sq = slice(4 * c, 4 * c + 4)
        m.update(xT_smp=np.ascontiguousarray(x_sample[sq].reshape(32, 1024).T), ptab=np.ascontiguousarray(page_table[sq]),
                 poolKT=poolKT, poolV=poolV,
                 sgla=np.ascontiguousarray(state_gla[0, sq].reshape(4, 2, 2, 64, 128).transpose(0, 2, 3, 1, 4).reshape(4, 128, 2, 128)),
                 sconv=np.ascontiguousarray(state_conv[0, sq].reshape(4, 2, NF, 128).transpose(3, 2, 0, 1)),
                 mkTs=np.ascontiguousarray(cmk[0, sq].transpose(0, 2, 3, 1)), mvs=np.ascontiguousarray(cmv[0, sq].reshape(4, 256, 512)),
                 csm=csm)
        in_maps.append(m)
    return in_maps


def kernel(**inp):
    n_phys = int(np.asarray(inp["cache_moba_k"]).shape[1])
    if n_phys not in _NC_CACHE:
        _NC_CACHE[n_phys] = build_nc(n_phys=n_phys)
    nc = _NC_CACHE[n_phys]
    in_maps = make_in_maps(inp)
    res = run_bass_kernel_spmd(nc, in_maps, core_ids=list(range(8))).results
    B = 2
    y_prompt = np.zeros((B, SEQ, 1024), np.float32)
    nk = np.zeros((1, B, SEQ, 8, 64), np.float32)
    nv = np.zeros((1, B, SEQ, 8, 64), np.float32)
    gp = np.zeros((1, B, 4, 64, 128), np.float32)
    cp = np.zeros((1, B, 2, DFF), np.float32)
    mkp = np.zeros((1, B, 256, 4, 128), np.float32)
    mvp = np.zeros((1, B, 256, 4, 128), np.float32)
    for c in range(8):
        b, r = c // 4, c % 4
        o = res[c]
        sl = slice(r * NOWN, (r + 1) * NOWN)
        y_prompt[b, sl] = o["yT"].T
        nk[0, b, sl] = o["kT_o"].T.reshape(NOWN, 8, 64)
        nv[0, b, sl] = o["v_o"].reshape(NOWN, 8, 64)
        if r == 3:
            g = o["gla_o"]
            gp[0, b] = g.reshape(2, 64, 2, 128).transpose(2, 0, 1, 3).reshape(4, 64, 128)
            cp[0, b] = o["conv_o"].transpose(2, 1, 0).reshape(2, DFF)
        if r == 0:
            mkp[0, b] = o["mk_o"].reshape(256, 4, 128)
            mvp[0, b] = o["mv_o"].reshape(256, 4, 128)
    DB = 32
    y_sample = np.zeros((DB, 8, 1024), np.float32)
    nks = np.zeros((1, DB, 8, 8, 64), np.float32)
    nvs = np.zeros((1, DB, 8, 8, 64), np.float32)
    gs = np.zeros((1, DB, 4, 64, 128), np.float32)
    cs = np.zeros((1, DB, 2, DFF), np.float32)
    for c in range(8):
        o = res[c]
        sq = slice(4 * c, 4 * c + 4)
        y_sample[sq] = o["yT_s"].T.reshape(4, 8, 1024)
        nks[0, sq] = o["kT_s"].T.reshape(4, 8, 8, 64)
        nvs[0, sq] = o["v_s"].reshape(4, 8, 8, 64)
        gs[0, sq] = o["gla_s"].reshape(4, 2, 64, 2, 128).transpose(0, 3, 1, 2, 4).reshape(4, 4, 64, 128)
        cs[0, sq] = o["conv_s"].transpose(2, 3, 1, 0).reshape(4, 2, DFF)
    return (y_prompt, y_sample, nk, nv, nks, nvs, gp, gs, cp, cs, mkp, mvp)
```

```python
import contextlib
import numpy as np
import concourse.bass as bass
import concourse.mybir as mybir
from concourse.bass_utils import run_bass_kernel_spmd

F32 = mybir.dt.float32
BF16 = mybir.dt.bfloat16
I32 = mybir.dt.int32
AF = mybir.ActivationFunctionType
ALU = mybir.AluOpType
AX = mybir.AxisListType
ENG = ("pe", "act", "dve", "pool", "sp")
NEG = -30000.0


class _Buf:
    __slots__ = ("lw", "rd")

    def __init__(self):
        self.lw = None
        self.rd = []


class _Op:
    __slots__ = ("eng", "fn", "deps", "dma", "sig", "sem", "val", "idx", "guard")

    def __init__(self, eng, fn, dma):
        self.eng, self.fn, self.dma = eng, fn, dma
        self.deps = set()
        self.sig = False
        self.sem = None
        self.val = 0
        self.guard = None


class Sched:
    NDMA = 6

    def __init__(self):
        self.ops = []
        self.bufs = {}
        self.bar = None

    def barrier(self, fn):
        o = _Op("dve", fn, False)
        o.idx = len(self.ops)
        for b in self.bufs.values():
            if b.lw is not None:
                o.deps.add(b.lw)
            o.deps.update(b.rd)
        self.ops.append(o)
        self.bar = o.idx
        self.bufs = {}

    limit = 10 ** 9

    def _last_per_engine(self, idxs):
        last = {}
        out = []
        for i in idxs:
            o = self.ops[i]
            if o.dma:
                out.append(i)
            elif last.get(o.eng, -1) < i:
                last[o.eng] = i
        out.extend(last.values())
        return out

    def op(self, eng, fn, reads=(), writes=(), dma=False):
        if len(self.ops) >= self.limit:
            return None
        o = _Op(eng, fn, dma)
        o.idx = len(self.ops)
        if self.bar is not None:
            o.deps.add(self.bar)
        bufs = self.bufs
        for r in reads:
            b = bufs.get(r)
            if b is None:
                b = bufs[r] = _Buf()
            if b.lw is not None:
                o.deps.add(b.lw)
            if r.startswith("ps"):
                o.deps.update(self._last_per_engine(b.rd))
        for w in writes:
            b = bufs.get(w)
            if b is None:
                b = bufs[w] = _Buf()
            if b.lw is not None:
                o.deps.add(b.lw)
            o.deps.update(self._last_per_engine(b.rd))
        for r in reads:
            bufs[r].rd.append(o.idx)
        for w in writes:
            b = bufs[w]
            b.lw = o.idx
            b.rd = []
        o.deps.discard(o.idx)
        self.ops.append(o)
        return o

    def emit(self, nc):
        ops = self.ops
        for o in ops:
            for d in o.deps:
                p = ops[d]
                if p.eng == "pe" and o.eng == "pe" and not p.dma and not o.dma:
                    continue
                p.sig = True
        alld = [o for o in ops if o.dma]
        for o in alld:
            o.sig = True
        with contextlib.ExitStack() as st:
            csem = {e: st.enter_context(nc.semaphore("c_" + e)) for e in ENG}
            dsem = {e: [st.enter_context(nc.semaphore("d_%s%d" % (e, i))) for i in range(self.NDMA)]
                    for e in ("sp", "pool")}
            ccount = {e: 0 for e in ENG}
            dcount = {e: 0 for e in ENG}
            lastd = {}
            for o in ops:
                if not o.sig:
                    continue
                if o.dma:
                    i = dcount[o.eng]
                    dcount[o.eng] += 1
                    o.sem = dsem[o.eng][i % self.NDMA]
                    o.val = 16 * (i // self.NDMA + 1)
                    o.guard = lastd.get(id(o.sem))
                    lastd[id(o.sem)] = o
                else:
                    ccount[o.eng] += 1
                    o.sem = csem[o.eng]
                    o.val = ccount[o.eng]
            block = st.enter_context(nc.Block())
            per = {e: [o for o in ops if o.eng == e] for e in ENG}

            def run(e, engobj):
                waited = {}
                for o in per[e]:
                    need = {}
                    for d in o.deps:
                        p = ops[d]
                        if not p.sig:
                            continue
                        k = id(p.sem)
                        if need.get(k, (None, 0))[1] < p.val:
                            need[k] = (p.sem, p.val)
                    if o.guard is not None:
                        g = o.guard
                        k = id(g.sem)
                        if need.get(k, (None, 0))[1] < g.val:
                            need[k] = (g.sem, g.val)
                    for k, (s, v) in need.items():
                        if waited.get(k, 0) < v:
                            engobj.wait_ge(s, v)
                            waited[k] = v
                    ins = o.fn(engobj)
                    if o.sig:
                        ins.then_inc(o.sem, 16 if o.dma else 1)
                if e == "sp":
                    fin = {}
                    for o in alld:
                        k = id(o.sem)
                        if fin.get(k, (None, 0))[1] < o.val:
                            fin[k] = (o.sem, o.val)
                    for k, (s, v) in fin.items():
                        if waited.get(k, 0) < v:
                            engobj.wait_ge(s, v)
                    for ee in ENG:
                        if ccount[ee] > 0:
                            engobj.wait_ge(csem[ee], ccount[ee])

            @block.sync
            def _(eng):
                run("sp", eng)

            @block.scalar
            def _(eng):
                run("act", eng)

            @block.vector
            def _(eng):
                run("dve", eng)

            @block.gpsimd
            def _(eng):
                run("pool", eng)

            @block.tensor
            def _(eng):
                run("pe", eng)


SEQ = 8192
NOWN = 2048
HALO = 256
GROUPS = [(i * 256, 256) for i in range(9)]
NCOL = HALO + NOWN
DFF = 2816
NF = 22
INC = 3600


def build_nc(phases=("full", "moba", "gla", "cross", "tail", "smp"), ngroups=9, nfull=32, debug=False, n_phys=5120):
    nc = bass.Bass("TRN2", target_bir_lowering=False)

    def din(name, shape, dt=F32):
        return nc.dram_tensor(name, list(shape), dt, kind="ExternalInput").ap()

    def dout(name, shape, dt=F32):
        return nc.dram_tensor(name, list(shape), dt, kind="ExternalOutput").ap()

    xT_full = din("xT_full", [1024, SEQ])
    xT_own = din("xT_own", [1024, NCOL])
    memT = din("memT", [1024, 256])
    w_in = din("w_in", [1024, INC])
    w_a2 = din("w_a2", [16, 256])
    b_a = din("b_a", [1, 256])
    w_mem = din("w_mem", [1024, 1024])
    w_brm = din("w_brm", [512, 1024])
    w_brg = din("w_brg", [512, 1024])
    w_brc = din("w_brc", [512, 1024])
    w_gate = din("w_gate", [1024, 3072])
    w_out = din("w_out", [1024, 1024])
    w_up = din("w_up", [1024, 2 * DFF])
    w_down = din("w_down", [DFF, 1024])
    vecs = din("vecs", [128, 8 * 4 + 24 + 4 + NF * 4])
    blkvalid = din("blkvalid", [128, 9, 32])
    selr = din("selr", [128, 4])
    consts = din("consts", [128, 128 * 6])
    c32 = din("c32", [128, 516])
    ohrows = din("ohrows", [32, SEQ])
    xT_smp = din("xT_smp", [1024, 32])
    ptab = din("ptab", [4, 128], I32)
    poolKT = din("poolKT", [n_phys * 128, 512])
    poolV = din("poolV", [n_phys * 128, 512])
    sgla = din("sgla", [4, 128, 2, 128])
    sconv = din("sconv", [128, NF, 4, 2])
    mkTs = din("mkTs", [4, 4, 128, 256])
    mvs = din("mvs", [4, 256, 512])
    csm_d = din("csm", [128, 192])
    yT_s = dout("yT_s", [1024, 32])
    kT_s = dout("kT_s", [512, 32])
    v_s = dout("v_s", [32, 512])
    gla_s = dout("gla_s", [4, 128, 2, 128])
    conv_s = dout("conv_s", [128, NF, 4, 2])
    Ks = nc.dram_tensor("Ks", [512, SEQ], BF16, kind="Internal").ap()
    wsrc = dict(w_mem=(w_mem, [1024, 1024]), w_in=(w_in, [1024, INC]), w_brm=(w_brm, [512, 1024]), w_brg=(w_brg, [512, 1024]),
                w_brc=(w_brc, [512, 1024]), w_gate=(w_gate, [1024, 3072]), w_out=(w_out, [1024, 1024]), w_up=(w_up, [1024, 2 * DFF]),
                w_down=(w_down, [DFF, 1024]))
    wbf = {k: nc.dram_tensor(k + "_bf", shp, BF16, kind="Internal").ap() for k, (_, shp) in wsrc.items()}
    Vs = nc.dram_tensor("Vs", [SEQ, 512], BF16, kind="Internal").ap()

    yT = dout("yT", [1024, NOWN])
    kT_o = dout("kT_o", [512, NOWN])
    v_o = dout("v_o", [NOWN, 512])
    gla_o = dout("gla_o", [128, 2, 128])
    conv_o = dout("conv_o", [128, NF, 2])
    mk_o = dout("mk_o", [256, 512])
    mv_o = dout("mv_o", [256, 512])

    dbgo = {}
    if debug:
        for nm in ("om", "og", "oc", "mrg"):
            dbgo[nm] = nc.dram_tensor("d_" + nm, [128, 8, 256], BF16, kind="ExternalOutput").ap()
        for nm in ("h1", "h2", "qta", "xn"):
            dbgo[nm] = nc.dram_tensor("d_" + nm, [128, 8, 256], F32 if nm.startswith("h") else BF16, kind="ExternalOutput").ap()
    S = Sched()
    st = contextlib.ExitStack()
    with st:
        def sb(name, shape, dt=F32):
            return st.enter_context(nc.sbuf_tensor(name, list(shape), dt))

        psf = [st.enter_context(nc.psum_tensor("psf%d" % i, [128, 512], F32)) for i in range(6)]
        psb = [st.enter_context(nc.psum_tensor("psb%d" % i, [128, 1024], BF16)) for i in range(2)]
        pctr = [0, 0]

        def PS():
            i = pctr[0] % 4
            pctr[0] += 1
            return psf[i], "psf%d" % i

        def PSA(i):
            return psf[4 + i], "psf%d" % (4 + i)

        def PSB():
            i = pctr[1] % 2
            pctr[1] += 1
            return psb[i], "psb%d" % i

        def mm(out, lhsT, rhs, start, stop, R, W):
            S.op("pe", lambda e: e.matmul(out, lhsT=lhsT, rhs=rhs, start=start, stop=stop), reads=R, writes=W)

        def act(out, in_, func, R, W, **kw):
            S.op("act", lambda e: e.activation(out=out, in_=in_, func=func, **kw), reads=R, writes=W)

        def dve(fn, R, W):
            S.op("dve", fn, reads=R, writes=W)

        def pool(fn, R, W):
            S.op("pool", fn, reads=R, writes=W)

        def dma(out, in_, R, W, q="sp", **kw):
            S.op(q, lambda e: e.dma_start(out=out, in_=in_, **kw), reads=R, writes=W, dma=True)

        for k_, (src_, shp_) in wsrc.items():
            rows = shp_[0]
            for r0 in range(0, rows, 256):
                r1 = min(rows, r0 + 256)
                dma(wbf[k_][r0:r1, :], src_[r0:r1, :], [], ["wscr_" + k_ + "_bf"], q="pool")
        w_in, w_brm, w_brg, w_brc, w_gate, w_out, w_up, w_down, w_mem = (wbf[k_] for k_ in (
            "w_in", "w_brm", "w_brg", "w_brc", "w_gate", "w_out", "w_up", "w_down", "w_mem"))
        cb = sb("cb", [128, 6, 128], BF16)
        c32t = sb("c32t", [128, 516], F32)
        vt = sb("vt", [128, 8 * 4 + 24 + 4 + NF * 4], F32)
        bv = sb("bv", [128, 9, 32], F32)
        selt = sb("selt", [128, 4], F32)
        ones = sb("ones", [128, 128], BF16)
        dma(cb[:], consts.rearrange("p (a b) -> p a b", a=6), [], ["cb"], q="pool")
        dma(c32t[:], c32, [], ["c32t"])
        dma(vt[:], vecs, [], ["vt"])
        dma(bv[:], blkvalid, [], ["bv"])
        dma(selt[:], selr, [], ["selt"])
        dve(lambda e: e.memset(ones[:], 1.0), [], ["ones"])
        keep = sb("keep", [128, 1], F32)
        dve(lambda e: e.tensor_scalar(out=keep[:], in0=selt[:, 0:1], scalar1=-1.0, scalar2=1.0, op0=ALU.mult, op1=ALU.add), ["selt"], ["keep"])
        ident = cb[:, 0, :]
        tri01 = cb[:, 1, :]
        blk2 = cb[:, 2, :]
        triU32 = c32t[:, 0:128]
        triI32 = c32t[:, 128:256]
        chk32 = c32t[:, 256:258]
        g_mix = lambda k: vt[:, k:k + 1]
        g_ffn = lambda k: vt[:, 8 + k:9 + k]
        g_fin = lambda k: vt[:, 16 + k:17 + k]
        g_mem = lambda k: vt[:, 24 + k:25 + k]
        b_gate = lambda j: vt[:, 32 + j:33 + j]
        g_gla = lambda h: vt[:, 56 + h:57 + h]
        wc = lambda i, f: vt[:, 60 + i * NF + f:61 + i * NF + f]

        xTt0_ = sb("xTt0", [128, 8, 256], F32)
        xTt = [xTt0_, xTt0_]
        sqt = sb("sqt", [128, 8, 256], BF16)
        xn = sb("xn", [128, 8, 256], BF16)
        rstd = sb("rstd", [128, 256], F32)
        wt = [sb("wt%d" % i, [128, 8, 512], BF16) for i in range(2)]
        wctr = [0]

        def norm(src, srcname, N, gfn, dst, dstname):
            act(sqt[:, :, :N], src[:, :, :N], AF.Square, [srcname], ["sqt"])
            p, pn = PS()
            for k in range(8):
                mm(p[:, :N], ones[:], sqt[:, k, :N], k == 0, k == 7, ["ones", "sqt"], [pn])
            act(rstd[:, :N], p[:, :N], AF.Sqrt, [pn], ["rstd"], scale=1.0 / 1024, bias=1e-6)
            dve(lambda e: e.reciprocal(out=rstd[:, :N], in_=rstd[:, :N]), ["rstd"], ["rstd"])
            for k in range(8):
                dve(lambda e, k=k: e.scalar_tensor_tensor(out=dst[:, k, :N], in0=src[:, k, :N], scalar=gfn(k), in1=rstd[:, :N],
                                                          op0=ALU.mult, op1=ALU.mult),
                    [srcname, "vt", "rstd"], [dstname])

        def loadw(src_ap, ncols, krows=8):
            i = wctr[0] % 2
            wctr[0] += 1
            t = wt[i]
            dma(t[:, :krows, :ncols], src_ap.rearrange("(k p) c -> p k c", p=128), ["wscr_" + src_ap.tensor.name], ["wt%d" % i])
            return t, "wt%d" % i

        mkT = sb("mkT", [128, 4, 256], BF16)
        mvb = sb("mvb", [128, 2, 512], BF16)
        stg = sb("stg", [128, 512], F32)
        dma(xTt[0][:, :, :256], memT.rearrange("(k p) t -> p k t", p=128), [], ["xTt0"])
        norm(xTt[0], "xTt0", 256, g_mem, xn, "xn")
        for half, (dst_o,) in enumerate([(mk_o,), (mv_o,)]):
            w, wn = loadw(w_mem[:, half * 512:(half + 1) * 512], 512)
            for tt in range(2):
                p, pn = PS()
                for k in range(8):
                    mm(p[:], xn[:, k, tt * 128:(tt + 1) * 128], w[:, k, :], k == 0, k == 7, ["xn", wn], [pn])
                act(stg[:], p[:], AF.Copy, [pn], ["stg"])
                if half == 1:
                    dve(lambda e, p=p, tt=tt: e.tensor_copy(out=mvb[:, tt, :], in_=p[:]), [pn], ["mvb", pn])
                dma(dst_o[tt * 128:(tt + 1) * 128, :], stg[:], ["stg"], [])
            if half == 0:
                for h in range(4):
                    p, pn = PS()
                    for k in range(8):
                        mm(p[:, :256], w[:, k, h * 128:(h + 1) * 128], xn[:, k, :256], k == 0, k == 7, ["xn", wn], [pn])
                    dve(lambda e, p=p, h=h: e.tensor_copy(out=mkT[:, h, :], in_=p[:, :256]), [pn], ["mkT"])

        wf = sb("wf", [128, 8, 1808], BF16)
        dma(wf[:, :, 0:1024], w_in[:, 512:1536].rearrange("(k p) c -> p k c", p=128), ["wscr_w_in_bf"], ["wf"])
        dma(wf[:, :, 1024:1280], w_in[:, 1792:2048].rearrange("(k p) c -> p k c", p=128), ["wscr_w_in_bf"], ["wf"])
        dma(wf[:, :, 1280:1808], w_in[:, 2048:2576].rearrange("(k p) c -> p k c", p=128), ["wscr_w_in_bf"], ["wf"])
        dma(wf[:, :, 1792:1808], w_in[:, 3072:3088].rearrange("(k p) c -> p k c", p=128), ["wf", "wscr_w_in_bf"], ["wf"])
        wa2 = sb("wa2", [16, 256], BF16)
        ba = sb("ba", [1, 256], BF16)
        dma(wa2[:], w_a2, [], ["wa2"], q="pool")
        dma(ba[:], b_a, [], ["ba"], q="pool")
        kmT = sb("kmT", [128, 4, 32], F32)
        kst = sb("kst", [128, 4, 256], BF16)
        vst = sb("vst", [128, 2, 512], BF16)
        Sst = sb("Sst", [128, 2, 128], F32)
        Ssave = sb("Ssave", [128, 3, 2, 128], F32)
        agT = sb("agT", [16, 512], BF16)
        nl = sb("nl", [128, 256], F32)
        eR = sb("eR", [128, 256], F32)
        ktil = sb("ktil", [128, 256], BF16)
        vgb = sb("vgb", [128, 512], BF16)
        dch = sb("dch", [128, 2, 2], F32)
        dve(lambda e: e.memset(Sst[:], 0.0), [], ["Sst"])

        def gla_prep(xnt, xnname, col0, wk_ap, wv_ap, wa_ap, wname):
            p, pn = PS()
            for k in range(8):
                mm(p[:16, :128], wa_ap(k), xnt[:, k, col0:col0 + 128], k == 0, k == 7, [xnname, wname], [pn])
            dve(lambda e, p=p: e.tensor_copy(out=agT[:, :128], in_=p[:16, :128]), [pn], ["agT"])
            p2, pn2 = PS()
            mm(p2[:, :256], agT[:, :128], wa2[:], True, False, ["agT", "wa2"], [pn2])
            mm(p2[:, :256], ones[0:1, :], ba[:], False, True, ["ones", "ba"], [pn2])
            act(nl[:], p2[:, :256], AF.Exp, [pn2], ["nl"], scale=-1.0)
            act(nl[:], nl[:], AF.Ln, ["nl"], ["nl"], bias=1.0)
            p3, pn3 = PS()
            mm(p3[:, :256], triU32, nl[:], True, True, ["c32t", "nl"], [pn3])
            act(eR[:], p3[:, :256], AF.Exp, [pn3], ["eR"])
            pk, pkn = PS()
            for k in range(8):
                mm(pk[:, :256], xnt[:, k, col0:col0 + 128], wk_ap(k), k == 0, k == 7, [xnname, wname], [pkn])
            dve(lambda e, pk=pk: e.tensor_tensor(out=ktil[:], in0=pk[:, :256], in1=eR[:], op=ALU.mult), [pkn, "eR"], ["ktil"])
            pv, pvn = PS()
            for k in range(8):
                mm(pv[:], xnt[:, k, col0:col0 + 128], wv_ap(k), k == 0, k == 7, [xnname, wname], [pvn])
            act(vgb[:], pv[:], AF.Copy, [pvn], ["vgb"])
            pd, pdn = PS()
            for half in range(2):
                mm(pd[:, half * 2:half * 2 + 2], nl[:, half * 128:(half + 1) * 128], chk32, True, True, ["nl", "c32t"], [pdn])
            act(dch[:], pd[:, 0:4].rearrange("p (a b) -> p a b", a=2), AF.Exp, [pdn], ["dch"])

        def gla_state_update(ci):
            for hp in range(2):
                pu, pun = PS()
                mm(pu[:, :256], ktil[ci * 64:(ci + 1) * 64, hp * 128:(hp + 1) * 128], vgb[ci * 64:(ci + 1) * 64, hp * 256:(hp + 1) * 256],
                   True, True, ["ktil", "vgb"], [pun])
                for hh in range(2):
                    rs = slice(hh * 64, (hh + 1) * 64)
                    dve(lambda e, pu=pu, hp=hp, hh=hh, rs=rs: e.scalar_tensor_tensor(
                        out=Sst[rs, hp, :], in0=Sst[rs, hp, :], scalar=dch[rs, hp, ci:ci + 1], in1=pu[rs, hh * 128:(hh + 1) * 128],
                        op0=ALU.mult, op1=ALU.add), [pun, "dch", "Sst"], ["Sst"])

        for g in range(nfull if "full" in phases else 0):
            xt = xTt[0]
            xname = "xTt0"
            dma(xt[:], xT_full[:, g * 256:(g + 1) * 256].rearrange("(k p) t -> p k t", p=128), [], [xname])
            norm(xt, xname, 256, g_mix, xn, "xn")
            for c in range(4):
                p, pn = PS()
                for k in range(8):
                    mm(p[:, :256], wf[:, k, c * 128:(c + 1) * 128], xn[:, k, :], k == 0, k == 7, ["xn", "wf"], [pn])
                act(kst[:, c, :], p[:, :256], AF.Copy, [pn], ["kst"])
                dve(lambda e, p=p, c=c, g=g: e.reduce_sum(out=kmT[:, c, g:g + 1], in_=p[:, :256], axis=AX.X), [pn], ["kmT"])
            dma(Ks.rearrange("(c p) t -> p c t", p=128)[:, :, g * 256:(g + 1) * 256], kst[:], ["kst"], ["Ks"])
            for tt in range(2):
                p, pn = PS()
                for k in range(8):
                    mm(p[:], xn[:, k, tt * 128:(tt + 1) * 128], wf[:, k, 512:1024], k == 0, k == 7, ["xn", "wf"], [pn])
                act(vst[:, tt, :], p[:], AF.Copy, [pn], ["vst"])
            dma(Vs[g * 256:(g + 1) * 256, :].rearrange("(t p) c -> p t c", p=128), vst[:], ["vst"], ["Vs"])
            for tt in range(2):
                gla_prep(xn, "xn", tt * 128, lambda k: wf[:, k, 1024:1280], lambda k: wf[:, k, 1280:1792],
                         lambda k: wf[:, k, 1792:1808], "wf")
                for ci in range(2):
                    chunk = g * 4 + tt * 2 + ci
                    for i, cb_ in enumerate((28, 60, 92)):
                        if chunk == cb_:
                            dve(lambda e, i=i: e.tensor_copy(out=Ssave[:, i, :, :], in_=Sst[:]), ["Sst"], ["Ssave"])
                    gla_state_update(ci)
        dma(gla_o, Sst[:], ["Sst"], [])
        kmbd = sb("kmbd", [128, 4, 64], BF16)
        dve(lambda e: e.memset(kmbd[:], 0.0), [], ["kmbd"])
        dve(lambda e: e.tensor_copy(out=kmbd[0:64, :, 0:32], in_=kmT[0:64, :, :]), ["kmT", "kmbd"], ["kmbd"])
        dve(lambda e: e.tensor_copy(out=kmbd[64:128, :, 32:64], in_=kmT[64:128, :, :]), ["kmT", "kmbd"], ["kmbd"])

        Srun = sb("Srun", [128, 2, 128], F32)
        dve(lambda e: e.tensor_scalar(out=Srun[:], in0=Ssave[:, 0, :, :], scalar1=selt[:, 1:2], scalar2=None, op0=ALU.mult), ["Ssave", "selt"], ["Srun"])
        for i in (1, 2):
            dve(lambda e, i=i: e.scalar_tensor_tensor(out=Srun[:], in0=Ssave[:, i, :, :], scalar=selt[:, i + 1:i + 2], in1=Srun[:],
                                                      op0=ALU.mult, op1=ALU.add), ["Ssave", "selt", "Srun"], ["Srun"])

        xo = xTt[0]
        hT = sb("hT", [128, 8, 256], F32)
        qT = sb("qT", [128, 4, 256], BF16)
        Bq = sb("Bq", [128, 2, 8, 96], BF16)
        qTa = sb("qTa", [96, 8, 256], BF16)
        kla = sb("kla", [96, 8, 256], BF16)
        vloc = sb("vloc", [128, 2, 512], BF16)
        gsel = sb("gsel", [128, 8, 32], F32)
        m8 = sb("m8", [128, 8, 8], F32)
        kaug0 = sb("kaug0", [96, SEQ], BF16)
        kaug = [kaug0, kaug0]
        vaug0 = sb("vaug0", [128, 64, 128], BF16)
        pT = [sb("pT%d" % i, [128, 512], BF16) for i in range(2)]
        rden = sb("rden", [128, 256], F32)
        omT = sb("omT", [128, 4, 256], BF16)
        ogT = sb("ogT", [128, 4, 256], BF16)
        ocT = sb("ocT", [128, 4, 256], BF16)
        mrg = sb("mrg", [128, 8, 256], BF16)
        hn = mrg
        sg = sb("sg", [128, 3, 256], F32)
        tmpf = sb("tmpf", [128, 256], F32)
        uext2 = sb("uext2", [128, 258], F32)
        tmpf2 = uext2
        aT = sb("aT", [128, NF, 256], BF16)
        uext = sb("uext", [128, 258], F32)
        uprev = sb("uprev", [128, NF, 2], F32)
        yst = xTt[0]
        kstg = hT
        qgT = sb("qgT", [128, 2, 256], F32)
        kgT = sb("kgT", [128, 2, 256], F32)
        qtl = sb("qtl", [128, 4, 256], BF16)
        ktl = sb("ktl", [128, 2, 256], BF16)
        egp = sb("egp", [128, 128], F32)
        egn = sb("egn", [128, 128], F32)
        rsil = sb("rsil", [128, 4, 256], F32)
        ogf = sb("ogf", [128, 4, 256], F32)
        qcT = sb("qcT", [128, 4, 256], BF16)

        dma(kaug0[64:96, :], ohrows, [], ["kaug0"], q="pool")
        dve(lambda e: e.memset(kla[:], 0.0), [], ["kla"])
        dve(lambda e: e.memset(uprev[:], 0.0), [], ["uprev"])
        dve(lambda e: e.memset(Bq[:], 0.0), [], ["Bq"])
        dve(lambda e: e.memset(qtl[:], 0.0), [], ["qtl"])
        selmat = [c32t[:, 260 + i * 128:260 + (i + 1) * 128] for i in range(2)]
        triB = cb[:, 3, :]

        def st_update(ci, St, Sn):
            for hp in range(2):
                pu, pun = PS()
                mm(pu[:, :256], ktil[ci * 64:(ci + 1) * 64, hp * 128:(hp + 1) * 128], vgb[ci * 64:(ci + 1) * 64, hp * 256:(hp + 1) * 256],
                   True, True, ["ktil", "vgb"], [pun])
                for hh in range(2):
                    rs = slice(hh * 64, (hh + 1) * 64)
                    dve(lambda e, pu=pu, hp=hp, hh=hh, rs=rs: e.scalar_tensor_tensor(
                        out=St[rs, hp, :], in0=St[rs, hp, :], scalar=dch[rs, hp, ci:ci + 1], in1=pu[rs, hh * 128:(hh + 1) * 128],
                        op0=ALU.mult, op1=ALU.add), [pun, "dch", Sn], [Sn])

        Sbf = [sb("Sbf%d" % i, [128, 2, 128], BF16) for i in range(2)]
        ATf = sb("ATf", [128, 128], BF16)

        def gla_own(gi, c0, N):
            ntile = N // 128
            w, wn = loadw(w_in[:, 1536:2048], 512)
            for j, dst, dn in ((0, qgT, "qgT"), (1, kgT, "kgT")):
                for c in range(2):
                    p, pn = PS()
                    for k in range(8):
                        mm(p[:, :N], w[:, k, j * 256 + c * 128:j * 256 + (c + 1) * 128], xn[:, k, :N], k == 0, k == 7, ["xn", wn], [pn])
                    act(dst[:, c, :N], p[:, :N], AF.Copy, [pn], [dn])
            w, wn = loadw(w_in[:, 2560:3072], 512)
            for c in range(4):
                p, pn = PS()
                for k in range(8):
                    mm(p[:, :N], w[:, k, c * 128:(c + 1) * 128], xn[:, k, :N], k == 0, k == 7, ["xn", wn], [pn])
                act(rsil[:, c, :N], p[:, :N], AF.Silu, [pn], ["rsil"])
            for tt in range(ntile):
                ts_ = slice(tt * 128, (tt + 1) * 128)
                gla_prep(xn, "xn", tt * 128, lambda k: wf[:, k, 1024:1280], lambda k: wf[:, k, 1280:1792],
                         lambda k: wf[:, k, 1792:1808], "wf")
                for hp in range(2):
                    pgm, pgn = PS()
                    mm(pgm[:, :128], nl[:, hp * 128:(hp + 1) * 128], triI32, True, True, ["nl", "c32t"], [pgn])
                    act(egp[:], pgm[:, :128], AF.Exp, [pgn], ["egp"])
                    act(egn[:], pgm[:, :128], AF.Exp, [pgn], ["egn"], scale=-1.0)
                    for par in range(2):
                        prs = slice(par * 64, par * 64 + 64)
                        dve(lambda e, hp=hp, ts_=ts_, prs=prs, par=par: e.scalar_tensor_tensor(
                            out=qtl[prs, hp * 2 + par, ts_], in0=qgT[prs, hp, ts_], scalar=0.125, in1=egp[prs, :],
                            op0=ALU.mult, op1=ALU.mult), ["qgT", "egp"], ["qtl"])
                    dve(lambda e, hp=hp, ts_=ts_: e.tensor_tensor(out=ktl[:, hp, ts_], in0=kgT[:, hp, ts_], in1=egn[:], op=ALU.mult),
                        ["kgT", "egn"], ["ktl"])
                dve(lambda e: e.tensor_copy(out=Sbf[0][:], in_=Srun[:]), ["Srun"], ["Sbf0"])
                st_update(0, Srun, "Srun")
                dve(lambda e: e.tensor_copy(out=Sbf[1][:], in_=Srun[:]), ["Srun"], ["Sbf1"])
                st_update(1, Srun, "Srun")
                for h in range(4):
                    rs = slice((h % 2) * 64, (h % 2) * 64 + 64)
                    hp = h // 2
                    pa, pan = PS()
                    mm(pa[:, :128], ktl[:, hp, ts_], qtl[:, h, ts_], True, True, ["ktl", "qtl"], [pan])
                    dve(lambda e, pa=pa: e.tensor_tensor(out=ATf[:], in0=pa[:, :128], in1=triB, op=ALU.mult), [pan, "cb"], ["ATf"])
                    po, pon = PSA(0)
                    mm(po[:, :128], vgb[:, h * 128:(h + 1) * 128], ATf[:], True, False, ["vgb", "ATf"], [pon])
                    for ci in range(2):
                        mm(po[:, ci * 64:(ci + 1) * 64], Sbf[ci][:, hp, :], qtl[:, h, tt * 128 + ci * 64:tt * 128 + (ci + 1) * 64],
                           False, ci == 1, ["Sbf%d" % ci, "qtl"], [pon])
                    act(ogf[:, h, ts_], po[:, :128], AF.Copy, [pon], ["ogf"])
            gla_norm(N)

        def gla_norm(N):
            for h in range(4):
                act(sqt[:, h, :N], ogf[:, h, :N], AF.Square, ["ogf"], ["sqt"])
                p, pn = PS()
                mm(p[:, :N], ones[:], sqt[:, h, :N], True, True, ["ones", "sqt"], [pn])
                act(rstd[:, :N], p[:, :N], AF.Sqrt, [pn], ["rstd"], scale=1.0 / 128, bias=1e-6)
                dve(lambda e: e.reciprocal(out=rstd[:, :N], in_=rstd[:, :N]), ["rstd"], ["rstd"])
                dve(lambda e, h=h: e.scalar_tensor_tensor(out=ogf[:, h, :N], in0=ogf[:, h, :N], scalar=g_gla(h), in1=rstd[:, :N],
                                                          op0=ALU.mult, op1=ALU.mult), ["ogf", "vt", "rstd"], ["ogf"])
                dve(lambda e, h=h: e.tensor_tensor(out=ogT[:, h, :N], in0=ogf[:, h, :N], in1=rsil[:, h, :N], op=ALU.mult), ["ogf", "rsil"], ["ogT"])

        def gla_proj(N):
            w, wn = loadw(w_in[:, 1536:2048], 512)
            for j, dst, dn in ((0, qgT, "qgT"), (1, kgT, "kgT")):
                for c in range(2):
                    p, pn = PS()
                    for k in range(8):
                        mm(p[:, :N], w[:, k, j * 256 + c * 128:j * 256 + (c + 1) * 128], xn[:, k, :N], k == 0, k == 7, ["xn", wn], [pn])
                    act(dst[:, c, :N], p[:, :N], AF.Copy, [pn], [dn])
            w, wn = loadw(w_in[:, 2560:3072], 512)
            for c in range(4):
                p, pn = PS()
                for k in range(8):
                    mm(p[:, :N], w[:, k, c * 128:(c + 1) * 128], xn[:, k, :N], k == 0, k == 7, ["xn", wn], [pn])
                act(rsil[:, c, :N], p[:, :N], AF.Silu, [pn], ["rsil"])

        def cross_q(N):
            w, wn = loadw(w_in[:, 3088:3600], 512)
            for h in range(4):
                p, pn = PS()
                for k in range(8):
                    mm(p[:, :N], w[:, k, h * 128:(h + 1) * 128], xn[:, k, :N], k == 0, k == 7, ["xn", wn], [pn])
                act(qcT[:, h, :N], p[:, :N], AF.Copy, [pn], ["qcT"])

        def cross_att(c0_, N):
            cs = slice(c0_, c0_ + N)
            for h in range(4):
                po, pon = PSA(0)
                pd, pdn = PSA(1)
                for mt in range(2):
                    ps_, psn = PS()
                    mm(ps_[:, :N], mkT[:, h, mt * 128:(mt + 1) * 128], qcT[:, h, cs], True, True, ["mkT", "qcT"], [psn])
                    ptile = pT[mt]
                    ptn = "pT%d" % mt
                    act(ptile[:, :N], ps_[:, :N], AF.Exp, [psn], [ptn], scale=128 ** -0.5)
                    mm(po[:, :N], mvb[:, mt, h * 128:(h + 1) * 128], ptile[:, :N], mt == 0, mt == 1, ["mvb", ptn], [pon])
                    mm(pd[:, :N], ones[:], ptile[:, :N], mt == 0, mt == 1, ["ones", ptn], [pdn])
                dve(lambda e, pd=pd: e.reciprocal(out=rden[:, :N], in_=pd[:, :N]), [pdn], ["rden"])
                dve(lambda e, po=po, h=h: e.tensor_tensor(out=ocT[:, h, cs], in0=po[:, :N], in1=rden[:, :N], op=ALU.mult), [pon, "rden"], ["ocT"])

        def cross_own(gi, c0, N):
            cross_q(N)
            cross_att(0, N)

        def tail_own(gi, c0, N, smp=False):
            oc0 = c0 - HALO
            brs = ((w_brm, omT, "omT"), (w_brg, ogT, "ogT"), (w_brc, ocT, "ocT"))
            wbr = []
            for (wd, _, _) in brs:
                wbr.append(None)
            for half in range(8):
                wg_t = []
                for i in range(3):
                    wg_t.append(loadw3[i](w_gate[:, i * 1024 + half * 128:i * 1024 + (half + 1) * 128], 128))
                wb_t = []
                for i, (wd, _, _) in enumerate(brs):
                    wb_t.append(loadw3b[i](wd[:, half * 128:(half + 1) * 128], 128, 4))
                for c4 in range(1):
                    c8 = half
                    cs = slice(c4 * 128, (c4 + 1) * 128)
                    for i in range(3):
                        w, wn = wg_t[i]
                        p, pn = PS()
                        for k in range(8):
                            mm(p[:, :N], w[:, k, cs], xn[:, k, :N], k == 0, k == 7, ["xn", wn], [pn])
                        act(sg[:, i, :N], p[:, :N], AF.Sigmoid, [pn, "vt"], ["sg%d" % i], bias=b_gate(i * 8 + c8))
                    for i, (wd, oT_, on_) in enumerate(brs):
                        w, wn = wb_t[i]
                        p, pn = PS()
                        for k in range(4):
                            mm(p[:, :N], w[:, k, cs], oT_[:, k, :N], k == 0, k == 3, [on_, wn], [pn])
                        if i == 0:
                            dve(lambda e, p=p: e.tensor_tensor(out=tmpf[:, :N], in0=p[:, :N], in1=sg[:, 0, :N], op=ALU.mult), [pn, "sg0"], ["tmpf"])
                        else:
                            dve(lambda e, p=p, i=i: e.tensor_tensor(out=tmpf2[:, :N], in0=p[:, :N], in1=sg[:, i, :N], op=ALU.mult), [pn, "sg%d" % i], ["uext2"])
                            if i == 1:
                                dve(lambda e: e.tensor_tensor(out=tmpf[:, :N], in0=tmpf[:, :N], in1=tmpf2[:, :N], op=ALU.add), ["tmpf", "uext2"], ["tmpf"])
                            else:
                                dve(lambda e, c8=c8: e.tensor_tensor(out=mrg[:, c8, :N], in0=tmpf[:, :N], in1=tmpf2[:, :N], op=ALU.add),
                                    ["tmpf", "uext2"], ["mrg"])
            for half in range(2):
                w, wn = loadw(w_out[:, half * 512:(half + 1) * 512], 512)
                for c4 in range(4):
                    c8 = half * 4 + c4
                    p, pn = PS()
                    for k in range(8):
                        mm(p[:, :N], w[:, k, c4 * 128:(c4 + 1) * 128], mrg[:, k, :N], k == 0, k == 7, ["mrg", wn], [pn])
                    dve(lambda e, p=p, c8=c8: e.tensor_tensor(out=hT[:, c8, :N], in0=p[:, :N], in1=xo[:, c8, :N], op=ALU.add), [pn, "xTt0"], ["hT"])
            if debug and gi == 1:
                dma(dbgo["mrg"], mrg[:], ["mrg"], [])
                dma(dbgo["h1"], hT[:], ["hT"], [])
            norm(hT, "hT", N, g_ffn, hn, "mrg")
            for f4 in range(0, NF, 4):
                nf = min(4, NF - f4)
                wu, wun = loadw(w_up[:, f4 * 128:(f4 + nf) * 128], nf * 128)
                for fi in range(nf):
                    f = f4 + fi
                    pu, pun = PS()
                    for k in range(8):
                        mm(pu[:, :N], wu[:, k, fi * 128:(fi + 1) * 128], hn[:, k, :N], k == 0, k == 7, ["mrg", wun], [pun])
                    if gi == 0 and not smp:
                        dve(lambda e, f=f, pu=pu: e.tensor_copy(out=uprev[:, f, :], in_=pu[:, N - 2:N]), [pun], ["uprev"])
                        continue
                    wgt, wgn = loadw3[f % 3](w_up[:, DFF + f * 128:DFF + (f + 1) * 128], 128)
                    pg_, pgn_ = PS()
                    for k in range(8):
                        mm(pg_[:, :N], wgt[:, k, :128], hn[:, k, :N], k == 0, k == 7, ["mrg", wgn], [pgn_])
                    ux, uxn = (uext, "uext") if f % 2 == 0 else (uext2, "uext2")
                    tf, tfn = (tmpf, "tmpf") if f % 2 == 0 else (sg[:, 0, :], "sg0")
                    if smp:
                        u3 = ux[:, 0:40].rearrange("p (s c) -> p s c", c=10)
                        t3 = tf[:, 0:32].rearrange("p (s c) -> p s c", c=8)
                        dve(lambda e, f=f, u3=u3: e.tensor_copy(out=u3[:, :, 0:2], in_=sconvt[:, f, :, :]), ["kst"], [uxn])
                        act(u3[:, :, 2:10], pu[:, :32].rearrange("p (s c) -> p s c", c=8), AF.Copy, [pun, uxn], [uxn])
                        dve(lambda e, f=f, u3=u3: e.tensor_copy(out=ulasts[:, f, :, :], in_=u3[:, :, 8:10]), [uxn], ["kst"])
                        dve(lambda e, f=f, u3=u3, t3=t3: e.tensor_scalar(out=t3, in0=u3[:, :, 2:10], scalar1=wc(2, f), scalar2=wc(3, f), op0=ALU.mult, op1=ALU.add),
                            [uxn, "vt"], [tfn])
                        dve(lambda e, f=f, u3=u3, t3=t3: e.scalar_tensor_tensor(out=t3, in0=u3[:, :, 1:9], scalar=wc(1, f), in1=t3, op0=ALU.mult, op1=ALU.add),
                            [uxn, "vt", tfn], [tfn])
                        dve(lambda e, f=f, u3=u3, t3=t3: e.scalar_tensor_tensor(out=t3, in0=u3[:, :, 0:8], scalar=wc(0, f), in1=t3, op0=ALU.mult, op1=ALU.add),
                            [uxn, "vt", tfn], [tfn])
                    else:
                        if gi == 1:
                            dve(lambda e, f=f, ux=ux: e.tensor_scalar(out=ux[:, 0:2], in0=uprev[:, f, :], scalar1=keep[:, 0:1], scalar2=None, op0=ALU.mult),
                                ["uprev", "keep"], [uxn])
                        else:
                            dve(lambda e, f=f, ux=ux: e.tensor_copy(out=ux[:, 0:2], in_=uprev[:, f, :]), ["uprev"], [uxn])
                        act(ux[:, 2:2 + N], pu[:, :N], AF.Copy, [pun, uxn], [uxn])
                        dve(lambda e, f=f, ux=ux: e.tensor_copy(out=uprev[:, f, :], in_=ux[:, N:N + 2]), [uxn], ["uprev"])
                        dve(lambda e, f=f, ux=ux, tf=tf: e.tensor_scalar(out=tf[:, :N], in0=ux[:, 2:2 + N], scalar1=wc(2, f), scalar2=wc(3, f), op0=ALU.mult, op1=ALU.add),
                            [uxn, "vt"], [tfn])
                        dve(lambda e, f=f, ux=ux, tf=tf: e.scalar_tensor_tensor(out=tf[:, :N], in0=ux[:, 1:1 + N], scalar=wc(1, f), in1=tf[:, :N], op0=ALU.mult, op1=ALU.add),
                            [uxn, "vt", tfn], [tfn])
                        dve(lambda e, f=f, ux=ux, tf=tf: e.scalar_tensor_tensor(out=tf[:, :N], in0=ux[:, 0:N], scalar=wc(0, f), in1=tf[:, :N], op0=ALU.mult, op1=ALU.add),
                            [uxn, "vt", tfn], [tfn])
                    act(tf[:, :N], tf[:, :N], AF.Gelu_apprx_tanh, [tfn], [tfn])
                    dve(lambda e, pg_=pg_, f=f, tf=tf: e.tensor_tensor(out=aT[:, f, :N], in0=tf[:, :N], in1=pg_[:, :N], op=ALU.mult), [tfn, pgn_], ["aT"])
            if smp:
                dma(conv_s, ulasts, ["kst"], [])
            elif gi == len(GROUPS) - 1:
                dma(conv_o, uprev[:], ["uprev"], [])
            for c8 in range(8 if (gi > 0 or smp) else 0):
                p, pn = PSA(c8 % 2)
                for f4 in range(0, NF, 4):
                    nf = min(4, NF - f4)
                    w, wn = loadwd(w_down[f4 * 128:(f4 + nf) * 128, c8 * 128:(c8 + 1) * 128], 128, nf)
                    for fi in range(nf):
                        f = f4 + fi
                        mm(p[:, :N], w[:, fi, :128], aT[:, f, :N], f == 0, f == NF - 1, ["aT", wn], [pn])
                dve(lambda e, p=p, c8=c8: e.tensor_tensor(out=hT[:, c8, :N], in0=p[:, :N], in1=hT[:, c8, :N], op=ALU.add), [pn, "hT"], ["hT"])
            if debug and gi == 1:
                dma(dbgo["h2"], hT[:], ["hT"], [])
            if gi > 0 or smp:
                act(sqt[:, :, :N], hT[:, :, :N], AF.Square, ["hT"], ["sqt"])
                p, pn = PS()
                for k in range(8):
                    mm(p[:, :N], ones[:], sqt[:, k, :N], k == 0, k == 7, ["ones", "sqt"], [pn])
                act(rstd[:, :N], p[:, :N], AF.Sqrt, [pn], ["rstd"], scale=1.0 / 1024, bias=1e-6)
                dve(lambda e: e.reciprocal(out=rstd[:, :N], in_=rstd[:, :N]), ["rstd"], ["rstd"])
                for k in range(8):
                    dve(lambda e, k=k: e.scalar_tensor_tensor(out=yst[:, k, :N], in0=hT[:, k, :N], scalar=g_fin(k), in1=rstd[:, :N],
                                                              op0=ALU.mult, op1=ALU.mult), ["hT", "vt", "rstd"], ["xTt0"])
                if smp:
                    dma(yT_s.rearrange("(k p) t -> p k t", p=128), yst[:, :, :N], ["xTt0"], [])
                else:
                    dma(yT.rearrange("(k p) t -> p k t", p=128)[:, :, oc0:oc0 + N], yst[:, :, :N], ["xTt0"], [])

        wg3 = [sb("wg3_%d" % i, [128, 8, 128], BF16) for i in range(3)]
        wb3 = [sb("wb3_%d" % i, [128, 4, 128], BF16) for i in range(3)]
        wdt = [sb("wdt%d" % i, [128, 4, 128], BF16) for i in range(2)]
        wdctr = [0]

        def mk_loadw3(i):
            def f(src_ap, ncols):
                dma(wg3[i][:, :, :ncols], src_ap.rearrange("(k p) c -> p k c", p=128), ["wscr_" + src_ap.tensor.name], ["wg3_%d" % i])
                return wg3[i], "wg3_%d" % i
            return f

        def mk_loadw3b(i):
            def f(src_ap, ncols, kr):
                dma(wb3[i][:, :kr, :ncols], src_ap.rearrange("(k p) c -> p k c", p=128), ["wscr_" + src_ap.tensor.name], ["wb3_%d" % i])
                return wb3[i], "wb3_%d" % i
            return f

        loadw3 = [mk_loadw3(i) for i in range(3)]
        loadw3b = [mk_loadw3b(i) for i in range(3)]

        def loadwd(src_ap, ncols, kr):
            i = wdctr[0] % 2
            wdctr[0] += 1
            dma(wdt[i][:, :kr, :ncols], src_ap.rearrange("(k p) c -> p k c", p=128), ["wscr_" + src_ap.tensor.name], ["wdt%d" % i])
            return wdt[i], "wdt%d" % i

        vstf = vst[:, :, :].rearrange("p a b -> p (a b)")
        kstf = kst[:, :, :].rearrange("p a b -> p (a b)").bitcast(F32)
        csm = vstf[:, 128:512].bitcast(F32)
        knT = vstf[:, 0:128].rearrange("p (c t) -> p c t", t=32)
        gs2 = vstf[0:64, 512:640].bitcast(F32)
        biasb = vstf[0:64, 640:704]
        biasT = vstf[0:64, 704:768]
        m8s = vstf[0:64, 768:784].bitcast(F32)
        rd1 = vstf[0:64, 784:786].bitcast(F32)
        sconvt = kstf[:, 0:176].rearrange("p (f s i) -> p f s i", s=4, i=2)
        ulasts = kstf[:, 176:352].rearrange("p (f s i) -> p f s i", s=4, i=2)
        Qbd = Bq[:, :, :, :].rearrange("p a b c -> p (a b c)")[:, 0:1024].rearrange("p (a b c) -> p a b c", a=4, b=4)
        vnew = qTa[0:8, :, :].rearrange("p a b -> p (a b)").rearrange("p (s c) -> p s c", c=512)
        gself = gsel[:, :, :].rearrange("p a b -> p (a b)")
        ptf = gself[:, 0:128]
        pti = gself[:, 128:256].bitcast(I32)
        idxu = rden[:, 0:128].bitcast(I32)
        cm8 = csm[0:8, 1:65]
        EtA = csm[0:64, 65:129].rearrange("p (h t) -> p h t", t=8)
        tri8I = csm[0:8, 129:137]
        tri8U = csm[0:8, 137:145]
        tri8m = csm[0:8, 145:153]
        m16c = csm[0:8, 153:154]

        def sample_phase():
            N = 32
            dma(csm, csm_d, [], ["vst"])
            dma(sconvt, sconv, [], ["kst"])
            dma(xo[:, :, :N], xT_smp.rearrange("(k p) t -> p k t", p=128), [], ["xTt0"])
            norm(xo, "xTt0", N, g_mix, xn, "xn")
            w, wn = loadw(w_in[:, 0:512], 512)
            for c in range(4):
                p, pn = PS()
                for k in range(8):
                    mm(p[:, :N], w[:, k, c * 128:(c + 1) * 128], xn[:, k, :N], k == 0, k == 7, ["xn", wn], [pn])
                act(qT[:, c, :N], p[:, :N], AF.Copy, [pn], ["qT"])
            w, wn = loadw(w_in[:, 512:1024], 512)
            for c in range(4):
                p, pn = PS()
                for k in range(8):
                    mm(p[:, :N], w[:, k, c * 128:(c + 1) * 128], xn[:, k, :N], k == 0, k == 7, ["xn", wn], [pn])
                act(knT[:, c, :], p[:, :N], AF.Copy, [pn], ["vst"])
                dve(lambda e, p=p, c=c: e.tensor_copy(out=kstg[:, c, :N], in_=p[:, :N]), [pn], ["hT"])
            dma(kT_s.rearrange("(c p) t -> p c t", p=128), kstg[:, 0:4, :N], ["hT"], [])
            w, wn = loadw(w_in[:, 1024:1536], 512)
            for s_ in range(4):
                p, pn = PS()
                for k in range(8):
                    mm(p[0:8, :], xn[:, k, s_ * 8:(s_ + 1) * 8], w[:, k, :], k == 0, k == 7, ["xn", wn], [pn])
                act(vnew[:, s_, :], p[0:8, :], AF.Copy, [pn], ["qTa"])
                dve(lambda e, p=p: e.tensor_copy(out=stg[0:8, :], in_=p[0:8, :]), [pn], ["stg"])
                dma(v_s[s_ * 8:(s_ + 1) * 8, :], stg[0:8, :], ["stg"], [])
            dve(lambda e: e.memset(Qbd, 0.0), [], ["Bq"])
            for hp in range(4):
                for h2 in range(2):
                    h = 2 * hp + h2
                    prs = slice(h2 * 64, h2 * 64 + 64)
                    dve(lambda e, hp=hp, h=h, prs=prs: e.tensor_copy(out=Qbd[prs, hp, :, h * 8:(h + 1) * 8],
                                                                     in_=qT[prs, hp, 0:32].rearrange("p (s t) -> p s t", t=8)), ["qT", "Bq"], ["Bq"])
            Sraw = vaug0[:, :, :].rearrange("p a (b c) -> p (a b) c", c=64)
            NS = 8
            kslot = [("wt0s%d" % j, wt[0][:, j, :]) for j in range(NS)]
            vslot = [("wt1s%d" % j, wt[1][:, j, :]) for j in range(NS)]
            dve(lambda e: e.memset(m8[0:1, 0, 0:1], 0.0), ["wt0", "wt1"], [n for n, _ in kslot + vslot] + ["m8"])
            for s_ in range(4):
                cs = slice(s_ * 8, (s_ + 1) * 8)
                dma(pti, ptab[s_:s_ + 1, :].to_broadcast([128, 128]), [], ["gsel"])
                dve(lambda e: e.tensor_copy(out=ptf, in_=pti), ["gsel"], ["gsel"])
                dve(lambda e: e.tensor_scalar(out=ptf, in0=ptf, scalar1=128.0, scalar2=csm[:, 0:1], op0=ALU.mult, op1=ALU.add), ["gsel", "vst"], ["gsel"])
                dve(lambda e: e.tensor_copy(out=idxu, in_=ptf), ["gsel"], ["rden"])
                pgate, pgaten = PSA(1)
                for pg in range(128):
                    ktn, ktf = kslot[pg % NS]
                    kt = ktf.rearrange("p (a b) -> p a b", b=128)
                    S.op("pool", lambda e, ktf=ktf, pg=pg: e.indirect_dma_start(
                        out=ktf, out_offset=None, in_=poolKT,
                        in_offset=bass.IndirectOffsetOnAxis(ap=idxu[:, pg:pg + 1], axis=0)), reads=["rden"], writes=[ktn], dma=True)
                    ps_, psn = PS()
                    for hp in range(4):
                        mm(ps_[:, :64], kt[:, hp, :], Qbd[:, hp, s_, :], hp == 0, hp == 3, [ktn, "Bq"], [psn])
                    act(Sraw[:, pg, :], ps_[:, :64], AF.Copy, [psn], ["vaug0"])
                    if pg >= 1:
                        q_ = pg - 1
                        mm(pgate[0:64, q_ // 2:q_ // 2 + 1], Sraw[:, q_, :], ones[:, 0:1], q_ % 2 == 0, q_ % 2 == 1, ["vaug0", "ones"], [pgaten])
                mm(pgate[0:64, 63:64], Sraw[:, 127, :], ones[:, 0:1], False, True, ["vaug0", "ones"], [pgaten])
                dve(lambda e, pgate=pgate: e.tensor_copy(out=gs2, in_=pgate[0:64, 0:64]), [pgaten], ["vst"])
                dve(lambda e: e.max(out=m8s, in_=gs2), ["vst"], ["vst"])
                dve(lambda e: e.tensor_scalar(out=gs2, in0=gs2, scalar1=m8s[:, 2:3], scalar2=None, op0=ALU.is_ge), ["vst", "vst"], ["vst"])
                dve(lambda e: e.tensor_scalar(out=biasb, in0=gs2, scalar1=-1.0, scalar2=-NEG, op0=ALU.add, op1=ALU.mult), ["vst"], ["vst"])
                pb, pbn = PSB()
                S.op("pe", lambda e, pb=pb: e.transpose(out=pb[0:64, 0:64], in_=biasb, identity=cb[0:64, 0, 0:64]), reads=["vst", "cb"], writes=[pbn])
                act(biasT, pb[0:64, 0:64], AF.Copy, [pbn], ["vst"])
                po, pon = PSA(0)
                pden, pdenn = PSA(1)
                pbl = {}

                def issue_bias(b4):
                    pb_, pb_n = PS()
                    for j in range(4):
                        blk = (b4 * 4 + j) // 2
                        mm(pb_[:, j * 64:(j + 1) * 64], cb[0:64, 0, blk:blk + 1].to_broadcast([64, 128]), biasT, True, True, ["cb", "vst"], [pb_n])
                    pbl[b4] = (pb_, pb_n)

                issue_bias(0)
                for b4 in range(32):
                    if b4 + 1 < 32:
                        issue_bias(b4 + 1)
                    pb_, pb_n = pbl.pop(b4)
                    dve(lambda e, pb_=pb_, b4=b4: e.scalar_tensor_tensor(
                        out=tmpf[:, :256], in0=Sraw[:, b4 * 4:(b4 + 1) * 4, :].rearrange("p a b -> p (a b)"), scalar=0.125, in1=pb_[:, :256],
                        op0=ALU.mult, op1=ALU.add), ["vaug0", pb_n], ["tmpf"])
                    ptile = pT[b4 % 2]
                    ptn = "pT%d" % (b4 % 2)
                    act(ptile[:, :256], tmpf[:, :256], AF.Exp, ["tmpf"], [ptn])
                    for j in range(4):
                        pg = b4 * 4 + j
                        vtn, vt_ = vslot[pg % NS]
                        S.op("pool", lambda e, vt_=vt_, pg=pg: e.indirect_dma_start(
                            out=vt_, out_offset=None, in_=poolV,
                            in_offset=bass.IndirectOffsetOnAxis(ap=idxu[:, pg:pg + 1], axis=0)), reads=["rden"], writes=[vtn], dma=True)
                        mm(po[0:64, :], ptile[:, j * 64:(j + 1) * 64], vt_, pg == 0, False, [ptn, vtn], [pon])
                        mm(pden[0:64, 100:101], ptile[:, j * 64:(j + 1) * 64], ones[:, 0:1], pg == 0, False, [ptn, "ones"], [pdenn])
                pso, pson = PS()
                for hp in range(4):
                    mm(pso[0:8, :64], knT[:, hp, cs], Qbd[:, hp, s_, :], hp == 0, hp == 3, ["vst", "Bq"], [pson])
                dve(lambda e, pso=pso: e.scalar_tensor_tensor(out=tmpf[0:8, :64], in0=pso[0:8, :64], scalar=0.125, in1=cm8, op0=ALU.mult, op1=ALU.add),
                    [pson, "vst"], ["tmpf"])
                act(pT[0][0:8, :64], tmpf[0:8, :64], AF.Exp, ["tmpf"], ["pT0"])
                mm(po[0:64, :], pT[0][0:8, :64], vnew[:, s_, :], False, True, ["pT0", "qTa"], [pon])
                mm(pden[0:64, 100:101], pT[0][0:8, :64], ones[0:8, 0:1], False, True, ["pT0", "ones"], [pdenn])
                act(stg[0:64, :], po[0:64, :], AF.Copy, [pon], ["stg"])
                dve(lambda e, pden=pden: e.reciprocal(out=rd1, in_=pden[0:64, 100:101]), [pdenn], ["vst"])
                dve(lambda e: e.tensor_scalar(out=stg[0:64, :], in0=stg[0:64, :], scalar1=rd1[:, 0:1], scalar2=None, op0=ALU.mult),
                    ["stg", "vst"], ["stg"])
                for h in range(8):
                    hp = h // 2
                    rs = slice((h % 2) * 64, (h % 2) * 64 + 64)
                    pz, pzn = PS()
                    mm(pz[:, 0:8], stg[0:64, hp * 128:(hp + 1) * 128], EtA[:, h, :], True, True, ["stg", "vst"], [pzn])
                    act(omT[rs, hp, cs], pz[rs, 0:8], AF.Copy, [pzn], ["omT"])
            dve(lambda e: e.memset(m8[0:1, 0, 0:1], 0.0), [n for n, _ in kslot + vslot], ["wt0", "wt1", "m8"])
            gla_proj(N)
            for s_ in range(4):
                cs = slice(s_ * 8, (s_ + 1) * 8)
                p, pn = PS()
                for k in range(8):
                    mm(p[:16, :8], wf[:, k, 1792:1808], xn[:, k, cs], k == 0, k == 7, ["xn", "wf"], [pn])
                dve(lambda e, p=p: e.tensor_copy(out=agT[:, :8], in_=p[:16, :8]), [pn], ["agT"])
                p2, pn2 = PS()
                mm(p2[0:8, :256], agT[:, :8], wa2[:], True, False, ["agT", "wa2"], [pn2])
                mm(p2[0:8, :256], ones[0:1, 0:8], ba[:], False, True, ["ones", "ba"], [pn2])
                act(nl[0:8, :], p2[0:8, :256], AF.Exp, [pn2], ["nl"], scale=-1.0)
                act(nl[0:8, :], nl[0:8, :], AF.Ln, ["nl"], ["nl"], bias=1.0)
                p3, pn3 = PS()
                mm(p3[0:8, :256], tri8U, nl[0:8, :], True, True, ["vst", "nl"], [pn3])
                act(eR[0:8, :], p3[0:8, :256], AF.Exp, [pn3], ["eR"])
                pk, pkn = PS()
                for k in range(8):
                    mm(pk[0:8, :256], xn[:, k, cs], wf[:, k, 1024:1280], k == 0, k == 7, ["xn", "wf"], [pkn])
                dve(lambda e, pk=pk: e.tensor_tensor(out=ktil[0:8, :], in0=pk[0:8, :256], in1=eR[0:8, :], op=ALU.mult), [pkn, "eR"], ["ktil"])
                pv, pvn = PS()
                for k in range(8):
                    mm(pv[0:8, :], xn[:, k, cs], wf[:, k, 1280:1792], k == 0, k == 7, ["xn", "wf"], [pvn])
                act(vgb[0:8, :], pv[0:8, :], AF.Copy, [pvn], ["vgb"])
                pd, pdn = PS()
                for half in range(2):
                    mm(pd[:, half:half + 1], nl[0:8, half * 128:(half + 1) * 128], m16c, True, True, ["nl", "vst"], [pdn])
                act(dch[:, :, 0], pd[:, 0:2], AF.Exp, [pdn], ["dch"])
                for hp in range(2):
                    pgm, pgn = PS()
                    mm(pgm[:, :8], nl[0:8, hp * 128:(hp + 1) * 128], tri8I, True, True, ["nl", "vst"], [pgn])
                    act(egp[:, :8], pgm[:, :8], AF.Exp, [pgn], ["egp"])
                    act(egn[:, :8], pgm[:, :8], AF.Exp, [pgn], ["egn"], scale=-1.0)
                    for par in range(2):
                        prs = slice(par * 64, par * 64 + 64)
                        dve(lambda e, hp=hp, cs=cs, prs=prs, par=par: e.scalar_tensor_tensor(
                            out=qtl[prs, hp * 2 + par, cs], in0=qgT[prs, hp, cs], scalar=0.125, in1=egp[prs, :8],
                            op0=ALU.mult, op1=ALU.mult), ["qgT", "egp"], ["qtl"])
                    dve(lambda e, hp=hp, cs=cs: e.tensor_tensor(out=ktl[:, hp, cs], in0=kgT[:, hp, cs], in1=egn[:, :8], op=ALU.mult),
                        ["kgT", "egn"], ["ktl"])
                dma(Sst[:], sgla[s_], [], ["Sst"])
                dve(lambda e: e.tensor_copy(out=Sbf[0][:], in_=Sst[:]), ["Sst"], ["Sbf0"])
                for h in range(4):
                    hp = h // 2
                    pa, pan = PS()
                    mm(pa[0:8, 0:8], ktl[:, hp, cs], qtl[:, h, cs], True, True, ["ktl", "qtl"], [pan])
                    dve(lambda e, pa=pa: e.tensor_tensor(out=ATf[0:8, 0:8], in0=pa[0:8, 0:8], in1=tri8m, op=ALU.mult), [pan, "vst"], ["ATf"])
                    po, pon = PSA(0)
                    mm(po[:, 0:8], vgb[0:8, h * 128:(h + 1) * 128], ATf[0:8, 0:8], True, False, ["vgb", "ATf"], [pon])
                    mm(po[:, 0:8], Sbf[0][:, hp, :], qtl[:, h, cs], False, True, ["Sbf0", "qtl"], [pon])
                    act(ogf[:, h, cs], po[:, 0:8], AF.Copy, [pon], ["ogf"])
                for hp in range(2):
                    pu, pun = PS()
                    mm(pu[:, :256], ktil[0:8, hp * 128:(hp + 1) * 128], vgb[0:8, hp * 256:(hp + 1) * 256], True, True, ["ktil", "vgb"], [pun])
                    for hh in range(2):
                        rs = slice(hh * 64, (hh + 1) * 64)
                        dve(lambda e, pu=pu, hp=hp, hh=hh, rs=rs: e.scalar_tensor_tensor(
                            out=Sst[rs, hp, :], in0=Sst[rs, hp, :], scalar=dch[rs, hp, 0:1], in1=pu[rs, hh * 128:(hh + 1) * 128],
                            op0=ALU.mult, op1=ALU.add), [pun, "dch", "Sst"], ["Sst"])
                dma(gla_s[s_], Sst[:], ["Sst"], [])
            gla_norm(N)
            cross_q(N)
            for s_ in range(4):
                dma(mkT[:], mkTs[s_].rearrange("h d m -> d h m"), [], ["mkT"], q="pool")
                dma(mvb[:], mvs[s_].rearrange("(t p) c -> p t c", p=128), [], ["mvb"], q="pool")
                cross_att(s_ * 8, 8)
            tail_own(99, 0, N, smp=True)

        for gi, (c0, N) in enumerate(GROUPS[:ngroups]):
            ntile = N // 128
            dma(xo[:, :, :N], xT_own[:, c0:c0 + N].rearrange("(k p) t -> p k t", p=128), [], ["xTt0"])
            norm(xo, "xTt0", N, g_mix, xn, "xn")
            oc0 = c0 - HALO

            if "moba" in phases:
                w, wn = loadw(w_in[:, 0:512], 512)
                for c in range(4):
                    p, pn = PS()
                    for k in range(8):
                        mm(p[:, :N], w[:, k, c * 128:(c + 1) * 128], xn[:, k, :N], k == 0, k == 7, ["xn", wn], [pn])
                    act(qT[:, c, :N], p[:, :N], AF.Copy, [pn], ["qT"])
                for tt in range(ntile):
                    p, pn = PS()
                    for k in range(8):
                        mm(p[:], xn[:, k, tt * 128:(tt + 1) * 128], w[:, k, :], k == 0, k == 7, ["xn", wn], [pn])
                    act(Bq[:, tt, :, 0:64], p[:].rearrange("p (h d) -> p h d", h=8), AF.Copy, [pn], ["Bq"])
                w, wn = loadw(w_in[:, 512:1024], 512)
                for h in range(8):
                    p, pn = PS()
                    for k in range(8):
                        mm(p[:64, :N], w[:, k, h * 64:(h + 1) * 64], xn[:, k, :N], k == 0, k == 7, ["xn", wn], [pn])
                    act(kla[0:64, h, :N], p[:64, :N], AF.Copy, [pn], ["kla"])
                    if gi > 0:
                        dve(lambda e, p=p, h=h: e.tensor_copy(out=kstg[0:64, h, :N], in_=p[:64, :N]), [pn], ["hT"])
                if gi > 0:
                    dma(kT_o.rearrange("(h d) t -> d h t", d=64)[:, :, oc0:oc0 + N], kstg[0:64, :, :N], ["hT"], [])
                w, wn = loadw(w_in[:, 1024:1536], 512)
                for tt in range(ntile):
                    p, pn = PS()
                    for k in range(8):
                        mm(p[:], xn[:, k, tt * 128:(tt + 1) * 128], w[:, k, :], k == 0, k == 7, ["xn", wn], [pn])
                    act(vloc[:, tt, :], p[:], AF.Copy, [pn], ["vloc"])
                    if gi > 0:
                        dve(lambda e, p=p: e.tensor_copy(out=stg[:], in_=p[:]), [pn], ["stg"])
                        dma(v_o[oc0 + tt * 128:oc0 + (tt + 1) * 128, :], stg[:], ["stg"], [])
                for tt in range(ntile):
                    qb = (c0 + tt * 128) // 256
                    pg, pgn = PS()
                    for hp in range(4):
                        mm(pg[:, hp * 64:(hp + 1) * 64], qT[:, hp, tt * 128:(tt + 1) * 128], kmbd[:, hp, :], True, True, ["qT", "kmbd"], [pgn])
                    dve(lambda e, pg=pg, qb=qb: e.tensor_tensor(out=gsel[:], in0=pg[:, :256].rearrange("p (h b) -> p h b", h=8),
                                                                in1=bv[:, qb:qb + 1, :].to_broadcast([128, 8, 32]), op=ALU.add), [pgn, "bv"], ["gsel"])
                    for h in range(8):
                        dve(lambda e, h=h: e.max(out=m8[:, h, :], in_=gsel[:, h, :]), ["gsel"], ["m8"])
                    dve(lambda e: e.tensor_scalar_max(out=m8[:, :, 2:3], in0=m8[:, :, 2:3], scalar1=-1e29), ["m8"], ["m8"])
                    dve(lambda e: e.tensor_tensor(out=gsel[:], in0=gsel[:], in1=m8[:, :, 2:3].to_broadcast([128, 8, 32]), op=ALU.is_ge), ["gsel", "m8"], ["gsel"])
                    dve(lambda e, tt=tt: e.tensor_scalar(out=Bq[:, tt, :, 64:96], in0=gsel[:], scalar1=-1.0, scalar2=-NEG, op0=ALU.add, op1=ALU.mult),
                        ["gsel"], ["Bq"])
                    for hq in range(2):
                        pb, pbn = PSB()
                        for h4 in range(4):
                            h = hq * 4 + h4
                            S.op("pe", lambda e, pb=pb, h4=h4, h=h, tt=tt: e.transpose(out=pb[:96, h4 * 128:(h4 + 1) * 128], in_=Bq[:, tt, h, :], identity=ident),
                                 reads=["Bq", "cb"], writes=[pbn])
                        act(qTa[:, hq * 4:(hq + 1) * 4, tt * 128:(tt + 1) * 128], pb[:96, :512].rearrange("p (h q) -> p h q", h=4), AF.Copy, [pbn], ["qTa"])
                npast = min(32, 23 + gi)
                for h in range(8):
                    par = h % 2
                    hp = h // 2
                    ka = kaug0
                    kan = "kaug0"
                    L = npast * 256
                    nkt = npast * 2
                    dma(ka[0:64, :L], Ks[h * 64:(h + 1) * 64, :L], ["Ks"], [kan])
                    if par == 0:
                        dma(vaug0[:, :nkt, :], Vs[:L, hp * 128:(hp + 1) * 128].rearrange("(t p) d -> p t d", p=128), ["Vs"], ["vaug0"])
                    po, pon = PSA(0)
                    pd, pdn = PSA(1)
                    items = []
                    for kt in range(0, nkt, 2):
                        items.append([(ka[:, (kt + j) * 128:(kt + j + 1) * 128], kan, vaug0[:, kt + j, :], "vaug0", 0, 2, False, j * 256) for j in range(2)])
                    items.append([(kla[:, h, 0:128], "kla", vloc[:, 0, hp * 128:(hp + 1) * 128], "vloc", 0, 2, True, 0),
                                  (kla[:, h, 128:256], "kla", vloc[:, 1, hp * 128:(hp + 1) * 128], "vloc", 1, 1, True, 256)])
                    LA = 3
                    psl = {}

                    def issue_qk(i):
                        ps_, psn = PS()
                        for (kap, kn_, vap, vn_, q0, qn_, diag, col0) in items[i]:
                            mm(ps_[:, col0:col0 + qn_ * 128], kap, qTa[:, h, q0 * 128:(q0 + qn_) * 128], True, True, [kn_, "qTa"], [psn])
                        psl[i] = (ps_, psn)

                    for i in range(min(LA, len(items))):
                        issue_qk(i)
                    for si, it in enumerate(items):
                        if si + LA < len(items):
                            issue_qk(si + LA)
                        ps_, psn = psl.pop(si)
                        ptile = pT[si % 2]
                        ptn = "pT%d" % (si % 2)
                        wtot = it[-1][7] + it[-1][5] * 128
                        act(ptile[:, :wtot], ps_[:, :wtot], AF.Exp, [psn], [ptn], scale=0.125)
                        for (kap, kn_, vap, vn_, q0, qn_, diag, col0) in it:
                            if diag:
                                dve(lambda e, ptile=ptile, col0=col0: e.tensor_tensor(out=ptile[:, col0:col0 + 128], in0=ptile[:, col0:col0 + 128], in1=tri01, op=ALU.mult),
                                    [ptn, "cb"], [ptn])
                        for j, (kap, kn_, vap, vn_, q0, qn_, diag, col0) in enumerate(it):
                            qs = slice(q0 * 128, (q0 + qn_) * 128)
                            first = si == 0 and j == 0
                            last = si == len(items) - 1 and j == len(it) - 1
                            mm(po[:, qs], vap, ptile[:, col0:col0 + qn_ * 128], first, last, [vn_, ptn], [pon])
                            acc, accn, fn_ = (tmpf, "tmpf", dve) if j == 0 else (uext2, "uext2", pool)
                            if si == 0:
                                fn_(lambda e, ptile=ptile, col0=col0, qs=qs, qn_=qn_, acc=acc: e.tensor_copy(out=acc[:, qs], in_=ptile[:, col0:col0 + qn_ * 128]),
                                    [ptn], [accn])
                            else:
                                fn_(lambda e, ptile=ptile, col0=col0, qs=qs, qn_=qn_, acc=acc: e.tensor_tensor(out=acc[:, qs], in0=acc[:, qs], in1=ptile[:, col0:col0 + qn_ * 128], op=ALU.add),
                                    [ptn, accn], [accn])
                    mm(pd[:, :N], c32t[:, 388:516], tmpf[:, :N], True, False, ["c32t", "tmpf"], [pdn])
                    mm(pd[:, :N], c32t[:, 388:516], uext2[:, :N], False, True, ["c32t", "uext2"], [pdn])
                    rs = slice(par * 64, par * 64 + 64)
                    dve(lambda e, pd=pd, rs=rs: e.reciprocal(out=rden[rs, :N], in_=pd[rs, :N]), [pdn], ["rden"])
                    dve(lambda e, rs=rs, hp=hp, po=po: e.tensor_tensor(out=omT[rs, hp, :N], in0=po[rs, :N], in1=rden[rs, :N], op=ALU.mult),
                        [pon, "rden"], ["omT"])

            if debug and gi == 1:
                dma(dbgo["om"][:, 0:4, :], omT[:], ["omT"], [])
                dma(dbgo["qta"][0:96, :, :], qTa[:], ["qTa"], [])
                dma(dbgo["xn"], xn[:], ["xn"], [])
            if "gla" in phases:
                gla_own(gi, c0, N)
            if "cross" in phases:
                cross_own(gi, c0, N)
            if debug and gi == 1:
                dma(dbgo["og"][:, 0:4, :], ogT[:], ["ogT"], [])
                dma(dbgo["oc"][:, 0:4, :], ocT[:], ["ocT"], [])
            if "tail" in phases:
                tail_own(gi, c0, N)
        if "smp" in phases:
            sample_phase()
        S.emit(nc)
    return nc


_NC_CACHE = {}


def _consts():
    p = np.arange(128)
    ident = np.eye(128, dtype=np.float32)
    tri01 = (p[:, None] <= p[None, :]).astype(np.float32)
    blk2 = ((p[:, None] // 64) == (p[None, :] // 64)).astype(np.float32)
    triB = tri01 * blk2
    cb = np.concatenate([ident, tri01, blk2, triB, ident, ident], axis=1).astype(np.float32)
    triU = ((p[:, None] > p[None, :]) & ((p[:, None] // 64) == (p[None, :] // 64))).astype(np.float32) * (-1.0 / 16)
    triI = triB * (-1.0 / 16)
    chk = np.zeros((128, 2), np.float32)
    chk[:64, 0] = -1.0 / 16
    chk[64:, 1] = -1.0 / 16
    sel = np.zeros((2, 128, 128), np.float32)
    sel[0, 64, 0:64] = 1.0
    sel[1, :, :] = 1.0
    c32 = np.concatenate([triU, triI, chk, np.zeros((128, 2), np.float32), sel[0], sel[1]], axis=1)
    oh = np.zeros((32, SEQ), np.float32)
    for j in range(32):
        oh[j, j * 256:(j + 1) * 256] = 1.0
    return cb, c32, oh


def _csm():
    c = np.zeros((128, 192), np.float32)
    c[:, 0] = np.arange(128)
    i = np.arange(8)[:, None]
    t = np.tile(np.arange(8), 8)[None, :]
    c[0:8, 1:65] = np.where(i <= t, 0.0, NEG)
    c[0:64, 65:129] = np.eye(64)
    tri = (np.arange(8)[:, None] <= np.arange(8)[None, :]).astype(np.float32)
    c[0:8, 129:137] = tri * (-1.0 / 16)
    c[0:8, 137:145] = (np.arange(8)[:, None] > np.arange(8)[None, :]).astype(np.float32) * (-1.0 / 16)
    c[0:8, 145:153] = tri
    c[0:8, 153] = -1.0 / 16
    return c


def pool_layouts(cache_k, cache_v):
    n = cache_k.shape[1]
    kt = np.ascontiguousarray(np.asarray(cache_k[0], np.float32).reshape(n, 128, 4, 2, 64).transpose(0, 3, 4, 2, 1)).reshape(n * 128, 512)
    v = np.ascontiguousarray(np.asarray(cache_v[0], np.float32)).reshape(n * 128, 512)
    return kt, v


def make_in_maps(inp, pools=None):
    f = lambda a: np.ascontiguousarray(np.asarray(a, dtype=np.float32))
    if pools is None:
        pools = pool_layouts(inp["cache_moba_k"], inp["cache_moba_v"])
    poolKT, poolV = pools
    csm = _csm()
    x_sample = f(inp["x_sample"])
    page_table = np.asarray(inp["page_table"]).astype(np.int32)
    state_gla = f(inp["state_gla"])
    state_conv = f(inp["state_conv"])
    cmk = f(inp["cache_mem_k"])
    cmv = f(inp["cache_mem_v"])
    x_prompt = f(inp["x_prompt"])
    cb, c32, oh = _consts()
    fm = lambda v: f(v).reshape(-1, 128).T
    w_conv = inp["w_conv"]
    vecs = np.concatenate([fm(inp["norm_mix"][0]), fm(inp["norm_ffn"][0]), fm(inp["norm_final"]), fm(inp["norm_mem"][0]),
                           fm(inp["b_gate"][0]), fm(inp["norm_gla"][0]), fm(w_conv[0, 0]), fm(w_conv[0, 1]), fm(w_conv[0, 2]),
                           fm(inp["b_conv"][0])], axis=1)
    vecs = np.ascontiguousarray(vecs)
    shared = dict(w_in=f(inp["w_in"][0]), w_a2=f(inp["w_gla_a2"][0]), b_a=f(inp["b_gla_a"][0]).reshape(1, 256),
                  w_mem=f(inp["w_mem_kv"][0]), w_brm=f(inp["w_br_moba"][0]), w_brg=f(inp["w_br_gla"][0]),
                  w_brc=f(inp["w_br_cross"][0]), w_gate=f(inp["w_gate"][0]), w_out=f(inp["w_out"][0]),
                  w_up=f(inp["w_up"][0]), w_down=f(inp["w_down"][0]), vecs=vecs, consts=cb, c32=c32, ohrows=oh)
    in_maps = []
    for c in range(8):
        b, r = c // 4, c % 4
        xT = np.ascontiguousarray(x_prompt[b].T)
        own = np.zeros((1024, NCOL), np.float32)
        lo = r * NOWN - HALO
        if lo < 0:
            own[:, HALO:] = xT[:, 0:NOWN]
        else:
            own[:] = xT[:, lo:lo + NCOL]
        bvld = np.zeros((128, 9, 32), np.float32)
        for i in range(9):
            cur = 8 * r - 1 + i
            bvld[:, i, :] = np.where(np.arange(32) < cur, 0.0, -1e30)[None, :]
        selr = np.zeros((128, 4), np.float32)
        selr[:, r] = 1.0
        m = dict(shared)
        m.update(xT_full=xT, xT_own=own, memT=np.ascontiguousarray(f(inp["mem_prompt"][b]).T), blkvalid=bvld, selr=selr)
        sq = slice(4 * c, 4 * c + 4)
        m.update(xT_smp=np.ascontiguousarray(x_sample[sq].reshape(32, 1024).T), ptab=np.ascontiguousarray(page_table[sq]),
                 poolKT=poolKT, poolV=poolV,
                 sgla=np.ascontiguousarray(state_gla[0, sq].reshape(4, 2, 2, 64, 128).transpose(0, 2, 3, 1, 4).reshape(4, 128, 2, 128)),
                 sconv=np.ascontiguousarray(state_conv[0, sq].reshape(4, 2, NF, 128).transpose(3, 2, 0, 1)),
                 mkTs=np.ascontiguousarray(cmk[0, sq].transpose(0, 2, 3, 1)), mvs=np.ascontiguousarray(cmv[0, sq].reshape(4, 256, 512)),
                 csm=csm)
        in_maps.append(m)
    return in_maps


def kernel(**inp):
    n_phys = int(np.asarray(inp["cache_moba_k"]).shape[1])
    if n_phys not in _NC_CACHE:
        _NC_CACHE[n_phys] = build_nc(n_phys=n_phys)
    nc = _NC_CACHE[n_phys]
    in_maps = make_in_maps(inp)
    res = run_bass_kernel_spmd(nc, in_maps, core_ids=list(range(8))).results
    B = 2
    y_prompt = np.zeros((B, SEQ, 1024), np.float32)
    nk = np.zeros((1, B, SEQ, 8, 64), np.float32)
    nv = np.zeros((1, B, SEQ, 8, 64), np.float32)
    gp = np.zeros((1, B, 4, 64, 128), np.float32)
    cp = np.zeros((1, B, 2, DFF), np.float32)
    mkp = np.zeros((1, B, 256, 4, 128), np.float32)
    mvp = np.zeros((1, B, 256, 4, 128), np.float32)
    for c in range(8):
        b, r = c // 4, c % 4
        o = res[c]
        sl = slice(r * NOWN, (r + 1) * NOWN)
        y_prompt[b, sl] = o["yT"].T
        nk[0, b, sl] = o["kT_o"].T.reshape(NOWN, 8, 64)
        nv[0, b, sl] = o["v_o"].reshape(NOWN, 8, 64)
        if r == 3:
            g = o["gla_o"]
            gp[0, b] = g.reshape(2, 64, 2, 128).transpose(2, 0, 1, 3).reshape(4, 64, 128)
            cp[0, b] = o["conv_o"].transpose(2, 1, 0).reshape(2, DFF)
        if r == 0:
            mkp[0, b] = o["mk_o"].reshape(256, 4, 128)
            mvp[0, b] = o["mv_o"].reshape(256, 4, 128)
    DB = 32
    y_sample = np.zeros((DB, 8, 1024), np.float32)
    nks = np.zeros((1, DB, 8, 8, 64), np.float32)
    nvs = np.zeros((1, DB, 8, 8, 64), np.float32)
    gs = np.zeros((1, DB, 4, 64, 128), np.float32)
    cs = np.zeros((1, DB, 2, DFF), np.float32)
    for c in range(8):
        o = res[c]
        sq = slice(4 * c, 4 * c + 4)
        y_sample[sq] = o["yT_s"].T.reshape(4, 8, 1024)
        nks[0, sq] = o["kT_s"].T.reshape(4, 8, 8, 64)
        nvs[0, sq] = o["v_s"].reshape(4, 8, 8, 64)
        gs[0, sq] = o["gla_s"].reshape(4, 2, 64, 2, 128).transpose(0, 3, 1, 2, 4).reshape(4, 4, 64, 128)
        cs[0, sq] = o["conv_s"].transpose(2, 3, 1, 0).reshape(4, 2, DFF)
    return (y_prompt, y_sample, nk, nv, nks, nvs, gp, gs, cp, cs, mkp, mvp)
```
